# Optimizing a Trainium2 kernel written in Bass

```python
import jax, jax.numpy as jnp
from jax import lax
import numpy as np

D_MODEL = 1024
BATCH = 8
SEQ = 4096
DEPTH = 4

N_MIXERS = 2
N_LAYERS_A = (DEPTH + 1) // 2
N_LAYERS_B = DEPTH // 2
CHUNK = 128
GMLP_WIDTH = 2 * D_MODEL
GMLP_GROUPS = 8
GMLP_GROUP_DIM = GMLP_WIDTH // GMLP_GROUPS
HEAD_DIM = 64
N_Q_HEADS = D_MODEL // HEAD_DIM
N_KV_HEADS = 4
GQA_GROUP = N_Q_HEADS // N_KV_HEADS
WINDOW = 128
ATTN_BLOCK = 128
ROPE_DIM = HEAD_DIM // 4
ROPE_THETA = 500000.0
Q_WIDTH = N_Q_HEADS * HEAD_DIM
KV_WIDTH = N_KV_HEADS * HEAD_DIM
QKV_WIDTH = Q_WIDTH + 2 * KV_WIDTH
FFN_HIDDEN = -(-(8 * D_MODEL) // (3 * 256)) * 256
RMS_EPS = 1e-6
LN_EPS = 1e-5
NEG_INF = -1e30

kernel_name = "hybrid_gmlp_swa_sink_sandwich"


def rmsnorm(x, g):
    xf = x.astype(jnp.float32)
    y = xf * lax.rsqrt(jnp.mean(xf * xf, axis=-1, keepdims=True) + RMS_EPS)
    return y.astype(x.dtype) * g


def layernorm(x, g, b):
    xf = x.astype(jnp.float32)
    mu = jnp.mean(xf, axis=-1, keepdims=True)
    var = jnp.mean(jnp.square(xf - mu), axis=-1, keepdims=True)
    return ((xf - mu) * lax.rsqrt(var + LN_EPS)).astype(x.dtype) * g + b


def rope_tables(positions):
    inv_freq = ROPE_THETA ** (-jnp.arange(0, ROPE_DIM, 2, dtype=jnp.float32) / ROPE_DIM)
    ang = positions.astype(jnp.float32)[..., None] * inv_freq
    return jnp.cos(ang)[:, :, None, :], jnp.sin(ang)[:, :, None, :]


def apply_partial_rope(x, cos, sin):
    half = ROPE_DIM // 2
    cos = cos.astype(x.dtype)
    sin = sin.astype(x.dtype)
    x1, x2, rest = x[..., :half], x[..., half:ROPE_DIM], x[..., ROPE_DIM:]
    return jnp.concatenate([x1 * cos - x2 * sin, x2 * cos + x1 * sin, rest], axis=-1)


def gmlp_mixer(h, w_in, b_in, ln_g, ln_b, w_s, b_s, w_out):
    B, S, _ = h.shape
    nc = S // CHUNK
    z = jax.nn.gelu(h @ w_in + b_in, approximate=False)
    u, v = jnp.split(z, 2, axis=-1)
    v = layernorm(v, ln_g, ln_b)
    v = v.reshape(B, nc, CHUNK, GMLP_GROUPS, GMLP_GROUP_DIM)
    causal = jnp.tril(jnp.ones((CHUNK, CHUNK), dtype=bool))
    w = jnp.where(causal[None], w_s, 0.0)
    sv = jnp.einsum("gts,bnsgc->bntgc", w, v) + b_s.T[None, None, :, :, None]
    gated = u * sv.reshape(B, S, GMLP_WIDTH)
    return gated @ w_out


def swa_sink_mixer(h, cos, sin, w_qkv, b_qkv, sinks, w_o):
    B, S, _ = h.shape
    nb = S // ATTN_BLOCK
    qkv = h @ w_qkv + b_qkv
    q = qkv[..., :Q_WIDTH].reshape(B, S, N_Q_HEADS, HEAD_DIM)
    k = qkv[..., Q_WIDTH:Q_WIDTH + KV_WIDTH].reshape(B, S, N_KV_HEADS, HEAD_DIM)
    v = qkv[..., Q_WIDTH + KV_WIDTH:].reshape(B, S, N_KV_HEADS, HEAD_DIM)
    q = apply_partial_rope(q, cos, sin) * (HEAD_DIM ** -0.5)
    k = apply_partial_rope(k, cos, sin)
    qb = q.reshape(B, nb, ATTN_BLOCK, N_KV_HEADS, GQA_GROUP, HEAD_DIM)
    kb = k.reshape(B, nb, ATTN_BLOCK, N_KV_HEADS, HEAD_DIM)
    vb = v.reshape(B, nb, ATTN_BLOCK, N_KV_HEADS, HEAD_DIM)
    kk = jnp.concatenate([jnp.concatenate([jnp.zeros_like(kb[:, :1]), kb[:, :-1]], axis=1), kb], axis=2)
    vv = jnp.concatenate([jnp.concatenate([jnp.zeros_like(vb[:, :1]), vb[:, :-1]], axis=1), vb], axis=2)
    s = jnp.einsum("bnqkgd,bnskd->bnkgqs", qb, kk).astype(jnp.float32)
    qi = jnp.arange(ATTN_BLOCK)[:, None]
    sj = jnp.arange(2 * ATTN_BLOCK)[None, :]
    diff = ATTN_BLOCK + qi - sj
    band = (diff >= 0) & (diff < WINDOW)
    exists = (jnp.arange(nb)[:, None, None] > 0) | (sj >= ATTN_BLOCK)[None]
    valid = band[None] & exists
    s = jnp.where(valid[None, :, None, None], s, NEG_INF)
    sink = sinks.astype(jnp.float32).reshape(N_KV_HEADS, GQA_GROUP)[None, None, :, :, None, None]
    m = jnp.maximum(jnp.max(s, axis=-1, keepdims=True), sink)
    p = jnp.exp(s - m)
    denom = jnp.sum(p, axis=-1, keepdims=True) + jnp.exp(sink - m)
    p = (p / denom).astype(vv.dtype)
    o = jnp.einsum("bnkgqs,bnskd->bnqkgd", p, vv).reshape(B, S, Q_WIDTH)
    return o @ w_o


def swiglu_ffn(h, w_gu, w_down):
    g, up = jnp.split(h @ w_gu, 2, axis=-1)
    return (jax.nn.silu(g) * up) @ w_down


def setup_inputs(seed: int = 0) -> dict:
    key = jax.random.key(seed)
    ks = jax.random.split(key, 20)
    f32 = jnp.float32
    nrm = lambda k, shape, scale: jax.random.normal(k, shape, f32) * scale
    x = jax.random.normal(ks[0], (BATCH, SEQ, D_MODEL), f32)
    positions = jnp.broadcast_to(jnp.arange(SEQ, dtype=jnp.int32)[None, :], (BATCH, SEQ))
    gain = lambda k: 1.0 + nrm(k, (DEPTH, D_MODEL), 0.1)
    return {
        "x": x,
        "positions": positions,
        "pre_mix_g": gain(ks[1]),
        "post_mix_g": gain(ks[2]),
        "pre_ffn_g": gain(ks[3]),
        "post_ffn_g": gain(ks[4]),
        "a_w_in": nrm(ks[5], (N_LAYERS_A, D_MODEL, 2 * GMLP_WIDTH), D_MODEL ** -0.5),
        "a_b_in": nrm(ks[6], (N_LAYERS_A, 2 * GMLP_WIDTH), 0.01),
        "a_ln_g": 1.0 + nrm(ks[7], (N_LAYERS_A, GMLP_WIDTH), 0.1),
        "a_ln_b": nrm(ks[8], (N_LAYERS_A, GMLP_WIDTH), 0.01),
        "a_w_s": nrm(ks[9], (N_LAYERS_A, GMLP_GROUPS, CHUNK, CHUNK), 0.5 * CHUNK ** -0.5),
        "a_b_s": 1.0 + nrm(ks[10], (N_LAYERS_A, GMLP_GROUPS, CHUNK), 0.1),
        "a_w_out": nrm(ks[11], (N_LAYERS_A, GMLP_WIDTH, D_MODEL), GMLP_WIDTH ** -0.5),
        "b_w_qkv": nrm(ks[12], (N_LAYERS_B, D_MODEL, QKV_WIDTH), D_MODEL ** -0.5),
        "b_b_qkv": nrm(ks[13], (N_LAYERS_B, QKV_WIDTH), 0.01),
        "b_sinks": nrm(ks[14], (N_LAYERS_B, N_Q_HEADS), 1.0),
        "b_w_o": nrm(ks[15], (N_LAYERS_B, Q_WIDTH, D_MODEL), Q_WIDTH ** -0.5),
        "ffn_w_gu": nrm(ks[16], (DEPTH, D_MODEL, 2 * FFN_HIDDEN), D_MODEL ** -0.5),
        "ffn_w_down": nrm(ks[17], (DEPTH, FFN_HIDDEN, D_MODEL), FFN_HIDDEN ** -0.5),
    }


def reference(x, positions, pre_mix_g, post_mix_g, pre_ffn_g, post_ffn_g,
              a_w_in, a_b_in, a_ln_g, a_ln_b, a_w_s, a_b_s, a_w_out,
              b_w_qkv, b_b_qkv, b_sinks, b_w_o, ffn_w_gu, ffn_w_down):
    cos, sin = rope_tables(positions)
    h = x
    for i in range(DEPTH):
        j = i // N_MIXERS
        hn = rmsnorm(h, pre_mix_g[i])
        if i % N_MIXERS == 0:
            mix = gmlp_mixer(hn, a_w_in[j], a_b_in[j], a_ln_g[j], a_ln_b[j],
                             a_w_s[j], a_b_s[j], a_w_out[j])
        else:
            mix = swa_sink_mixer(hn, cos, sin, b_w_qkv[j], b_b_qkv[j], b_sinks[j], b_w_o[j])
        h = h + rmsnorm(mix, post_mix_g[i])
        f = swiglu_ffn(rmsnorm(h, pre_ffn_g[i]), ffn_w_gu[i], ffn_w_down[i])
        h = h + rmsnorm(f, post_ffn_g[i])
    return h
```

```python
import numpy as np
from contextlib import ExitStack
import concourse.bass as bass
import concourse.mybir as mybir
from concourse.bass_utils import run_bass_kernel_spmd

F32, BF16, I32 = mybir.dt.float32, mybir.dt.bfloat16, mybir.dt.int32
AF = mybir.ActivationFunctionType
ALU = mybir.AluOpType

D = 1024
SEQ = 4096
TT = 512
NBLK = TT // 128
GW = 2048
FH = 2816
NFC = FH // 128
RMS_EPS = 1e-6
LN_EPS = 1e-5
PERM = [0, 2, 1, 3]
SLOT_ELEMS = 5632
NSLOT = 4
SAME_ENGINE_SYNC = True
DBG_B = 0
SIM_CHECK = False

def c_norm(l, w):
    return (l * 4 + w) * 8
C_A = 128
C_B = 224
C_INVF = 248
C_SINK = 249
NCST = 288
R_BV = 0
R_BVD = 4096
NROWS = 5120
KC_MCUR, KC_MPREV, KC_RM = 0, 128, 256
NKC = 384


class Tok:
    __slots__ = ("key", "val")

    def __init__(self, key, val):
        self.key, self.val = key, val


def _flat(deps, out):
    for d in deps:
        if d is None:
            continue
        if isinstance(d, (list, tuple)):
            _flat(d, out)
        else:
            out.append(d)
    return out


class Stream:
    def __init__(self, name):
        self.name, self.ops, self.epoch, self.count, self.waited = name, [], 0, 0, {}


class Prog:
    def __init__(self):
        self.st = {n: Stream(n) for n in ("pe", "act", "dve", "pool", "sp")}
        self.semkeys = {}
        self.dmacount = {}

    def new_epoch(self):
        for s in self.st.values():
            s.epoch += 1
            s.count = 0

    def _waits(self, st, deps, force=False):
        need = {}
        for d in _flat(deps, []):
            if d.key[0] == "E" and d.key[1] == st.name and not (SAME_ENGINE_SYNC or force):
                continue
            if need.get(d.key, 0) < d.val:
                need[d.key] = d.val
        w = []
        for key, val in need.items():
            if st.waited.get(key, 0) >= val:
                continue
            st.waited[key] = val
            w.append((key, val))
        return w

    def op(self, eng, fn, deps=(), signal=True, force=False):
        st = self.st[eng]
        w = self._waits(st, deps, force)
        key = tok = None
        if signal:
            st.count += 1
            key = ("E", eng, st.epoch)
            self.semkeys[key] = True
            tok = Tok(key, st.count)
        st.ops.append((w, fn, key, 1))
        return tok

    def dma(self, eng, fn, semname, deps=()):
        st = self.st[eng]
        w = self._waits(st, deps)
        key = ("D", semname)
        self.semkeys[key] = True
        self.dmacount[key] = self.dmacount.get(key, 0) + 16
        st.ops.append((w, fn, key, 16))
        return Tok(key, self.dmacount[key])


class Ring:
    def __init__(self, n):
        self.n, self.i, self.free = n, 0, [None] * n

    def get(self):
        idx = self.i
        self.i = (self.i + 1) % self.n
        return idx, self.free[idx]

    def rel(self, idx, tok):
        self.free[idx] = tok


def _simulate(P):
    sem = {}
    pc = {n: 0 for n in P.st}
    progress = True
    while progress:
        progress = False
        for n, st in P.st.items():
            while pc[n] < len(st.ops):
                w, fn, key, inc = st.ops[pc[n]]
                if any(sem.get(k, 0) < v for (k, v) in w):
                    break
                if key is not None:
                    sem[key] = sem.get(key, 0) + inc
                pc[n] += 1
                progress = True
    stuck = {n: pc[n] for n in P.st if pc[n] < len(P.st[n].ops)}
    for n, i in stuck.items():
        w = P.st[n].ops[i][0]
        print("SIM STUCK", n, "op", i, "of", len(P.st[n].ops), "waits", [(k, v, sem.get(k, 0)) for (k, v) in w if sem.get(k, 0) < v])
    if not stuck:
        print("SIM OK", {n: len(P.st[n].ops) for n in P.st})
    return not stuck


def default_stages():
    st = []
    for i in range(4):
        st.append(("A" if i % 2 == 0 else "B", i))
        st.append(("F", i))
    return st


def build_nc(stages=None, n_tiles=SEQ // TT):
    stages = default_stages() if stages is None else stages
    S = n_tiles * TT
    nc = bass.Bass("TRN2", target_bir_lowering=False)
    P = Prog()
    es = ExitStack()

    def dram(name, shape, dt, kind):
        return nc.dram_tensor(name, list(shape), dt, kind=kind).ap()

    xT = dram("xT", [D, S], F32, "ExternalInput")
    pos = dram("pos", [1, S], I32, "ExternalInput")
    cstd = dram("cst", [128, NCST], F32, "ExternalInput")
    rowsd = dram("rows", [1, NROWS], F32, "ExternalInput")
    kcd = dram("kc", [128, NKC], F32, "ExternalInput")
    wsTd = dram("wsT", [2, 128, 8 * 128], F32, "ExternalInput")
    bsd = dram("bs", [2, 1, 8 * 128], F32, "ExternalInput")
    w_in = dram("a_w_in", [2, D, 2 * GW], F32, "ExternalInput")
    w_out = dram("a_w_out", [2, GW, D], F32, "ExternalInput")
    w_qkv = dram("b_w_qkv", [2, D, 1536], F32, "ExternalInput")
    w_o = dram("b_w_o", [2, D, D], F32, "ExternalInput")
    w_gu = dram("ffn_w_gu", [4, D, 2 * FH], F32, "ExternalInput")
    w_dn = dram("ffn_w_down", [4, FH, D], F32, "ExternalInput")
    oT = dram("oT", [D, S], F32, "ExternalOutput")
    dbg = dram("dbg", [128, 4096], F32, "ExternalOutput") if DBG_B == 7 else None
    dbg_toks = []
    wb_in = dram("wb_in", [2, D, 2 * GW], BF16, "Internal")
    wb_out = dram("wb_out", [2, GW, D], BF16, "Internal")
    wb_qkv = dram("wb_qkv", [2, D, 2048], BF16, "Internal")
    wb_o = dram("wb_o", [2, D, D], BF16, "Internal")
    wb_gu = dram("wb_gu", [4, D, 2 * FH], BF16, "Internal")
    wb_dn = dram("wb_dn", [4, FH, D], BF16, "Internal")

    def sb(name, shape, dt):
        return es.enter_context(nc.sbuf_tensor(name, list(shape), dt))

    hTs = [sb("hT0", [128, 8, TT], F32), sb("hT1", [128, 8, TT], F32)]
    hT_box = [hTs[0]]
    hn = sb("hn", [128, 8, TT], BF16)
    y = sb("y", [128, 8, TT], F32)
    big = sb("big", [128, 16384], BF16)
    wring_t = [sb(f"wslot{i}", [128, SLOT_ELEMS], BF16) for i in range(NSLOT)]
    cst = sb("cst_sb", [128, NCST], F32)
    rows_bf = sb("rows_bf", [1, 2048], BF16)
    kc_b = sb("kc_b", [128, NKC], BF16)
    ones_bf = sb("ones_bf", [128, 128], BF16)
    eps_t = sb("eps_t", [128, 2], F32)
    Cm = [sb(f"Cm{j}", [128, 16, 128], F32) for j in range(2)]
    wsTm = [sb(f"wsTm{j}", [128, 8, 128], BF16) for j in range(2)]
    cosF = sb("cosF", [128, TT], F32)
    sinF = sb("sinF", [128, TT], F32)
    sq_t = [sb(f"sq{i}", [128, TT], BF16) for i in range(3)]
    lnv = sb("lnv", [128, TT], F32)
    rinv_t = [sb(f"rinv{i}", [128, TT], F32) for i in range(2)]
    tmp_t = [sb(f"tmp{i}", [128, TT], F32) for i in range(2)]
    sg_t = [sb(f"sg{i}", [128, TT], BF16) for i in range(3)]
    tmpA = sb("tmpA", [128, 8 * TT], BF16)
    tmpB = sb("tmpB", [128, 4 * TT], F32)
    es2 = [sb(f"es2_{j}", [64, 16], F32) for j in range(2)]
    es2f = sb("es2f", [64, 16 * 128], BF16)
    esf = sb("esf", [128, 16], F32)
    esh = sb("esh", [128, 16], BF16)
    esl = sb("esl", [128, 16], F32)
    eshf = sb("eshf", [128, 16], F32)
    KTc = [sb(f"KTc{j}", [128, 4, 128], BF16) for j in range(2)]
    Vc = [sb(f"Vc{j}", [128, 512], BF16) for j in range(2)]
    lnst = sb("lnst", [128, NBLK, 4, 6], F32)
    lnmv = sb("lnmv", [128, NBLK, 2], F32)
    lnr = sb("lnr", [128, NBLK], F32)
    mhalf = sb("mhalf", [128, NBLK], F32)
    ps = es.enter_context(nc.psum_tensor("ps", [128, 8, 512], F32))
    pt_t = sb("pt_t", [128, 4 * 512], BF16)
    tmpA_f = tmpA[:].bitcast(F32)
    tmpA_i = tmpA[:].bitcast(I32)
    tmpB_i = tmpB[:].bitcast(I32)

    y_guard = [None]
    lnv_free = [None]
    main = Ring(6)
    aux = Ring(2)
    sqr, rinvr, tmpr, sgr, wring = Ring(3), Ring(2), Ring(2), Ring(3), Ring(NSLOT)

    def auxbank(i):
        return 6 + i

    def mm(out, lhsT, rhs, start, stop, deps=(), signal=False):
        return P.op("pe", lambda e: e.matmul(out, lhsT=lhsT, rhs=rhs, start=start, stop=stop), deps, signal)

    def act(out, in_, func, deps=(), bias=None, scale=None, signal=True):
        kw = {}
        if bias is not None:
            kw["bias"] = bias
        if scale is not None:
            kw["scale"] = scale
        return P.op("act", lambda e: e.activation(out=out, in_=in_, func=func, **kw), deps, signal)

    def tt(eng, out, in0, in1, op, deps=(), signal=True, force=False):
        return P.op(eng, lambda e: e.tensor_tensor(out=out, in0=in0, in1=in1, op=op), deps, signal, force)

    def ts(eng, out, in0, s1, s2, op0, op1=None, deps=(), signal=True, force=False):
        if op1 is None:
            return P.op(eng, lambda e: e.tensor_scalar(out=out, in0=in0, scalar1=s1, scalar2=None, op0=op0), deps, signal, force)
        return P.op(eng, lambda e: e.tensor_scalar(out=out, in0=in0, scalar1=s1, scalar2=s2, op0=op0, op1=op1), deps, signal, force)

    def stt(out, in0, scalar, in1, op0, op1, deps=(), signal=True):
        return P.op("dve", lambda e: e.scalar_tensor_tensor(out=out, in0=in0, scalar=scalar, in1=in1, op0=op0, op1=op1), deps, signal)

    def cp(eng, out, in_, deps=(), signal=True, force=False):
        return P.op(eng, lambda e: e.tensor_copy(out=out, in_=in_), deps, signal, force)

    def dma(eng, out, in_, sem, deps=()):
        return P.dma(eng, lambda e: e.dma_start(out=out, in_=in_), sem, deps)

    a_layers = sorted({l // 2 for (k, l) in stages if k == "A"})
    b_layers = sorted({l // 2 for (k, l) in stages if k == "B"})
    f_layers = sorted({l for (k, l) in stages if k == "F"})

    t_cst = dma("sp", cst[:], cstd, "c_cst")
    t_kc = dma("sp", tmpB[:, 0:NKC], kcd, "c_kc")
    t_kcb = cp("dve", kc_b[:], tmpB[:, 0:NKC], [t_kc])
    rows_free = [None]
    t_ones = P.op("dve", lambda e: e.memset(ones_bf[:], 1.0))
    t_eps = P.op("dve", lambda e: e.memset(eps_t[:, 0:1], RMS_EPS))
    t_eps = P.op("dve", lambda e: e.memset(eps_t[:, 1:2], LN_EPS))
    t_mh = P.op("pool", lambda e: e.memset(mhalf[:], -0.5))
    mcur = kc_b[:, KC_MCUR:KC_MCUR + 128]
    mprev = kc_b[:, KC_MPREV:KC_MPREV + 128]
    Rm = kc_b[:, KC_RM:KC_RM + 128]

    conv = {}

    conv_hist = []

    def cdma(dst, src, sem):
        thr = conv_hist[-2] if len(conv_hist) >= 2 else None
        return dma("pool", dst, src, sem, [thr])

    def convert(name, dst, src, rows_per, sem):
        n = src.shape[0]
        t = None
        for r0 in range(0, n, rows_per):
            r1 = min(n, r0 + rows_per)
            t = cdma(dst[r0:r1, :], src[r0:r1, :], sem)
        conv[name] = t
        conv_hist.append(t)

    def convert_stage(kind, l):
        j = l // 2
        if kind == "A" and ("in", j) not in conv:
            convert(("in", j), wb_in[j], w_in[j], 128, f"cv_in{j}")
            convert(("out", j), wb_out[j], w_out[j], 512, f"cv_out{j}")
        if kind == "B" and ("qkv", j) not in conv:
            t = None
            for r0 in range(0, D, 256):
                t = cdma(wb_qkv[j][r0:r0 + 256, 0:1024], w_qkv[j][r0:r0 + 256, 0:1024], f"cv_qkv{j}")
                for part, c0 in ((0, 1024), (1, 1280)):
                    src = w_qkv[j][r0:r0 + 256, c0:c0 + 256].rearrange("r (h d) -> r h d", d=64)
                    for dup in range(2):
                        dst = wb_qkv[j][r0:r0 + 256, 1024 + part * 512:1024 + part * 512 + 512].rearrange(
                            "r (h u d) -> r h u d", u=2, d=64)[:, :, dup, :]
                        t = cdma(dst, src, f"cv_qkv{j}")
            conv[("qkv", j)] = t
            conv_hist.append(t)
            convert(("o", j), wb_o[j], w_o[j], 512, f"cv_o{j}")
        if kind == "F" and ("gu", l) not in conv:
            convert(("gu", l), wb_gu[l], w_gu[l], 128, f"cv_gu{l}")
            convert(("dn", l), wb_dn[l], w_dn[l], 256, f"cv_dn{l}")

    convert_stage(*stages[0])

    for j in a_layers:
        wst_f = tmpB[:, 0:1024].rearrange("p (g t) -> p g t", g=8)
        bsb_f = tmpB[:, 1024:2048].rearrange("p (g t) -> p g t", g=8)
        t_w = dma("sp", tmpB[:, 0:1024], wsTd[j], "c_ws", [conv.get(("prevA", j)), t_kcb])
        t_b = dma("sp", tmpB[:, 1024:2048], bsd[j].partition_broadcast(128), "c_bs", [conv.get(("prevA", j)), t_kcb])
        t_m = tt("dve", wsTm[j][:], wst_f, mcur.unsqueeze(1).to_broadcast([128, 8, 128]), ALU.mult, [t_w, t_kcb])
        toks = []
        for half in range(2):
            bi, bfree = aux.get()
            b = auxbank(bi)
            for gg in range(4):
                g = half * 4 + gg
                t_r = mm(ps[:, b, gg * 128:(gg + 1) * 128], ones_bf[:], wsTm[j][:, g, :], True, True,
                         [t_m, t_ones, bfree], signal=True)
            for cc in range(8):
                cv = half * 8 + cc
                g = cv // 2
                gg = g - half * 4
                t_c = stt(Cm[j][:, cv, :], ps[:, b, gg * 128:(gg + 1) * 128],
                          cst[:, C_A + j * 48 + 32 + cv:C_A + j * 48 + 33 + cv], bsb_f[:, g, :],
                          ALU.mult, ALU.add, [t_r, t_b, t_cst])
            aux.rel(bi, t_c)
            toks.append(t_c)
        conv[("prevA", j + 1)] = toks[-1]
    t_prolog_tmpB = conv.get(("prevA", (a_layers[-1] + 1) if a_layers else 0))

    for j in b_layers:
        t_e = act(esf[:], cst[:, C_SINK + j * 16:C_SINK + j * 16 + 16], AF.Exp, [t_cst, conv.get(("es", j - 1))])
        t_h = cp("dve", esh[:], esf[:], [t_e, conv.get(("es", j - 1))], force=True)
        t_hf = cp("dve", eshf[:], esh[:], [t_h], force=True)
        t_l = tt("dve", esl[:], esf[:], eshf[:], ALU.subtract, [t_hf], force=True)
        t_z = P.op("dve", lambda e, j=j: e.memset(es2[j][:], 0.0), [t_l], force=True)
        t_1 = cp("dve", es2[j][0:1, :], eshf[0:1, :], [t_z], force=True)
        t_2 = cp("dve", es2[j][32:33, :], esl[32:33, :], [t_1], force=True)
        conv[("es", j)] = t_2

    def rms_rinv(src, src_toks):
        bi, bfree = aux.get()
        b = auxbank(bi)
        t_m = None
        for k in range(8):
            si, sfree = sqr.get()
            t_s = act(sq_t[si][:], src[:, k, :], AF.Square, [src_toks[k], sfree])
            t_m = mm(ps[:, b, :], ones_bf[:], sq_t[si][:], k == 0, k == 7, [t_s, t_ones, bfree if k == 0 else None], signal=True)
            sqr.rel(si, t_m)
        ri, rfree = rinvr.get()
        t_l = act(lnv[:], ps[:, b, :], AF.Ln, [t_m, t_eps, lnv_free[0]], bias=eps_t[:, 0:1], scale=1.0 / D)
        t_r = act(rinv_t[ri][:], lnv[:], AF.Exp, [rfree, t_l], scale=-0.5)
        aux.rel(bi, t_l)
        lnv_free[0] = t_r
        return ri, t_r

    def rms_pre(gcol, h_toks):
        hT = hT_box[0]
        ri, t_r = rms_rinv(hT, h_toks)
        toks = []
        for k in range(8):
            toks.append(stt(hn[:, k, :], hT[:, k, :], cst[:, gcol + k:gcol + k + 1], rinv_t[ri][:],
                            ALU.mult, ALU.mult, [t_r, h_toks[k], t_cst]))
        rinvr.rel(ri, toks[-1])
        return toks

    def rms_post(gcol, y_toks, h_toks, out_buf=None):
        hT = hT_box[0]
        out_buf = hT if out_buf is None else out_buf
        ri, t_r = rms_rinv(y, y_toks)
        toks = []
        t_s = None
        for k in range(8):
            ti, tfree = tmpr.get()
            t_s = stt(tmp_t[ti][:], y[:, k, :], cst[:, gcol + k:gcol + k + 1], rinv_t[ri][:],
                      ALU.mult, ALU.mult, [t_r, tfree, t_cst])
            t_a = tt("pool" if k % 2 else "dve", out_buf[:, k, :], hT[:, k, :], tmp_t[ti][:], ALU.add, [t_s, h_toks[k]])
            tmpr.rel(ti, t_a)
            toks.append(t_a)
        rinvr.rel(ri, t_s)
        return toks

    def load_panel(view_fn, srcs, cdeps):
        si, sfree = wring.get()
        toks = []
        for (sel, src) in srcs:
            toks.append(dma("sp", sel(wring_t[si]), src, f"w{si}", [sfree, cdeps]))
        return si, toks

    def ffn(l, h_toks, last=False, after_pre=None):
        hid = big[:, 0:NFC * TT].rearrange("p (c t) -> p c t", c=NFC)
        hn_toks = rms_pre(c_norm(l, 2), h_toks)
        if after_pre is not None:
            after_pre(h_toks)
        hid_toks = []
        for cp2 in range(NFC // 2):
            c0 = cp2 * 256
            si, dt = load_panel(None, [
                (lambda s: s[:, 0:4096].rearrange("p (k g c) -> p k g c", k=8, g=2)[:, :, 0, :],
                 wb_gu[l][:, c0:c0 + 256].rearrange("(k p) c -> p k c", p=128)),
                (lambda s: s[:, 0:4096].rearrange("p (k g c) -> p k g c", k=8, g=2)[:, :, 1, :],
                 wb_gu[l][:, FH + c0:FH + c0 + 256].rearrange("(k p) c -> p k c", p=128)),
            ], conv[("gu", l)])
            sv = wring_t[si][:, 0:4096].rearrange("p (k g c) -> p k g c", k=8, g=2)
            t_u = None
            for cc in range(2):
                gb, gfree = main.get()
                ub, ufree = main.get()
                for k in range(8):
                    t_g = mm(ps[:, gb, :], sv[:, k, 0, cc * 128:(cc + 1) * 128], hn[:, k, :], k == 0, k == 7,
                             [dt, hn_toks[k], gfree if k == 0 else None], signal=(k == 7))
                for k in range(8):
                    t_u = mm(ps[:, ub, :], sv[:, k, 1, cc * 128:(cc + 1) * 128], hn[:, k, :], k == 0, k == 7,
                             [ufree if k == 0 else None], signal=(k == 7))
                gi, gf = sgr.get()
                t_s = act(sg_t[gi][:], ps[:, gb, :], AF.Silu, [t_g, gf])
                t_m = tt("dve", hid[:, cp2 * 2 + cc, :], sg_t[gi][:], ps[:, ub, :], ALU.mult, [t_s, t_u])
                main.rel(gb, t_s)
                main.rel(ub, t_m)
                sgr.rel(gi, t_m)
                hid_toks.append(t_m)
            wring.rel(si, t_u)
        y_toks = [None] * 8
        for dp in range(4):
            si, dt = load_panel(None, [
                (lambda s: s[:, 0:NFC * 256].rearrange("p (k c) -> p k c", k=NFC),
                 wb_dn[l][:, dp * 256:(dp + 1) * 256].rearrange("(k p) c -> p k c", p=128)),
            ], conv[("dn", l)])
            sv = wring_t[si][:, 0:NFC * 256].rearrange("p (k c) -> p k c", k=NFC)
            t_m = None
            for cc in range(2):
                d = dp * 2 + cc
                b, bfree = main.get()
                for k in range(NFC):
                    t_m = mm(ps[:, b, :], sv[:, k, cc * 128:(cc + 1) * 128], hid[:, k, :], k == 0, k == NFC - 1,
                             [dt, hid_toks[k], bfree if k == 0 else None], signal=(k == NFC - 1))
                t_y = act(y[:, d, :], ps[:, b, :], AF.Copy, [t_m, y_guard[0]])
                main.rel(b, t_y)
                y_toks[d] = t_y
            wring.rel(si, t_m)
        return rms_post(c_norm(l, 3), y_toks, h_toks, y if last else None)

    def mix_a(l, h_toks, last=False, after_pre=None):
        j = l // 2
        uT = big[:, 0:8192].rearrange("p (c t) -> p c t", c=16)
        vt = big[:, 8192:16384].rearrange("p (b f) -> p b f", b=NBLK)
        svt = [tmpB[:, 0:512], tmpB[:, 512:1024]]
        svr = Ring(2)
        t_rows = dma("pool", rows_bf[0:1, 0:2048], rowsd[0:1, R_BV + j * 2048:R_BV + (j + 1) * 2048], "c_rows", [rows_free[0], h_toks])
        hn_toks = rms_pre(c_norm(l, 0), h_toks)
        if after_pre is not None:
            after_pre(h_toks)
        cb = C_A + j * 48
        st_toks = [[None] * 4 for _ in range(NBLK)]
        for vp in range(4):
            si, dt = load_panel(None, [
                (lambda s: s[:, 0:4096].rearrange("p (k c) -> p k c", k=8),
                 wb_in[j][:, GW + vp * 512:GW + (vp + 1) * 512].rearrange("(k p) c -> p k c", p=128))], conv[("in", j)])
            sv = wring_t[si][:, 0:4096].rearrange("p (k c) -> p k c", k=8)
            t_m = None
            for blk in range(NBLK):
                b, bfree = main.get()
                for k in range(8):
                    mm(ps[:, b, :], hn[:, k, blk * 128:(blk + 1) * 128], sv[:, k, :], k == 0, False,
                       [dt, hn_toks[k], bfree if k == 0 else None])
                t_m = mm(ps[:, b, :], ones_bf[0:1, :], rows_bf[0:1, vp * 512:(vp + 1) * 512],
                         False, True, [t_rows, t_ones], signal=True)
                rows_free[0] = t_m
                t_v = act(vt[:, blk, vp * 512:(vp + 1) * 512], ps[:, b, :], AF.Gelu, [t_m])
                main.rel(b, t_v)
                st_toks[blk][vp] = P.op("dve", lambda e, blk=blk, vp=vp: e.bn_stats(out=lnst[:, blk, vp, :], in_=vt[:, blk, vp * 512:(vp + 1) * 512]), [t_v])
            wring.rel(si, t_m)
        u_toks = []
        for up in range(4):
            si, dt = load_panel(None, [
                (lambda s: s[:, 0:4096].rearrange("p (k c) -> p k c", k=8),
                 wb_in[j][:, up * 512:(up + 1) * 512].rearrange("(k p) c -> p k c", p=128))], conv[("in", j)])
            sv = wring_t[si][:, 0:4096].rearrange("p (k c) -> p k c", k=8)
            t_m = None
            for cc in range(4):
                cu = up * 4 + cc
                b, bfree = main.get()
                for k in range(8):
                    t_m = mm(ps[:, b, :], sv[:, k, cc * 128:(cc + 1) * 128], hn[:, k, :], k == 0, k == 7,
                             [dt, hn_toks[k], bfree if k == 0 else None], signal=(k == 7))
                t_u = act(uT[:, cu, :], ps[:, b, :], AF.Gelu, [t_m, t_cst], bias=cst[:, cb + cu:cb + cu + 1])
                main.rel(b, t_u)
                u_toks.append(t_u)
            wring.rel(si, t_m)
        vn_toks = []
        for blk in range(NBLK):
            t_a = P.op("dve", lambda e, blk=blk: e.bn_aggr(out=lnmv[:, blk, :], in_=lnst[:, blk, :, :].rearrange("p a s -> p (a s)")), [st_toks[blk]], True, True)
            t_e = ts("pool", lnr[:, blk:blk + 1], lnmv[:, blk, 1:2], LN_EPS, None, ALU.add, None, [t_a])
            t_p = tt("pool", lnr[:, blk:blk + 1], lnr[:, blk:blk + 1], mhalf[:, blk:blk + 1], ALU.pow, [t_e, t_mh], force=True)
            t_n = ts("dve", vt[:, blk, :], vt[:, blk, :], lnmv[:, blk, 0:1], lnr[:, blk:blk + 1], ALU.subtract, ALU.mult, [t_p, t_a], force=True)
            vn_toks.append(t_n)
        g_toks = []
        for cv in range(16):
            g = cv // 2
            b, bfree = main.get()
            t_m = None
            for blk in range(NBLK):
                t_m = mm(ps[:, b, blk * 128:(blk + 1) * 128], vt[:, blk, cv * 128:(cv + 1) * 128], wsTm[j][:, g, :], True, True,
                         [vn_toks[blk], bfree if blk == 0 else None, t_prolog_tmpB], signal=(blk == NBLK - 1))
            si2, sf2 = svr.get()
            t_s = stt(svt[si2].rearrange("p (b t) -> p b t", b=NBLK), ps[:, b, :].rearrange("p (b t) -> p b t", b=NBLK),
                      cst[:, cb + 16 + cv:cb + 17 + cv], Cm[j][:, cv, :].unsqueeze(1).to_broadcast([128, NBLK, 128]),
                      ALU.mult, ALU.add, [t_m, sf2])
            main.rel(b, t_s)
            t_g = tt("pool" if cv % 2 else "dve", uT[:, cv, :], uT[:, cv, :], svt[si2], ALU.mult, [t_s, u_toks[cv]])
            svr.rel(si2, t_g)
            g_toks.append(t_g)
        y_toks = [None] * 8
        for op_ in range(4):
            si, dt = load_panel(None, [
                (lambda s: s[:, 0:4096].rearrange("p (k c) -> p k c", k=16),
                 wb_out[j][:, op_ * 256:(op_ + 1) * 256].rearrange("(k p) c -> p k c", p=128))], conv[("out", j)])
            sv = wring_t[si][:, 0:4096].rearrange("p (k c) -> p k c", k=16)
            t_m = None
            for cc in range(2):
                d = op_ * 2 + cc
                b, bfree = main.get()
                for k in range(16):
                    t_m = mm(ps[:, b, :], sv[:, k, cc * 128:(cc + 1) * 128], uT[:, k, :], k == 0, k == 15,
                             [dt, g_toks[k], bfree if k == 0 else None], signal=(k == 15))
                t_y = act(y[:, d, :], ps[:, b, :], AF.Copy, [t_m, y_guard[0]])
                main.rel(b, t_y)
                y_toks[d] = t_y
            wring.rel(si, t_m)
        return rms_post(c_norm(l, 1), y_toks, h_toks, y if last else None)

    def mix_b(l, h_toks, ti, rope_tok, last=False):
        j = l // 2
        QT = big[:, 0:4096].rearrange("p (c t) -> p c t", c=8)
        aT = big[:, 4096:8192].rearrange("p (c t) -> p c t", c=8)
        KTe = big[:, 8192:8192 + 2560].rearrange("p (h s t) -> p h s t", h=4, s=5)
        KTo = big[:, 10752:10752 + 2560].rearrange("p (h s t) -> p h s t", h=4, s=5)
        Vv = big[:, 13312:13312 + 2560].rearrange("p (s f) -> p s f", s=5)
        pt = [pt_t[:, i * 512:(i + 1) * 512] for i in range(4)]
        ptr = Ring(4)
        qf = [tmpB[:, 0:512], tmpB[:, 512:1024]]
        t1 = tmpB[:, 1024:1536]
        lden = tmpB[:, 1536:2048]
        qb = [tmpA[:, 0:512], tmpA[:, 512:1024]]
        rden = [tmpA_f[:, 512:1024], tmpA_f[:, 1024:1536]]
        qr, rdr = Ring(2), Ring(2)
        first = (ti == 0)
        t_rows = dma("pool", rows_bf[0:1, 0:512], rowsd[0:1, R_BVD + j * 512:R_BVD + (j + 1) * 512], "c_rows", [rows_free[0], h_toks])
        hn_toks = rms_pre(c_norm(l, 0), h_toks)
        cb = C_B + j * 12
        t_ck = t_cv = None
        t_esf = None
        for hq in range(16):
            t_esf = ts("dve", es2f[:, hq * 128:(hq + 1) * 128], ones_bf[0:64, :], es2[j][:, hq:hq + 1], None, ALU.mult, None,
                       [conv[("es", j)], h_toks, t_ones])
        P.op("pool", lambda e: e.memset(KTe[64:128, :, :, :], 0.0), [h_toks])
        t_kz = P.op("pool", lambda e: e.memset(KTo[0:64, :, :, :], 0.0))
        if not first:
            cp("pool", KTe[0:64, :, 0, :], KTc[j][0:64, :, :])
            t_ck = cp("pool", KTo[64:128, :, 0, :], KTc[j][64:128, :, :])
            t_cv = cp("pool", Vv[:, 0, :], Vc[j][:])

        def rope_p1(b, t_m, bias_ap):
            qi, qfree = qr.get()
            t_f = act(qf[qi], ps[:, b, :], AF.Identity, [t_m, qfree, t_cst], bias=bias_ap)
            t_b = act(qb[qi], ps[:, b, :], AF.Identity, [], bias=bias_ap)
            main.rel(b, t_b)
            return (qi, t_f, t_b)

        def rope_p2(state, outs):
            qi, t_f, t_b = state
            b2, b2free = main.get()
            t_r = mm(ps[:, b2, :], Rm, qb[qi], True, True, [t_b, t_kcb, b2free], signal=True)
            t_a = tt("dve", t1, qf[qi], cosF[:], ALU.mult, [t_f, rope_tok, t1_free[0]])
            t_s = tt("dve", qf[qi], ps[:, b2, :], sinF[:], ALU.mult, [t_r, t_a])
            main.rel(b2, t_s)
            t_o = None
            for (lo, hi, out_ap) in outs:
                t_o = tt("dve", out_ap, t1[lo:hi, :], qf[qi][lo:hi, :], ALU.add, [t_a, t_s])
            t1_free[0] = t_o
            qr.rel(qi, t_o)
            return t_o

        pending = []
        t1_free = [None]

        def flush_rope():
            while pending:
                kind_, idx_, state_, outs_ = pending.pop(0)
                t_o = rope_p2(state_, outs_)
                if kind_ == "q":
                    q_toks[idx_] = t_o
                else:
                    k_toks[idx_] = t_o

        q_toks = [None] * 8
        k_toks = [None] * 4
        v_toks = [None] * NBLK
        for pn in range(4):
            si, dt = load_panel(None, [
                (lambda s: s[:, 0:4096].rearrange("p (k c) -> p k c", k=8),
                 wb_qkv[j][:, pn * 512:(pn + 1) * 512].rearrange("(k p) c -> p k c", p=128))], conv[("qkv", j)])
            sv = wring_t[si][:, 0:4096].rearrange("p (k c) -> p k c", k=8)
            t_m = None
            if pn < 3:
                for cc in range(4):
                    b, bfree = main.get()
                    for k in range(8):
                        t_m = mm(ps[:, b, :], sv[:, k, cc * 128:(cc + 1) * 128], hn[:, k, :], k == 0, k == 7,
                                 [dt, hn_toks[k], bfree if k == 0 else None], signal=(k == 7))
                    if pn < 2:
                        cq = pn * 4 + cc
                        st_ = rope_p1(b, t_m, cst[:, cb + cq:cb + cq + 1])
                        flush_rope()
                        pending.append(("q", cq, st_, [(0, 128, QT[:, cq, :])]))
                    else:
                        st_ = rope_p1(b, t_m, cst[:, cb + 8 + cc:cb + 9 + cc])
                        flush_rope()
                        pending.append(("k", cc, st_, [(0, 64, KTe[0:64, cc, 1:5, :].rearrange("p s t -> p (s t)")),
                                                       (64, 128, KTo[64:128, cc, 1:5, :].rearrange("p s t -> p (s t)"))]))
            else:
                for blk in range(NBLK):
                    b, bfree = main.get()
                    for k in range(8):
                        mm(ps[:, b, :], hn[:, k, blk * 128:(blk + 1) * 128], sv[:, k, :], k == 0, False,
                           [dt, hn_toks[k], bfree if k == 0 else None])
                    t_m = mm(ps[:, b, :], ones_bf[0:1, :], rows_bf[0:1, 0:512],
                             False, True, [t_rows, t_ones], signal=True)
                    rows_free[0] = t_m
                    t_v = act(Vv[:, 1 + blk, :], ps[:, b, :], AF.Copy, [t_m, t_cv])
                    main.rel(b, t_v)
                    v_toks[blk] = t_v
                    flush_rope()
            wring.rel(si, t_m)
        a_toks = [None] * 8
        last_pe = [None]
        lden_free = [None]

        def s_phase(blk, kv):
            has_prev = not (first and blk == 0)
            pts = []
            for kb in (([0] if has_prev else []) + [1]):
                slot = blk + kb
                sb_, sfree = main.get()
                mm(ps[:, sb_, 0:256], KTe[:, kv, slot, :],
                   QT[:, 2 * kv:2 * kv + 2, blk * 128:(blk + 1) * 128], True, True,
                   [k_toks[kv], t_ck, t_kz, q_toks[2 * kv], q_toks[2 * kv + 1], sfree])
                t_m = mm(ps[:, sb_, 256:512], KTo[:, kv, slot, :],
                         QT[:, 2 * kv:2 * kv + 2, blk * 128:(blk + 1) * 128], True, True, [], signal=True)
                pi, pfree = ptr.get()
                t_e = act(pt[pi], ps[:, sb_, :], AF.Exp, [t_m, pfree], scale=0.125)
                main.rel(sb_, t_e)
                msk = mprev if kb == 0 else mcur
                t_k = tt("dve", pt[pi].rearrange("p (a q) -> p a q", a=4), pt[pi].rearrange("p (a q) -> p a q", a=4),
                         msk.unsqueeze(1).to_broadcast([128, 4, 128]), ALU.mult, [t_e, t_kcb])
                pts.append((pi, slot, t_k))
            return pts

        def pv_phase(blk, kv, pts):
            ob, ofree = main.get()
            db, dfree = main.get()
            for n_, (pi, slot, t_k) in enumerate(pts):
                mm(ps[:, ob, :], Vv[:, slot, kv * 128:(kv + 1) * 128], pt[pi], n_ == 0, n_ == len(pts) - 1,
                   [t_k, v_toks[blk], t_cv, ofree if n_ == 0 else None])
            for n_, (pi, slot, t_k) in enumerate(pts):
                t_pv = mm(ps[:, db, :], ones_bf[:], pt[pi], n_ == 0, False, [dfree if n_ == 0 else None, t_ones], signal=True)
                ptr.rel(pi, t_pv)
            t_d = mm(ps[:, db, :], ones_bf[0:64, :], es2f[:, kv * 512:(kv + 1) * 512], False, True, [t_esf], signal=True)
            last_pe[0] = t_d
            t_l = act(lden, ps[:, db, :], AF.Ln, [t_d, lden_free[0]])
            main.rel(db, t_l)
            ri, rfree = rdr.get()
            t_r = act(rden[ri], lden, AF.Exp, [rfree, t_l], scale=-1.0)
            lden_free[0] = t_r
            tt("dve", aT[0:64, 2 * kv:2 * kv + 2, blk * 128:(blk + 1) * 128],
               ps[0:64, ob, 0:256].rearrange("p (a q) -> p a q", a=2),
               rden[ri][0:64, 0:256].rearrange("p (a q) -> p a q", a=2), ALU.mult, [t_r, t_d])
            t_n1 = tt("dve", aT[64:128, 2 * kv:2 * kv + 2, blk * 128:(blk + 1) * 128],
                      ps[64:128, ob, 256:512].rearrange("p (a q) -> p a q", a=2),
                      rden[ri][64:128, 256:512].rearrange("p (a q) -> p a q", a=2), ALU.mult, [t_r, t_d])
            main.rel(ob, t_n1)
            rdr.rel(ri, t_n1)
            a_toks[2 * kv] = t_n1
            a_toks[2 * kv + 1] = t_n1

        items = [(blk, kv) for blk in range(NBLK) for kv in range(4)]
        nxt = s_phase(*items[0])
        for ii, (blk, kv) in enumerate(items):
            cur = nxt
            if ii + 1 < len(items):
                nxt = s_phase(*items[ii + 1])
            pv_phase(blk, kv, cur)
        last_pe = last_pe[0]
        cp("pool", KTc[j][0:64, :, :], KTe[0:64, :, 4, :], [k_toks, last_pe])
        t_ko = cp("pool", KTc[j][64:128, :, :], KTo[64:128, :, 4, :])
        t_vo = cp("pool", Vc[j][:], Vv[:, 4, :], [v_toks[NBLK - 1], last_pe])
        y_toks = [None] * 8
        for op_ in range(2):
            si, dt = load_panel(None, [
                (lambda s: s[:, 0:4096].rearrange("p (k c) -> p k c", k=8),
                 wb_o[j][:, op_ * 512:(op_ + 1) * 512].rearrange("(k p) c -> p k c", p=128))], conv[("o", j)])
            sv = wring_t[si][:, 0:4096].rearrange("p (k c) -> p k c", k=8)
            t_m = None
            for cc in range(4):
                d = op_ * 4 + cc
                b, bfree = main.get()
                for k in range(8):
                    t_m = mm(ps[:, b, :], sv[:, k, cc * 128:(cc + 1) * 128], aT[:, k, :], k == 0, k == 7,
                             [dt, a_toks[k], bfree if k == 0 else None], signal=(k == 7))
                t_y = act(y[:, d, :], ps[:, b, :], AF.Copy, [t_m, y_guard[0]])
                main.rel(b, t_y)
                y_toks[d] = t_y
            wring.rel(si, t_m)
        return rms_post(c_norm(l, 1), y_toks, h_toks, y if last else None), [t_ko, t_vo]

    def rope_tables(t0, guard):
        posi = tmpB_i[:, 0:512]
        r = tmpB[:, 512:1024]
        nf = tmpB[:, 1024:1536]
        m = tmpB[:, 1536:2048]
        ni = tmpA_i[:, 0:512]
        w = tmpA_f[:, 1024:1536]
        fl = tmpA_f[:, 1536:2048]
        t_p = dma("pool", posi, pos[:, t0:t0 + TT].partition_broadcast(128), "c_pos", [guard])
        c = cp("dve", r, posi, [t_p, guard])
        c = ts("dve", r, r, cst[:, C_INVF:C_INVF + 1], 1.0 / (2 * np.pi), ALU.mult, ALU.mult, [t_cst, c])
        last = None
        for (dst, shift) in ((sinF, 0.0), (cosF, 0.25)):
            if shift:
                c = ts("dve", nf, r, shift, None, ALU.add, None, [c])
                c = cp("dve", ni, nf, [c])
                c = cp("dve", fl, ni, [c])
                c = tt("dve", nf, nf, fl, ALU.subtract, [c])
                fr = nf
            else:
                c = cp("dve", ni, r, [c])
                c = cp("dve", m, ni, [c])
                c = tt("dve", m, r, m, ALU.subtract, [c])
                fr = m
            c = P.op("dve", lambda e, fr=fr: e.tensor_single_scalar(out=w, in_=fr, scalar=0.5, op=ALU.is_gt), [c])
            c = tt("dve", fr, fr, w, ALU.subtract, [c])
            c = P.op("dve", lambda e, fr=fr: e.tensor_single_scalar(out=w, in_=fr, scalar=-0.5, op=ALU.is_lt), [c])
            c = tt("dve", fr, fr, w, ALU.add, [c])
            last = act(dst[:], fr, AF.Sin, [c], scale=2 * np.pi * (1 - 2e-6))
        return last

    prev_store = None
    x_free = [None, None]
    stage_guard = [t_prolog_tmpB] + [conv.get(("es", j)) for j in b_layers]
    carry = []
    fb = next((i for i, (k_, _) in enumerate(stages) if k_ == "B"), None)
    rope_box = [None]

    def x_load(ti_):
        bi_ = ti_ % 2
        return dma("sp", hTs[bi_][:], xT[:, ti_ * TT:(ti_ + 1) * TT].rearrange("(k p) t -> p k t", p=128),
                   "c_x%d" % bi_, [x_free[bi_]])

    t_x_next = x_load(0)
    for ti in range(n_tiles):
        if ti > 0:
            P.new_epoch()
        t0 = ti * TT
        hT_box[0] = hTs[ti % 2]
        t_x = t_x_next
        if ti + 1 < n_tiles:
            t_x_next = x_load(ti + 1)
        h_toks = [t_x] * 8
        y_guard[0] = prev_store
        rope_box[0] = None

        def mk_rope(h_toks_, t0=t0):
            rope_box[0] = rope_tables(t0, [stage_guard, h_toks_, carry])

        for si_, (kind, l) in enumerate(stages):
            last = (si_ == len(stages) - 1)
            if ti == 0 and si_ + 1 < len(stages):
                convert_stage(*stages[si_ + 1])
            cb_ = mk_rope if (fb is not None and si_ == fb - 1) else None
            if kind == "A":
                h_toks = mix_a(l, h_toks, last, cb_)
            elif kind == "B":
                if rope_box[0] is None:
                    mk_rope(h_toks)
                h_toks, carry = mix_b(l, h_toks, ti, rope_box[0], last)
            else:
                h_toks = ffn(l, h_toks, last, cb_)
            if rope_box[0] is not None:
                stage_guard = []
        x_free[ti % 2] = h_toks
        prev_store = dma("sp", oT[:, t0:t0 + TT].rearrange("(k p) t -> p k t", p=128), y[:], "c_o", [h_toks])
    P.op("sp", lambda e: e.nop(), [prev_store, dbg_toks], False)

    if SIM_CHECK:
        _simulate(P)
    sems = {}
    for n_, key in enumerate(P.semkeys):
        sems[key] = es.enter_context(nc.semaphore("s%d" % n_))
    block = es.enter_context(nc.Block())

    def make(name):
        def body(e):
            for (w, fn, key, inc) in P.st[name].ops:
                for (k, v) in w:
                    e.wait_ge(sems[k], v)
                ins = fn(e)
                if key is not None:
                    ins.then_inc(sems[key], inc)
        return body

    block.tensor(make("pe"))
    block.scalar(make("act"))
    block.vector(make("dve"))
    block.gpsimd(make("pool"))
    block.sync(make("sp"))
    es.close()
    return nc


def _col(v):
    return np.ascontiguousarray(v.reshape(-1, 128).T)


def host_consts(inp):
    f = np.float32
    cst = np.zeros((128, NCST), f)
    for l in range(4):
        for w, nm in enumerate(("pre_mix_g", "post_mix_g", "pre_ffn_g", "post_ffn_g")):
            cst[:, c_norm(l, w):c_norm(l, w) + 8] = _col(np.asarray(inp[nm][l], f))
    for j in range(2):
        cb = C_A + j * 48
        cst[:, cb:cb + 16] = _col(np.asarray(inp["a_b_in"][j][:GW], f))
        cst[:, cb + 16:cb + 32] = _col(np.asarray(inp["a_ln_g"][j], f))
        cst[:, cb + 32:cb + 48] = _col(np.asarray(inp["a_ln_b"][j], f))
        bq = np.asarray(inp["b_b_qkv"][j], f)
        cb = C_B + j * 12
        cst[:, cb:cb + 8] = _col(bq[:1024])
        bk = bq[1024:1280].reshape(4, 64)
        cst[:, cb + 8:cb + 12] = np.concatenate([bk, bk], axis=1).T
        sk = np.asarray(inp["b_sinks"][j], f).reshape(4, 4)[:, PERM].reshape(16)
        cst[:, C_SINK + j * 16:C_SINK + j * 16 + 16] = np.broadcast_to(sk, (128, 16))
    inv_freq = (np.float32(500000.0) ** (-np.arange(0, 16, 2, dtype=np.float32) / np.float32(16))).astype(f)
    p = np.arange(128) % 64
    cst[:, C_INVF] = np.where(p < 16, inv_freq[p % 8], 0.0)
    rows = np.zeros((1, NROWS), f)
    for j in range(2):
        rows[0, R_BV + j * 2048:R_BV + (j + 1) * 2048] = np.asarray(inp["a_b_in"][j][GW:], f)
        bv = np.asarray(inp["b_b_qkv"][j], f)[1280:1536].reshape(4, 1, 64)
        rows[0, R_BVD + j * 512:R_BVD + (j + 1) * 512] = np.broadcast_to(bv, (4, 2, 64)).reshape(512)
    kc = np.zeros((128, NKC), f)
    s = np.arange(128)[:, None]
    q = np.arange(128)[None, :]
    kc[:, KC_MCUR:KC_MCUR + 128] = (s <= q)
    kc[:, KC_MPREV:KC_MPREV + 128] = (s > q)
    rm = np.zeros((128, 128), f)
    for m in range(128):
        d = m % 64
        if d < 8:
            rm[m + 8, m] = -1.0
        elif d < 16:
            rm[m - 8, m] = 1.0
    kc[:, KC_RM:KC_RM + 128] = rm
    wsT = np.ascontiguousarray(np.asarray(inp["a_w_s"], f).transpose(0, 3, 1, 2)).reshape(2, 128, 1024)
    bs = np.ascontiguousarray(np.asarray(inp["a_b_s"], f)).reshape(2, 1, 1024)
    return cst, rows, kc, wsT, bs


_NC_CACHE = {}


def run(inp, stages=None, n_tiles=SEQ // TT, n_cores=8, trace=False):
    key = (tuple(stages) if stages else None, n_tiles)
    if key not in _NC_CACHE:
        _NC_CACHE[key] = build_nc(stages, n_tiles)
    nc = _NC_CACHE[key]
    S = n_tiles * TT
    cst, rows, kc, wsT, bs = host_consts(inp)
    x = np.asarray(inp["x"], np.float32)
    posn = np.asarray(inp["positions"], np.int32)
    shared = {
        "cst": cst, "rows": rows, "kc": kc, "wsT": wsT, "bs": bs,
        "a_w_in": np.asarray(inp["a_w_in"], np.float32), "a_w_out": np.asarray(inp["a_w_out"], np.float32),
        "b_w_qkv": np.asarray(inp["b_w_qkv"], np.float32), "b_w_o": np.asarray(inp["b_w_o"], np.float32),
        "ffn_w_gu": np.asarray(inp["ffn_w_gu"], np.float32), "ffn_w_down": np.asarray(inp["ffn_w_down"], np.float32),
    }
    in_maps = []
    for b in range(n_cores):
        m = dict(shared)
        m["xT"] = np.ascontiguousarray(x[b, :S, :].T)
        m["pos"] = np.ascontiguousarray(posn[b, :S].reshape(1, S))
        in_maps.append(m)
    res = run_bass_kernel_spmd(nc, in_maps, core_ids=list(range(n_cores)), trace=trace)
    out = np.stack([np.ascontiguousarray(r["oT"].T) for r in res.results], axis=0)
    return out, res


def kernel(**inputs):
    out, _ = run(inputs)
    return out.astype(np.float32)
```

```python
import numpy as np
from contextlib import ExitStack
import concourse.bass as bass
import concourse.mybir as mybir
from concourse.bass_utils import run_bass_kernel_spmd

F32, BF16, I32 = mybir.dt.float32, mybir.dt.bfloat16, mybir.dt.int32
AF = mybir.ActivationFunctionType
ALU = mybir.AluOpType

D = 1024
SEQ = 4096
TT = 512
NBLK = TT // 128
GW = 2048
FH = 2816
NFC = FH // 128
RMS_EPS = 1e-6
LN_EPS = 1e-5
PERM = [0, 2, 1, 3]
SLOT_ELEMS = 5632
NSLOT = 4
SAME_ENGINE_SYNC = True
DBG_B = 0
SIM_CHECK = False

def c_norm(l, w):
    return (l * 4 + w) * 8
C_A = 128
C_B = 224
C_INVF = 248
C_SINK = 249
NCST = 288
R_BV = 0
R_BVD = 4096
NROWS = 5120
KC_MCUR, KC_MPREV, KC_RM = 0, 128, 256
NKC = 384


class Tok:
    __slots__ = ("key", "val")

    def __init__(self, key, val):
        self.key, self.val = key, val


def _flat(deps, out):
    for d in deps:
        if d is None:
            continue
        if isinstance(d, (list, tuple)):
            _flat(d, out)
        else:
            out.append(d)
    return out


class Stream:
    def __init__(self, name):
        self.name, self.ops, self.epoch, self.count, self.waited = name, [], 0, 0, {}


class Prog:
    def __init__(self):
        self.st = {n: Stream(n) for n in ("pe", "act", "dve", "pool", "sp")}
        self.semkeys = {}
        self.dmacount = {}

    def new_epoch(self):
        for s in self.st.values():
            s.epoch += 1
            s.count = 0

    def _waits(self, st, deps, force=False):
        need = {}
        for d in _flat(deps, []):
            if d.key[0] == "E" and d.key[1] == st.name and not (SAME_ENGINE_SYNC or force):
                continue
            if need.get(d.key, 0) < d.val:
                need[d.key] = d.val
        w = []
        for key, val in need.items():
            if st.waited.get(key, 0) >= val:
                continue
            st.waited[key] = val
            w.append((key, val))
        return w

    def op(self, eng, fn, deps=(), signal=True, force=False):
        st = self.st[eng]
        w = self._waits(st, deps, force)
        key = tok = None
        if signal:
            st.count += 1
            key = ("E", eng, st.epoch)
            self.semkeys[key] = True
            tok = Tok(key, st.count)
        st.ops.append((w, fn, key, 1))
        return tok

    def dma(self, eng, fn, semname, deps=()):
        st = self.st[eng]
        w = self._waits(st, deps)
        key = ("D", semname)
        self.semkeys[key] = True
        self.dmacount[key] = self.dmacount.get(key, 0) + 16
        st.ops.append((w, fn, key, 16))
        return Tok(key, self.dmacount[key])


class Ring:
    def __init__(self, n):
        self.n, self.i, self.free = n, 0, [None] * n

    def get(self):
        idx = self.i
        self.i = (self.i + 1) % self.n
        return idx, self.free[idx]

    def rel(self, idx, tok):
        self.free[idx] = tok


def _simulate(P):
    sem = {}
    pc = {n: 0 for n in P.st}
    progress = True
    while progress:
        progress = False
        for n, st in P.st.items():
            while pc[n] < len(st.ops):
                w, fn, key, inc = st.ops[pc[n]]
                if any(sem.get(k, 0) < v for (k, v) in w):
                    break
                if key is not None:
                    sem[key] = sem.get(key, 0) + inc
                pc[n] += 1
                progress = True
    stuck = {n: pc[n] for n in P.st if pc[n] < len(P.st[n].ops)}
    for n, i in stuck.items():
        w = P.st[n].ops[i][0]
        print("SIM STUCK", n, "op", i, "of", len(P.st[n].ops), "waits", [(k, v, sem.get(k, 0)) for (k, v) in w if sem.get(k, 0) < v])
    if not stuck:
        print("SIM OK", {n: len(P.st[n].ops) for n in P.st})
    return not stuck


def default_stages():
    st = []
    for i in range(4):
        st.append(("A" if i % 2 == 0 else "B", i))
        st.append(("F", i))
    return st


def build_nc(stages=None, n_tiles=SEQ // TT):
    stages = default_stages() if stages is None else stages
    S = n_tiles * TT
    nc = bass.Bass("TRN2", target_bir_lowering=False)
    P = Prog()
    es = ExitStack()

    def dram(name, shape, dt, kind):
        return nc.dram_tensor(name, list(shape), dt, kind=kind).ap()

    xT = dram("xT", [D, S], F32, "ExternalInput")
    pos = dram("pos", [1, S], I32, "ExternalInput")
    cstd = dram("cst", [128, NCST], F32, "ExternalInput")
    rowsd = dram("rows", [1, NROWS], F32, "ExternalInput")
    kcd = dram("kc", [128, NKC], F32, "ExternalInput")
    wsTd = dram("wsT", [2, 128, 8 * 128], F32, "ExternalInput")
    bsd = dram("bs", [2, 1, 8 * 128], F32, "ExternalInput")
    w_in = dram("a_w_in", [2, D, 2 * GW], F32, "ExternalInput")
    w_out = dram("a_w_out", [2, GW, D], F32, "ExternalInput")
    w_qkv = dram("b_w_qkv", [2, D, 1536], F32, "ExternalInput")
    w_o = dram("b_w_o", [2, D, D], F32, "ExternalInput")
    w_gu = dram("ffn_w_gu", [4, D, 2 * FH], F32, "ExternalInput")
    w_dn = dram("ffn_w_down", [4, FH, D], F32, "ExternalInput")
    oT = dram("oT", [D, S], F32, "ExternalOutput")
    dbg = dram("dbg", [128, 4096], F32, "ExternalOutput") if DBG_B == 7 else None
    dbg_toks = []
    wb_in = dram("wb_in", [2, D, 2 * GW], BF16, "Internal")
    wb_out = dram("wb_out", [2, GW, D], BF16, "Internal")
    wb_qkv = dram("wb_qkv", [2, D, 2048], BF16, "Internal")
    wb_o = dram("wb_o", [2, D, D], BF16, "Internal")
    wb_gu = dram("wb_gu", [4, D, 2 * FH], BF16, "Internal")
    wb_dn = dram("wb_dn", [4, FH, D], BF16, "Internal")

    def sb(name, shape, dt):
        return es.enter_context(nc.sbuf_tensor(name, list(shape), dt))

    hTs = [sb("hT0", [128, 8, TT], F32), sb("hT1", [128, 8, TT], F32)]
    hT_box = [hTs[0]]
    hn = sb("hn", [128, 8, TT], BF16)
    y = sb("y", [128, 8, TT], F32)
    big = sb("big", [128, 16384], BF16)
    wring_t = [sb(f"wslot{i}", [128, SLOT_ELEMS], BF16) for i in range(NSLOT)]
    cst = sb("cst_sb", [128, NCST], F32)
    rows_bf = sb("rows_bf", [1, 2048], BF16)
    kc_b = sb("kc_b", [128, NKC], BF16)
    ones_bf = sb("ones_bf", [128, 128], BF16)
    eps_t = sb("eps_t", [128, 2], F32)
    Cm = [sb(f"Cm{j}", [128, 16, 128], F32) for j in range(2)]
    wsTm = [sb(f"wsTm{j}", [128, 8, 128], BF16) for j in range(2)]
    cosF = sb("cosF", [128, TT], F32)
    sinF = sb("sinF", [128, TT], F32)
    sq_t = [sb(f"sq{i}", [128, TT], BF16) for i in range(3)]
    lnv = sb("lnv", [128, TT], F32)
    rinv_t = [sb(f"rinv{i}", [128, TT], F32) for i in range(2)]
    tmp_t = [sb(f"tmp{i}", [128, TT], F32) for i in range(2)]
    sg_t = [sb(f"sg{i}", [128, TT], BF16) for i in range(3)]
    tmpA = sb("tmpA", [128, 8 * TT], BF16)
    tmpB = sb("tmpB", [128, 4 * TT], F32)
    es2 = [sb(f"es2_{j}", [64, 16], F32) for j in range(2)]
    es2f = sb("es2f", [64, 16 * 128], BF16)
    esf = sb("esf", [128, 16], F32)
    esh = sb("esh", [128, 16], BF16)
    esl = sb("esl", [128, 16], F32)
    eshf = sb("eshf", [128, 16], F32)
    KTc = [sb(f"KTc{j}", [128, 4, 128], BF16) for j in range(2)]
    Vc = [sb(f"Vc{j}", [128, 512], BF16) for j in range(2)]
    lnst = sb("lnst", [128, NBLK, 4, 6], F32)
    lnmv = sb("lnmv", [128, NBLK, 2], F32)
    lnr = sb("lnr", [128, NBLK], F32)
    mhalf = sb("mhalf", [128, NBLK], F32)
    ps = es.enter_context(nc.psum_tensor("ps", [128, 8, 512], F32))
    pt_t = sb("pt_t", [128, 4 * 512], BF16)
    tmpA_f = tmpA[:].bitcast(F32)
    tmpA_i = tmpA[:].bitcast(I32)
    tmpB_i = tmpB[:].bitcast(I32)

    y_guard = [None]
    lnv_free = [None]
    main = Ring(6)
    aux = Ring(2)
    sqr, rinvr, tmpr, sgr, wring = Ring(3), Ring(2), Ring(2), Ring(3), Ring(NSLOT)

    def auxbank(i):
        return 6 + i

    def mm(out, lhsT, rhs, start, stop, deps=(), signal=False):
        return P.op("pe", lambda e: e.matmul(out, lhsT=lhsT, rhs=rhs, start=start, stop=stop), deps, signal)

    def act(out, in_, func, deps=(), bias=None, scale=None, signal=True):
        kw = {}
        if bias is not None:
            kw["bias"] = bias
        if scale is not None:
            kw["scale"] = scale
        return P.op("act", lambda e: e.activation(out=out, in_=in_, func=func, **kw), deps, signal)

    def tt(eng, out, in0, in1, op, deps=(), signal=True, force=False):
        return P.op(eng, lambda e: e.tensor_tensor(out=out, in0=in0, in1=in1, op=op), deps, signal, force)

    def ts(eng, out, in0, s1, s2, op0, op1=None, deps=(), signal=True, force=False):
        if op1 is None:
            return P.op(eng, lambda e: e.tensor_scalar(out=out, in0=in0, scalar1=s1, scalar2=None, op0=op0), deps, signal, force)
        return P.op(eng, lambda e: e.tensor_scalar(out=out, in0=in0, scalar1=s1, scalar2=s2, op0=op0, op1=op1), deps, signal, force)

    def stt(out, in0, scalar, in1, op0, op1, deps=(), signal=True):
        return P.op("dve", lambda e: e.scalar_tensor_tensor(out=out, in0=in0, scalar=scalar, in1=in1, op0=op0, op1=op1), deps, signal)

    def cp(eng, out, in_, deps=(), signal=True, force=False):
        return P.op(eng, lambda e: e.tensor_copy(out=out, in_=in_), deps, signal, force)

    def dma(eng, out, in_, sem, deps=()):
        return P.dma(eng, lambda e: e.dma_start(out=out, in_=in_), sem, deps)

    a_layers = sorted({l // 2 for (k, l) in stages if k == "A"})
    b_layers = sorted({l // 2 for (k, l) in stages if k == "B"})
    f_layers = sorted({l for (k, l) in stages if k == "F"})

    t_cst = dma("sp", cst[:], cstd, "c_cst")
    t_kc = dma("sp", tmpB[:, 0:NKC], kcd, "c_kc")
    t_kcb = cp("dve", kc_b[:], tmpB[:, 0:NKC], [t_kc])
    rows_free = [None]
    t_ones = P.op("dve", lambda e: e.memset(ones_bf[:], 1.0))
    t_eps = P.op("dve", lambda e: e.memset(eps_t[:, 0:1], RMS_EPS))
    t_eps = P.op("dve", lambda e: e.memset(eps_t[:, 1:2], LN_EPS))
    t_mh = P.op("pool", lambda e: e.memset(mhalf[:], -0.5))
    mcur = kc_b[:, KC_MCUR:KC_MCUR + 128]
    mprev = kc_b[:, KC_MPREV:KC_MPREV + 128]
    Rm = kc_b[:, KC_RM:KC_RM + 128]

    conv = {}

    conv_hist = []

    def cdma(dst, src, sem):
        thr = conv_hist[-2] if len(conv_hist) >= 2 else None
        return dma("pool", dst, src, sem, [thr])

    def convert(name, dst, src, rows_per, sem):
        n = src.shape[0]
        t = None
        for r0 in range(0, n, rows_per):
            r1 = min(n, r0 + rows_per)
            t = cdma(dst[r0:r1, :], src[r0:r1, :], sem)
        conv[name] = t
        conv_hist.append(t)

    def convert_stage(kind, l):
        j = l // 2
        if kind == "A" and ("in", j) not in conv:
            convert(("in", j), wb_in[j], w_in[j], 128, f"cv_in{j}")
            convert(("out", j), wb_out[j], w_out[j], 512, f"cv_out{j}")
        if kind == "B" and ("qkv", j) not in conv:
            t = None
            for r0 in range(0, D, 256):
                t = cdma(wb_qkv[j][r0:r0 + 256, 0:1024], w_qkv[j][r0:r0 + 256, 0:1024], f"cv_qkv{j}")
                for part, c0 in ((0, 1024), (1, 1280)):
                    src = w_qkv[j][r0:r0 + 256, c0:c0 + 256].rearrange("r (h d) -> r h d", d=64)
                    for dup in range(2):
                        dst = wb_qkv[j][r0:r0 + 256, 1024 + part * 512:1024 + part * 512 + 512].rearrange(
                            "r (h u d) -> r h u d", u=2, d=64)[:, :, dup, :]
                        t = cdma(dst, src, f"cv_qkv{j}")
            conv[("qkv", j)] = t
            conv_hist.append(t)
            convert(("o", j), wb_o[j], w_o[j], 512, f"cv_o{j}")
        if kind == "F" and ("gu", l) not in conv:
            convert(("gu", l), wb_gu[l], w_gu[l], 128, f"cv_gu{l}")
            convert(("dn", l), wb_dn[l], w_dn[l], 256, f"cv_dn{l}")

    convert_stage(*stages[0])

    for j in a_layers:
        wst_f = tmpB[:, 0:1024].rearrange("p (g t) -> p g t", g=8)
        bsb_f = tmpB[:, 1024:2048].rearrange("p (g t) -> p g t", g=8)
        t_w = dma("sp", tmpB[:, 0:1024], wsTd[j], "c_ws", [conv.get(("prevA", j)), t_kcb])
        t_b = dma("sp", tmpB[:, 1024:2048], bsd[j].partition_broadcast(128), "c_bs", [conv.get(("prevA", j)), t_kcb])
        t_m = tt("dve", wsTm[j][:], wst_f, mcur.unsqueeze(1).to_broadcast([128, 8, 128]), ALU.mult, [t_w, t_kcb])
        toks = []
        for half in range(2):
            bi, bfree = aux.get()
            b = auxbank(bi)
            for gg in range(4):
                g = half * 4 + gg
                t_r = mm(ps[:, b, gg * 128:(gg + 1) * 128], ones_bf[:], wsTm[j][:, g, :], True, True,
                         [t_m, t_ones, bfree], signal=True)
            for cc in range(8):
                cv = half * 8 + cc
                g = cv // 2
                gg = g - half * 4
                t_c = stt(Cm[j][:, cv, :], ps[:, b, gg * 128:(gg + 1) * 128],
                          cst[:, C_A + j * 48 + 32 + cv:C_A + j * 48 + 33 + cv], bsb_f[:, g, :],
                          ALU.mult, ALU.add, [t_r, t_b, t_cst])
            aux.rel(bi, t_c)
            toks.append(t_c)
        conv[("prevA", j + 1)] = toks[-1]
    t_prolog_tmpB = conv.get(("prevA", (a_layers[-1] + 1) if a_layers else 0))

    for j in b_layers:
        t_e = act(esf[:], cst[:, C_SINK + j * 16:C_SINK + j * 16 + 16], AF.Exp, [t_cst, conv.get(("es", j - 1))])
        t_h = cp("dve", esh[:], esf[:], [t_e, conv.get(("es", j - 1))], force=True)
        t_hf = cp("dve", eshf[:], esh[:], [t_h], force=True)
        t_l = tt("dve", esl[:], esf[:], eshf[:], ALU.subtract, [t_hf], force=True)
        t_z = P.op("dve", lambda e, j=j: e.memset(es2[j][:], 0.0), [t_l], force=True)
        t_1 = cp("dve", es2[j][0:1, :], eshf[0:1, :], [t_z], force=True)
        t_2 = cp("dve", es2[j][32:33, :], esl[32:33, :], [t_1], force=True)
        conv[("es", j)] = t_2

    def rms_rinv(src, src_toks):
        bi, bfree = aux.get()
        b = auxbank(bi)
        t_m = None
        for k in range(8):
            si, sfree = sqr.get()
            t_s = act(sq_t[si][:], src[:, k, :], AF.Square, [src_toks[k], sfree])
            t_m = mm(ps[:, b, :], ones_bf[:], sq_t[si][:], k == 0, k == 7, [t_s, t_ones, bfree if k == 0 else None], signal=True)
            sqr.rel(si, t_m)
        ri, rfree = rinvr.get()
        t_l = act(lnv[:], ps[:, b, :], AF.Ln, [t_m, t_eps, lnv_free[0]], bias=eps_t[:, 0:1], scale=1.0 / D)
        t_r = act(rinv_t[ri][:], lnv[:], AF.Exp, [rfree, t_l], scale=-0.5)
        aux.rel(bi, t_l)
        lnv_free[0] = t_r
        return ri, t_r

    def rms_pre(gcol, h_toks):
        hT = hT_box[0]
        ri, t_r = rms_rinv(hT, h_toks)
        toks = []
        for k in range(8):
            toks.append(stt(hn[:, k, :], hT[:, k, :], cst[:, gcol + k:gcol + k + 1], rinv_t[ri][:],
                            ALU.mult, ALU.mult, [t_r, h_toks[k], t_cst]))
        rinvr.rel(ri, toks[-1])
        return toks

    def rms_post(gcol, y_toks, h_toks, out_buf=None):
        hT = hT_box[0]
        out_buf = hT if out_buf is None else out_buf
        ri, t_r = rms_rinv(y, y_toks)
        toks = []
        t_s = None
        for k in range(8):
            ti, tfree = tmpr.get()
            t_s = stt(tmp_t[ti][:], y[:, k, :], cst[:, gcol + k:gcol + k + 1], rinv_t[ri][:],
                      ALU.mult, ALU.mult, [t_r, tfree, t_cst])
            if k % 2:
                t_a = tt("pool", out_buf[:, k, :], hT[:, k, :], tmp_t[ti][:], ALU.add, [t_s, h_toks[k]])
            else:
                t_a = stt(out_buf[:, k, :], tmp_t[ti][:], 1.0, hT[:, k, :], ALU.mult, ALU.add, [t_s, h_toks[k]])
            tmpr.rel(ti, t_a)
            toks.append(t_a)
        rinvr.rel(ri, t_s)
        return toks

    def load_panel(view_fn, srcs, cdeps):
        si, sfree = wring.get()
        toks = []
        for (sel, src) in srcs:
            toks.append(dma("sp", sel(wring_t[si]), src, f"w{si}", [sfree, cdeps]))
        return si, toks

    def ffn(l, h_toks, last=False, after_pre=None):
        hid = big[:, 0:NFC * TT].rearrange("p (c t) -> p c t", c=NFC)
        hn_toks = rms_pre(c_norm(l, 2), h_toks)
        if after_pre is not None:
            after_pre(h_toks)
        hid_toks = []
        for cp2 in range(NFC // 2):
            c0 = cp2 * 256
            si, dt = load_panel(None, [
                (lambda s: s[:, 0:4096].rearrange("p (k g c) -> p k g c", k=8, g=2)[:, :, 0, :],
                 wb_gu[l][:, c0:c0 + 256].rearrange("(k p) c -> p k c", p=128)),
                (lambda s: s[:, 0:4096].rearrange("p (k g c) -> p k g c", k=8, g=2)[:, :, 1, :],
                 wb_gu[l][:, FH + c0:FH + c0 + 256].rearrange("(k p) c -> p k c", p=128)),
            ], conv[("gu", l)])
            sv = wring_t[si][:, 0:4096].rearrange("p (k g c) -> p k g c", k=8, g=2)
            t_u = None
            for cc in range(2):
                gb, gfree = main.get()
                ub, ufree = main.get()
                for k in range(8):
                    t_g = mm(ps[:, gb, :], sv[:, k, 0, cc * 128:(cc + 1) * 128], hn[:, k, :], k == 0, k == 7,
                             [dt, hn_toks[k], gfree if k == 0 else None], signal=(k == 7))
                for k in range(8):
                    t_u = mm(ps[:, ub, :], sv[:, k, 1, cc * 128:(cc + 1) * 128], hn[:, k, :], k == 0, k == 7,
                             [ufree if k == 0 else None], signal=(k == 7))
                gi, gf = sgr.get()
                t_s = act(sg_t[gi][:], ps[:, gb, :], AF.Silu, [t_g, gf])
                t_m = tt("dve", hid[:, cp2 * 2 + cc, :], sg_t[gi][:], ps[:, ub, :], ALU.mult, [t_s, t_u])
                main.rel(gb, t_s)
                main.rel(ub, t_m)
                sgr.rel(gi, t_m)
                hid_toks.append(t_m)
            wring.rel(si, t_u)
        y_toks = [None] * 8
        for dp in range(4):
            si, dt = load_panel(None, [
                (lambda s: s[:, 0:NFC * 256].rearrange("p (k c) -> p k c", k=NFC),
                 wb_dn[l][:, dp * 256:(dp + 1) * 256].rearrange("(k p) c -> p k c", p=128)),
            ], conv[("dn", l)])
            sv = wring_t[si][:, 0:NFC * 256].rearrange("p (k c) -> p k c", k=NFC)
            t_m = None
            for cc in range(2):
                d = dp * 2 + cc
                b, bfree = main.get()
                for k in range(NFC):
                    t_m = mm(ps[:, b, :], sv[:, k, cc * 128:(cc + 1) * 128], hid[:, k, :], k == 0, k == NFC - 1,
                             [dt, hid_toks[k], bfree if k == 0 else None], signal=(k == NFC - 1))
                t_y = act(y[:, d, :], ps[:, b, :], AF.Copy, [t_m, y_guard[0]])
                main.rel(b, t_y)
                y_toks[d] = t_y
            wring.rel(si, t_m)
        return rms_post(c_norm(l, 3), y_toks, h_toks, y if last else None)

    def mix_a(l, h_toks, last=False, after_pre=None):
        j = l // 2
        uT = big[:, 0:8192].rearrange("p (c t) -> p c t", c=16)
        vt = big[:, 8192:16384].rearrange("p (b f) -> p b f", b=NBLK)
        svt = [tmpB[:, 0:512], tmpB[:, 512:1024]]
        svr = Ring(2)
        t_rows = dma("pool", rows_bf[0:1, 0:2048], rowsd[0:1, R_BV + j * 2048:R_BV + (j + 1) * 2048], "c_rows", [rows_free[0], h_toks])
        hn_toks = rms_pre(c_norm(l, 0), h_toks)
        if after_pre is not None:
            after_pre(h_toks)
        cb = C_A + j * 48
        st_toks = [[None] * 4 for _ in range(NBLK)]
        for vp in range(4):
            si, dt = load_panel(None, [
                (lambda s: s[:, 0:4096].rearrange("p (k c) -> p k c", k=8),
                 wb_in[j][:, GW + vp * 512:GW + (vp + 1) * 512].rearrange("(k p) c -> p k c", p=128))], conv[("in", j)])
            sv = wring_t[si][:, 0:4096].rearrange("p (k c) -> p k c", k=8)
            t_m = None
            for blk in range(NBLK):
                b, bfree = main.get()
                for k in range(8):
                    mm(ps[:, b, :], hn[:, k, blk * 128:(blk + 1) * 128], sv[:, k, :], k == 0, False,
                       [dt, hn_toks[k], bfree if k == 0 else None])
                t_m = mm(ps[:, b, :], ones_bf[0:1, :], rows_bf[0:1, vp * 512:(vp + 1) * 512],
                         False, True, [t_rows, t_ones], signal=True)
                rows_free[0] = t_m
                t_v = act(vt[:, blk, vp * 512:(vp + 1) * 512], ps[:, b, :], AF.Gelu, [t_m])
                main.rel(b, t_v)
                st_toks[blk][vp] = P.op("dve", lambda e, blk=blk, vp=vp: e.bn_stats(out=lnst[:, blk, vp, :], in_=vt[:, blk, vp * 512:(vp + 1) * 512]), [t_v])
            wring.rel(si, t_m)
        u_toks = []
        for up in range(4):
            si, dt = load_panel(None, [
                (lambda s: s[:, 0:4096].rearrange("p (k c) -> p k c", k=8),
                 wb_in[j][:, up * 512:(up + 1) * 512].rearrange("(k p) c -> p k c", p=128))], conv[("in", j)])
            sv = wring_t[si][:, 0:4096].rearrange("p (k c) -> p k c", k=8)
            t_m = None
            for cc in range(4):
                cu = up * 4 + cc
                b, bfree = main.get()
                for k in range(8):
                    t_m = mm(ps[:, b, :], sv[:, k, cc * 128:(cc + 1) * 128], hn[:, k, :], k == 0, k == 7,
                             [dt, hn_toks[k], bfree if k == 0 else None], signal=(k == 7))
                t_u = act(uT[:, cu, :], ps[:, b, :], AF.Gelu, [t_m, t_cst], bias=cst[:, cb + cu:cb + cu + 1])
                main.rel(b, t_u)
                u_toks.append(t_u)
            wring.rel(si, t_m)
        vn_toks = []
        for blk in range(NBLK):
            t_a = P.op("dve", lambda e, blk=blk: e.bn_aggr(out=lnmv[:, blk, :], in_=lnst[:, blk, :, :].rearrange("p a s -> p (a s)")), [st_toks[blk]], True, True)
            t_e = ts("pool", lnr[:, blk:blk + 1], lnmv[:, blk, 1:2], LN_EPS, None, ALU.add, None, [t_a])
            t_p = tt("pool", lnr[:, blk:blk + 1], lnr[:, blk:blk + 1], mhalf[:, blk:blk + 1], ALU.pow, [t_e, t_mh], force=True)
            t_n = ts("dve", vt[:, blk, :], vt[:, blk, :], lnmv[:, blk, 0:1], lnr[:, blk:blk + 1], ALU.subtract, ALU.mult, [t_p, t_a], force=True)
            vn_toks.append(t_n)
        g_toks = []
        for cv in range(16):
            g = cv // 2
            b, bfree = main.get()
            t_m = None
            for blk in range(NBLK):
                t_m = mm(ps[:, b, blk * 128:(blk + 1) * 128], vt[:, blk, cv * 128:(cv + 1) * 128], wsTm[j][:, g, :], True, True,
                         [vn_toks[blk], bfree if blk == 0 else None, t_prolog_tmpB], signal=(blk == NBLK - 1))
            si2, sf2 = svr.get()
            t_s = stt(svt[si2].rearrange("p (b t) -> p b t", b=NBLK), ps[:, b, :].rearrange("p (b t) -> p b t", b=NBLK),
                      cst[:, cb + 16 + cv:cb + 17 + cv], Cm[j][:, cv, :].unsqueeze(1).to_broadcast([128, NBLK, 128]),
                      ALU.mult, ALU.add, [t_m, sf2])
            main.rel(b, t_s)
            t_g = tt("pool" if cv % 2 else "dve", uT[:, cv, :], uT[:, cv, :], svt[si2], ALU.mult, [t_s, u_toks[cv]])
            svr.rel(si2, t_g)
            g_toks.append(t_g)
        y_toks = [None] * 8
        for op_ in range(4):
            si, dt = load_panel(None, [
                (lambda s: s[:, 0:4096].rearrange("p (k c) -> p k c", k=16),
                 wb_out[j][:, op_ * 256:(op_ + 1) * 256].rearrange("(k p) c -> p k c", p=128))], conv[("out", j)])
            sv = wring_t[si][:, 0:4096].rearrange("p (k c) -> p k c", k=16)
            t_m = None
            for cc in range(2):
                d = op_ * 2 + cc
                b, bfree = main.get()
                for k in range(16):
                    t_m = mm(ps[:, b, :], sv[:, k, cc * 128:(cc + 1) * 128], uT[:, k, :], k == 0, k == 15,
                             [dt, g_toks[k], bfree if k == 0 else None], signal=(k == 15))
                t_y = act(y[:, d, :], ps[:, b, :], AF.Copy, [t_m, y_guard[0]])
                main.rel(b, t_y)
                y_toks[d] = t_y
            wring.rel(si, t_m)
        return rms_post(c_norm(l, 1), y_toks, h_toks, y if last else None)

    def mix_b(l, h_toks, ti, rope_tok, last=False):
        j = l // 2
        QT = big[:, 0:4096].rearrange("p (c t) -> p c t", c=8)
        aT = big[:, 4096:8192].rearrange("p (c t) -> p c t", c=8)
        KTe = big[:, 8192:8192 + 2560].rearrange("p (h s t) -> p h s t", h=4, s=5)
        KTo = big[:, 10752:10752 + 2560].rearrange("p (h s t) -> p h s t", h=4, s=5)
        Vv = big[:, 13312:13312 + 2560].rearrange("p (s f) -> p s f", s=5)
        pt = [pt_t[:, i * 512:(i + 1) * 512] for i in range(4)]
        ptr = Ring(4)
        qf = [tmpB[:, 0:512], tmpB[:, 512:1024]]
        t1 = tmpB[:, 1024:1536]
        lden = tmpB[:, 1536:2048]
        qb = [tmpA[:, 0:512], tmpA[:, 512:1024]]
        rden = [tmpA_f[:, 512:1024], tmpA_f[:, 1024:1536]]
        qr, rdr = Ring(2), Ring(2)
        first = (ti == 0)
        t_rows = dma("pool", rows_bf[0:1, 0:512], rowsd[0:1, R_BVD + j * 512:R_BVD + (j + 1) * 512], "c_rows", [rows_free[0], h_toks])
        hn_toks = rms_pre(c_norm(l, 0), h_toks)
        cb = C_B + j * 12
        t_ck = t_cv = None
        t_esf = None
        for hq in range(16):
            t_esf = ts("dve", es2f[:, hq * 128:(hq + 1) * 128], ones_bf[0:64, :], es2[j][:, hq:hq + 1], None, ALU.mult, None,
                       [conv[("es", j)], h_toks, t_ones])
        P.op("pool", lambda e: e.memset(KTe[64:128, :, :, :], 0.0), [h_toks])
        t_kz = P.op("pool", lambda e: e.memset(KTo[0:64, :, :, :], 0.0))
        if not first:
            cp("pool", KTe[0:64, :, 0, :], KTc[j][0:64, :, :])
            t_ck = cp("pool", KTo[64:128, :, 0, :], KTc[j][64:128, :, :])
            t_cv = cp("pool", Vv[:, 0, :], Vc[j][:])

        def rope_p1(b, t_m, bias_ap):
            qi, qfree = qr.get()
            t_f = act(qf[qi], ps[:, b, :], AF.Identity, [t_m, qfree, t_cst], bias=bias_ap)
            t_b = act(qb[qi], ps[:, b, :], AF.Identity, [], bias=bias_ap)
            main.rel(b, t_b)
            return (qi, t_f, t_b)

        def rope_p2(state, outs):
            qi, t_f, t_b = state
            b2, b2free = main.get()
            t_r = mm(ps[:, b2, :], Rm, qb[qi], True, True, [t_b, t_kcb, b2free], signal=True)
            t_a = tt("dve", t1, qf[qi], cosF[:], ALU.mult, [t_f, rope_tok, t1_free[0]])
            t_s = tt("dve", qf[qi], ps[:, b2, :], sinF[:], ALU.mult, [t_r, t_a])
            main.rel(b2, t_s)
            t_o = None
            for (lo, hi, out_ap) in outs:
                t_o = tt("dve", out_ap, t1[lo:hi, :], qf[qi][lo:hi, :], ALU.add, [t_a, t_s])
            t1_free[0] = t_o
            qr.rel(qi, t_o)
            return t_o

        pending = []
        t1_free = [None]

        def flush_rope():
            while pending:
                kind_, idx_, state_, outs_ = pending.pop(0)
                t_o = rope_p2(state_, outs_)
                if kind_ == "q":
                    q_toks[idx_] = t_o
                else:
                    k_toks[idx_] = t_o

        q_toks = [None] * 8
        k_toks = [None] * 4
        v_toks = [None] * NBLK
        for pn in range(4):
            si, dt = load_panel(None, [
                (lambda s: s[:, 0:4096].rearrange("p (k c) -> p k c", k=8),
                 wb_qkv[j][:, pn * 512:(pn + 1) * 512].rearrange("(k p) c -> p k c", p=128))], conv[("qkv", j)])
            sv = wring_t[si][:, 0:4096].rearrange("p (k c) -> p k c", k=8)
            t_m = None
            if pn < 3:
                for cc in range(4):
                    b, bfree = main.get()
                    for k in range(8):
                        t_m = mm(ps[:, b, :], sv[:, k, cc * 128:(cc + 1) * 128], hn[:, k, :], k == 0, k == 7,
                                 [dt, hn_toks[k], bfree if k == 0 else None], signal=(k == 7))
                    if pn < 2:
                        cq = pn * 4 + cc
                        st_ = rope_p1(b, t_m, cst[:, cb + cq:cb + cq + 1])
                        flush_rope()
                        pending.append(("q", cq, st_, [(0, 128, QT[:, cq, :])]))
                    else:
                        st_ = rope_p1(b, t_m, cst[:, cb + 8 + cc:cb + 9 + cc])
                        flush_rope()
                        pending.append(("k", cc, st_, [(0, 64, KTe[0:64, cc, 1:5, :].rearrange("p s t -> p (s t)")),
                                                       (64, 128, KTo[64:128, cc, 1:5, :].rearrange("p s t -> p (s t)"))]))
            else:
                for blk in range(NBLK):
                    b, bfree = main.get()
                    for k in range(8):
                        mm(ps[:, b, :], hn[:, k, blk * 128:(blk + 1) * 128], sv[:, k, :], k == 0, False,
                           [dt, hn_toks[k], bfree if k == 0 else None])
                    t_m = mm(ps[:, b, :], ones_bf[0:1, :], rows_bf[0:1, 0:512],
                             False, True, [t_rows, t_ones], signal=True)
                    rows_free[0] = t_m
                    t_v = act(Vv[:, 1 + blk, :], ps[:, b, :], AF.Copy, [t_m, t_cv])
                    main.rel(b, t_v)
                    v_toks[blk] = t_v
                    flush_rope()
            wring.rel(si, t_m)
        a_toks = [None] * 8
        last_pe = [None]
        lden_free = [None]

        def s_phase(blk, kv):
            has_prev = not (first and blk == 0)
            pts = []
            for kb in (([0] if has_prev else []) + [1]):
                slot = blk + kb
                sb_, sfree = main.get()
                mm(ps[:, sb_, 0:256], KTe[:, kv, slot, :],
                   QT[:, 2 * kv:2 * kv + 2, blk * 128:(blk + 1) * 128], True, True,
                   [k_toks[kv], t_ck, t_kz, q_toks[2 * kv], q_toks[2 * kv + 1], sfree])
                t_m = mm(ps[:, sb_, 256:512], KTo[:, kv, slot, :],
                         QT[:, 2 * kv:2 * kv + 2, blk * 128:(blk + 1) * 128], True, True, [], signal=True)
                pi, pfree = ptr.get()
                t_e = act(pt[pi], ps[:, sb_, :], AF.Exp, [t_m, pfree], scale=0.125)
                main.rel(sb_, t_e)
                msk = mprev if kb == 0 else mcur
                t_k = tt("dve", pt[pi].rearrange("p (a q) -> p a q", a=4), pt[pi].rearrange("p (a q) -> p a q", a=4),
                         msk.unsqueeze(1).to_broadcast([128, 4, 128]), ALU.mult, [t_e, t_kcb])
                pts.append((pi, slot, t_k))
            return pts

        def pv_phase(blk, kv, pts):
            ob, ofree = main.get()
            db, dfree = main.get()
            for n_, (pi, slot, t_k) in enumerate(pts):
                mm(ps[:, ob, :], Vv[:, slot, kv * 128:(kv + 1) * 128], pt[pi], n_ == 0, n_ == len(pts) - 1,
                   [t_k, v_toks[blk], t_cv, ofree if n_ == 0 else None])
            for n_, (pi, slot, t_k) in enumerate(pts):
                t_pv = mm(ps[:, db, :], ones_bf[:], pt[pi], n_ == 0, False, [dfree if n_ == 0 else None, t_ones], signal=True)
                ptr.rel(pi, t_pv)
            t_d = mm(ps[:, db, :], ones_bf[0:64, :], es2f[:, kv * 512:(kv + 1) * 512], False, True, [t_esf], signal=True)
            last_pe[0] = t_d
            t_l = act(lden, ps[:, db, :], AF.Ln, [t_d, lden_free[0]])
            main.rel(db, t_l)
            ri, rfree = rdr.get()
            t_r = act(rden[ri], lden, AF.Exp, [rfree, t_l], scale=-1.0)
            lden_free[0] = t_r
            tt("dve", aT[0:64, 2 * kv:2 * kv + 2, blk * 128:(blk + 1) * 128],
               ps[0:64, ob, 0:256].rearrange("p (a q) -> p a q", a=2),
               rden[ri][0:64, 0:256].rearrange("p (a q) -> p a q", a=2), ALU.mult, [t_r, t_d])
            t_n1 = tt("dve", aT[64:128, 2 * kv:2 * kv + 2, blk * 128:(blk + 1) * 128],
                      ps[64:128, ob, 256:512].rearrange("p (a q) -> p a q", a=2),
                      rden[ri][64:128, 256:512].rearrange("p (a q) -> p a q", a=2), ALU.mult, [t_r, t_d])
            main.rel(ob, t_n1)
            rdr.rel(ri, t_n1)
            a_toks[2 * kv] = t_n1
            a_toks[2 * kv + 1] = t_n1

        items = [(blk, kv) for blk in range(NBLK) for kv in range(4)]
        nxt = s_phase(*items[0])
        for ii, (blk, kv) in enumerate(items):
            cur = nxt
            if ii + 1 < len(items):
                nxt = s_phase(*items[ii + 1])
            pv_phase(blk, kv, cur)
        last_pe = last_pe[0]
        cp("pool", KTc[j][0:64, :, :], KTe[0:64, :, 4, :], [k_toks, last_pe])
        t_ko = cp("pool", KTc[j][64:128, :, :], KTo[64:128, :, 4, :])
        t_vo = cp("pool", Vc[j][:], Vv[:, 4, :], [v_toks[NBLK - 1], last_pe])
        y_toks = [None] * 8
        for op_ in range(2):
            si, dt = load_panel(None, [
                (lambda s: s[:, 0:4096].rearrange("p (k c) -> p k c", k=8),
                 wb_o[j][:, op_ * 512:(op_ + 1) * 512].rearrange("(k p) c -> p k c", p=128))], conv[("o", j)])
            sv = wring_t[si][:, 0:4096].rearrange("p (k c) -> p k c", k=8)
            t_m = None
            for cc in range(4):
                d = op_ * 4 + cc
                b, bfree = main.get()
                for k in range(8):
                    t_m = mm(ps[:, b, :], sv[:, k, cc * 128:(cc + 1) * 128], aT[:, k, :], k == 0, k == 7,
                             [dt, a_toks[k], bfree if k == 0 else None], signal=(k == 7))
                t_y = act(y[:, d, :], ps[:, b, :], AF.Copy, [t_m, y_guard[0]])
                main.rel(b, t_y)
                y_toks[d] = t_y
            wring.rel(si, t_m)
        return rms_post(c_norm(l, 1), y_toks, h_toks, y if last else None), [t_ko, t_vo]

    def rope_tables(t0, guard):
        posi = tmpB_i[:, 0:512]
        r = tmpB[:, 512:1024]
        nf = tmpB[:, 1024:1536]
        m = tmpB[:, 1536:2048]
        ni = tmpA_i[:, 0:512]
        w = tmpA_f[:, 1024:1536]
        fl = tmpA_f[:, 1536:2048]
        t_p = dma("pool", posi, pos[:, t0:t0 + TT].partition_broadcast(128), "c_pos", [guard])
        c = cp("dve", r, posi, [t_p, guard])
        c = ts("dve", r, r, cst[:, C_INVF:C_INVF + 1], 1.0 / (2 * np.pi), ALU.mult, ALU.mult, [t_cst, c])
        last = None
        for (dst, shift) in ((sinF, 0.0), (cosF, 0.25)):
            if shift:
                c = ts("dve", nf, r, shift, None, ALU.add, None, [c])
                c = cp("dve", ni, nf, [c])
                c = cp("dve", fl, ni, [c])
                c = tt("dve", nf, nf, fl, ALU.subtract, [c])
                fr = nf
            else:
                c = cp("dve", ni, r, [c])
                c = cp("dve", m, ni, [c])
                c = tt("dve", m, r, m, ALU.subtract, [c])
                fr = m
            c = P.op("dve", lambda e, fr=fr: e.tensor_single_scalar(out=w, in_=fr, scalar=0.5, op=ALU.is_gt), [c])
            c = tt("dve", fr, fr, w, ALU.subtract, [c])
            c = P.op("dve", lambda e, fr=fr: e.tensor_single_scalar(out=w, in_=fr, scalar=-0.5, op=ALU.is_lt), [c])
            c = tt("dve", fr, fr, w, ALU.add, [c])
            last = act(dst[:], fr, AF.Sin, [c], scale=2 * np.pi * (1 - 2e-6))
        return last

    prev_store = None
    x_free = [None, None]
    stage_guard = [t_prolog_tmpB] + [conv.get(("es", j)) for j in b_layers]
    carry = []
    fb = next((i for i, (k_, _) in enumerate(stages) if k_ == "B"), None)
    rope_box = [None]

    def x_load(ti_):
        bi_ = ti_ % 2
        return dma("sp", hTs[bi_][:], xT[:, ti_ * TT:(ti_ + 1) * TT].rearrange("(k p) t -> p k t", p=128),
                   "c_x%d" % bi_, [x_free[bi_]])

    t_x_next = x_load(0)
    for ti in range(n_tiles):
        if ti > 0:
            P.new_epoch()
        t0 = ti * TT
        hT_box[0] = hTs[ti % 2]
        t_x = t_x_next
        h_toks = [t_x] * 8
        y_guard[0] = prev_store
        rope_box[0] = None

        def mk_rope(h_toks_, t0=t0):
            rope_box[0] = rope_tables(t0, [stage_guard, h_toks_, carry])

        for si_, (kind, l) in enumerate(stages):
            last = (si_ == len(stages) - 1)
            if ti == 0 and si_ + 1 < len(stages):
                convert_stage(*stages[si_ + 1])
            cb_ = mk_rope if (fb is not None and si_ == fb - 1) else None
            if kind == "A":
                h_toks = mix_a(l, h_toks, last, cb_)
            elif kind == "B":
                if rope_box[0] is None:
                    mk_rope(h_toks)
                h_toks, carry = mix_b(l, h_toks, ti, rope_box[0], last)
            else:
                h_toks = ffn(l, h_toks, last, cb_)
            if rope_box[0] is not None:
                stage_guard = []
            if si_ == 0 and ti + 1 < n_tiles:
                t_x_next = x_load(ti + 1)
        x_free[ti % 2] = h_toks
        prev_store = dma("pool", oT[:, t0:t0 + TT].rearrange("(k p) t -> p k t", p=128), y[:], "c_o", [h_toks])
    P.op("sp", lambda e: e.nop(), [prev_store, dbg_toks], False)

    if SIM_CHECK:
        _simulate(P)
    sems = {}
    for n_, key in enumerate(P.semkeys):
        sems[key] = es.enter_context(nc.semaphore("s%d" % n_))
    block = es.enter_context(nc.Block())

    def make(name):
        def body(e):
            for (w, fn, key, inc) in P.st[name].ops:
                for (k, v) in w:
                    e.wait_ge(sems[k], v)
                ins = fn(e)
                if key is not None:
                    ins.then_inc(sems[key], inc)
        return body

    block.tensor(make("pe"))
    block.scalar(make("act"))
    block.vector(make("dve"))
    block.gpsimd(make("pool"))
    block.sync(make("sp"))
    es.close()
    return nc


def _col(v):
    return np.ascontiguousarray(v.reshape(-1, 128).T)


def host_consts(inp):
    f = np.float32
    cst = np.zeros((128, NCST), f)
    for l in range(4):
        for w, nm in enumerate(("pre_mix_g", "post_mix_g", "pre_ffn_g", "post_ffn_g")):
            cst[:, c_norm(l, w):c_norm(l, w) + 8] = _col(np.asarray(inp[nm][l], f))
    for j in range(2):
        cb = C_A + j * 48
        cst[:, cb:cb + 16] = _col(np.asarray(inp["a_b_in"][j][:GW], f))
        cst[:, cb + 16:cb + 32] = _col(np.asarray(inp["a_ln_g"][j], f))
        cst[:, cb + 32:cb + 48] = _col(np.asarray(inp["a_ln_b"][j], f))
        bq = np.asarray(inp["b_b_qkv"][j], f)
        cb = C_B + j * 12
        cst[:, cb:cb + 8] = _col(bq[:1024])
        bk = bq[1024:1280].reshape(4, 64)
        cst[:, cb + 8:cb + 12] = np.concatenate([bk, bk], axis=1).T
        sk = np.asarray(inp["b_sinks"][j], f).reshape(4, 4)[:, PERM].reshape(16)
        cst[:, C_SINK + j * 16:C_SINK + j * 16 + 16] = np.broadcast_to(sk, (128, 16))
    inv_freq = (np.float32(500000.0) ** (-np.arange(0, 16, 2, dtype=np.float32) / np.float32(16))).astype(f)
    p = np.arange(128) % 64
    cst[:, C_INVF] = np.where(p < 16, inv_freq[p % 8], 0.0)
    rows = np.zeros((1, NROWS), f)
    for j in range(2):
        rows[0, R_BV + j * 2048:R_BV + (j + 1) * 2048] = np.asarray(inp["a_b_in"][j][GW:], f)
        bv = np.asarray(inp["b_b_qkv"][j], f)[1280:1536].reshape(4, 1, 64)
        rows[0, R_BVD + j * 512:R_BVD + (j + 1) * 512] = np.broadcast_to(bv, (4, 2, 64)).reshape(512)
    kc = np.zeros((128, NKC), f)
    s = np.arange(128)[:, None]
    q = np.arange(128)[None, :]
    kc[:, KC_MCUR:KC_MCUR + 128] = (s <= q)
    kc[:, KC_MPREV:KC_MPREV + 128] = (s > q)
    rm = np.zeros((128, 128), f)
    for m in range(128):
        d = m % 64
        if d < 8:
            rm[m + 8, m] = -1.0
        elif d < 16:
            rm[m - 8, m] = 1.0
    kc[:, KC_RM:KC_RM + 128] = rm
    wsT = np.ascontiguousarray(np.asarray(inp["a_w_s"], f).transpose(0, 3, 1, 2)).reshape(2, 128, 1024)
    bs = np.ascontiguousarray(np.asarray(inp["a_b_s"], f)).reshape(2, 1, 1024)
    return cst, rows, kc, wsT, bs


_NC_CACHE = {}


def run(inp, stages=None, n_tiles=SEQ // TT, n_cores=8, trace=False):
    key = (tuple(stages) if stages else None, n_tiles)
    if key not in _NC_CACHE:
        _NC_CACHE[key] = build_nc(stages, n_tiles)
    nc = _NC_CACHE[key]
    S = n_tiles * TT
    cst, rows, kc, wsT, bs = host_consts(inp)
    x = np.asarray(inp["x"], np.float32)
    posn = np.asarray(inp["positions"], np.int32)
    shared = {
        "cst": cst, "rows": rows, "kc": kc, "wsT": wsT, "bs": bs,
        "a_w_in": np.asarray(inp["a_w_in"], np.float32), "a_w_out": np.asarray(inp["a_w_out"], np.float32),
        "b_w_qkv": np.asarray(inp["b_w_qkv"], np.float32), "b_w_o": np.asarray(inp["b_w_o"], np.float32),
        "ffn_w_gu": np.asarray(inp["ffn_w_gu"], np.float32), "ffn_w_down": np.asarray(inp["ffn_w_down"], np.float32),
    }
    in_maps = []
    for b in range(n_cores):
        m = dict(shared)
        m["xT"] = np.ascontiguousarray(x[b, :S, :].T)
        m["pos"] = np.ascontiguousarray(posn[b, :S].reshape(1, S))
        in_maps.append(m)
    res = run_bass_kernel_spmd(nc, in_maps, core_ids=list(range(n_cores)), trace=trace)
    out = np.stack([np.ascontiguousarray(r["oT"].T) for r in res.results], axis=0)
    return out, res


def kernel(**inputs):
    out, _ = run(inputs)
    return out.astype(np.float32)
```

```python
import numpy as np
from contextlib import ExitStack
import concourse.bass as bass
import concourse.mybir as mybir
from concourse.bass_utils import run_bass_kernel_spmd

F32, BF16, I32 = mybir.dt.float32, mybir.dt.bfloat16, mybir.dt.int32
AF = mybir.ActivationFunctionType
ALU = mybir.AluOpType

D = 1024
SEQ = 4096
TT = 512
NBLK = TT // 128
GW = 2048
FH = 2816
NFC = FH // 128
RMS_EPS = 1e-6
LN_EPS = 1e-5
PERM = [0, 2, 1, 3]
SLOT_ELEMS = 5632
NSLOT = 4
SAME_ENGINE_SYNC = True
DBG_B = 0
SIM_CHECK = False

def c_norm(l, w):
    return (l * 4 + w) * 8
C_A = 128
C_B = 224
C_INVF = 248
C_SINK = 249
NCST = 288
R_BV = 0
R_BVD = 4096
NROWS = 5120
KC_MCUR, KC_MPREV, KC_RM = 0, 128, 256
NKC = 384


class Tok:
    __slots__ = ("key", "val")

    def __init__(self, key, val):
        self.key, self.val = key, val


def _flat(deps, out):
    for d in deps:
        if d is None:
            continue
        if isinstance(d, (list, tuple)):
            _flat(d, out)
        else:
            out.append(d)
    return out


class Stream:
    def __init__(self, name):
        self.name, self.ops, self.epoch, self.count, self.waited = name, [], 0, 0, {}


class Prog:
    def __init__(self):
        self.st = {n: Stream(n) for n in ("pe", "act", "dve", "pool", "sp")}
        self.semkeys = {}
        self.dmacount = {}

    def new_epoch(self):
        for s in self.st.values():
            s.epoch += 1
            s.count = 0

    def _waits(self, st, deps, force=False):
        need = {}
        for d in _flat(deps, []):
            if d.key[0] == "E" and d.key[1] == st.name and not (SAME_ENGINE_SYNC or force):
                continue
            if need.get(d.key, 0) < d.val:
                need[d.key] = d.val
        w = []
        for key, val in need.items():
            if st.waited.get(key, 0) >= val:
                continue
            st.waited[key] = val
            w.append((key, val))
        return w

    def op(self, eng, fn, deps=(), signal=True, force=False):
        st = self.st[eng]
        w = self._waits(st, deps, force)
        key = tok = None
        if signal:
            st.count += 1
            key = ("E", eng, st.epoch)
            self.semkeys[key] = True
            tok = Tok(key, st.count)
        st.ops.append((w, fn, key, 1))
        return tok

    def dma(self, eng, fn, semname, deps=()):
        st = self.st[eng]
        w = self._waits(st, deps)
        key = ("D", semname)
        self.semkeys[key] = True
        self.dmacount[key] = self.dmacount.get(key, 0) + 16
        st.ops.append((w, fn, key, 16))
        return Tok(key, self.dmacount[key])


class Ring:
    def __init__(self, n):
        self.n, self.i, self.free = n, 0, [None] * n

    def get(self):
        idx = self.i
        self.i = (self.i + 1) % self.n
        return idx, self.free[idx]

    def rel(self, idx, tok):
        self.free[idx] = tok


def _simulate(P):
    sem = {}
    pc = {n: 0 for n in P.st}
    progress = True
    while progress:
        progress = False
        for n, st in P.st.items():
            while pc[n] < len(st.ops):
                w, fn, key, inc = st.ops[pc[n]]
                if any(sem.get(k, 0) < v for (k, v) in w):
                    break
                if key is not None:
                    sem[key] = sem.get(key, 0) + inc
                pc[n] += 1
                progress = True
    stuck = {n: pc[n] for n in P.st if pc[n] < len(P.st[n].ops)}
    for n, i in stuck.items():
        w = P.st[n].ops[i][0]
        print("SIM STUCK", n, "op", i, "of", len(P.st[n].ops), "waits", [(k, v, sem.get(k, 0)) for (k, v) in w if sem.get(k, 0) < v])
    if not stuck:
        print("SIM OK", {n: len(P.st[n].ops) for n in P.st})
    return not stuck


def default_stages():
    st = []
    for i in range(4):
        st.append(("A" if i % 2 == 0 else "B", i))
        st.append(("F", i))
    return st


def build_nc(stages=None, n_tiles=SEQ // TT):
    stages = default_stages() if stages is None else stages
    S = n_tiles * TT
    nc = bass.Bass("TRN2", target_bir_lowering=False)
    P = Prog()
    es = ExitStack()

    def dram(name, shape, dt, kind):
        return nc.dram_tensor(name, list(shape), dt, kind=kind).ap()

    xT = dram("xT", [D, S], F32, "ExternalInput")
    pos = dram("pos", [1, S], I32, "ExternalInput")
    cstd = dram("cst", [128, NCST], F32, "ExternalInput")
    rowsd = dram("rows", [1, NROWS], F32, "ExternalInput")
    kcd = dram("kc", [128, NKC], F32, "ExternalInput")
    wsTd = dram("wsT", [2, 128, 8 * 128], F32, "ExternalInput")
    bsd = dram("bs", [2, 1, 8 * 128], F32, "ExternalInput")
    w_in = dram("a_w_in", [2, D, 2 * GW], F32, "ExternalInput")
    w_out = dram("a_w_out", [2, GW, D], F32, "ExternalInput")
    w_qkv = dram("b_w_qkv", [2, D, 1536], F32, "ExternalInput")
    w_o = dram("b_w_o", [2, D, D], F32, "ExternalInput")
    w_gu = dram("ffn_w_gu", [4, D, 2 * FH], F32, "ExternalInput")
    w_dn = dram("ffn_w_down", [4, FH, D], F32, "ExternalInput")
    oT = dram("oT", [D, S], F32, "ExternalOutput")
    dbg = dram("dbg", [128, 4096], F32, "ExternalOutput") if DBG_B == 7 else None
    dbg_toks = []
    rows_b16 = dram("rows_b16", [1, NROWS], BF16, "Internal")
    wb_in = dram("wb_in", [2, D, 2 * GW], BF16, "Internal")
    wb_out = dram("wb_out", [2, GW, D], BF16, "Internal")
    wb_qkv = dram("wb_qkv", [2, D, 2048], BF16, "Internal")
    wb_o = dram("wb_o", [2, D, D], BF16, "Internal")
    wb_gu = dram("wb_gu", [4, D, 2 * FH], BF16, "Internal")
    wb_dn = dram("wb_dn", [4, FH, D], BF16, "Internal")

    def sb(name, shape, dt):
        return es.enter_context(nc.sbuf_tensor(name, list(shape), dt))

    hTs = [sb("hT0", [128, 8, TT], F32), sb("hT1", [128, 8, TT], F32)]
    hT_box = [hTs[0]]
    hn = sb("hn", [128, 8, TT], BF16)
    y = sb("y", [128, 8, TT], F32)
    big = sb("big", [128, 16384], BF16)
    wring_t = [sb(f"wslot{i}", [128, SLOT_ELEMS], BF16) for i in range(NSLOT)]
    cst = sb("cst_sb", [128, NCST], F32)
    rows_bf = sb("rows_bf", [1, 2048], BF16)
    kc_b = sb("kc_b", [128, NKC], BF16)
    ones_bf = sb("ones_bf", [128, 128], BF16)
    eps_t = sb("eps_t", [128, 2], F32)
    Cm = [sb(f"Cm{j}", [128, 16, 128], F32) for j in range(2)]
    wsTm = [sb(f"wsTm{j}", [128, 8, 128], BF16) for j in range(2)]
    cosF = sb("cosF", [128, TT], F32)
    sinF = sb("sinF", [128, TT], F32)
    sq_t = [sb(f"sq{i}", [128, TT], BF16) for i in range(3)]
    lnv = sb("lnv", [128, TT], F32)
    rinv_t = [sb(f"rinv{i}", [128, TT], F32) for i in range(2)]
    tmp_t = [sb(f"tmp{i}", [128, TT], F32) for i in range(2)]
    sg_t = [sb(f"sg{i}", [128, TT], BF16) for i in range(3)]
    tmpA = sb("tmpA", [128, 8 * TT], BF16)
    tmpB = sb("tmpB", [128, 4 * TT], F32)
    es2 = [sb(f"es2_{j}", [64, 16], F32) for j in range(2)]
    es2f = sb("es2f", [64, 16 * 128], BF16)
    esf = sb("esf", [128, 16], F32)
    esh = sb("esh", [128, 16], BF16)
    esl = sb("esl", [128, 16], F32)
    eshf = sb("eshf", [128, 16], F32)
    KTc = [sb(f"KTc{j}", [128, 4, 128], BF16) for j in range(2)]
    Vc = [sb(f"Vc{j}", [128, 512], BF16) for j in range(2)]
    lnst = sb("lnst", [128, NBLK, 4, 6], F32)
    lnmv = sb("lnmv", [128, NBLK, 2], F32)
    lnr = sb("lnr", [128, NBLK], F32)
    mhalf = sb("mhalf", [128, NBLK], F32)
    ps = es.enter_context(nc.psum_tensor("ps", [128, 8, 512], F32))
    pt_t = sb("pt_t", [128, 4 * 512], BF16)
    tmpA_f = tmpA[:].bitcast(F32)
    tmpA_i = tmpA[:].bitcast(I32)
    tmpB_i = tmpB[:].bitcast(I32)

    y_guard = [None]
    lnv_free = [None]
    main = Ring(6)
    aux = Ring(2)
    sqr, rinvr, tmpr, sgr, wring = Ring(3), Ring(2), Ring(2), Ring(3), Ring(NSLOT)

    def auxbank(i):
        return 6 + i

    def mm(out, lhsT, rhs, start, stop, deps=(), signal=False):
        return P.op("pe", lambda e: e.matmul(out, lhsT=lhsT, rhs=rhs, start=start, stop=stop), deps, signal)

    def act(out, in_, func, deps=(), bias=None, scale=None, signal=True):
        kw = {}
        if bias is not None:
            kw["bias"] = bias
        if scale is not None:
            kw["scale"] = scale
        return P.op("act", lambda e: e.activation(out=out, in_=in_, func=func, **kw), deps, signal)

    def tt(eng, out, in0, in1, op, deps=(), signal=True, force=False):
        return P.op(eng, lambda e: e.tensor_tensor(out=out, in0=in0, in1=in1, op=op), deps, signal, force)

    def ts(eng, out, in0, s1, s2, op0, op1=None, deps=(), signal=True, force=False):
        if op1 is None:
            return P.op(eng, lambda e: e.tensor_scalar(out=out, in0=in0, scalar1=s1, scalar2=None, op0=op0), deps, signal, force)
        return P.op(eng, lambda e: e.tensor_scalar(out=out, in0=in0, scalar1=s1, scalar2=s2, op0=op0, op1=op1), deps, signal, force)

    def stt(out, in0, scalar, in1, op0, op1, deps=(), signal=True):
        return P.op("dve", lambda e: e.scalar_tensor_tensor(out=out, in0=in0, scalar=scalar, in1=in1, op0=op0, op1=op1), deps, signal)

    def cp(eng, out, in_, deps=(), signal=True, force=False):
        return P.op(eng, lambda e: e.tensor_copy(out=out, in_=in_), deps, signal, force)

    def dma(eng, out, in_, sem, deps=()):
        return P.dma(eng, lambda e: e.dma_start(out=out, in_=in_), sem, deps)

    a_layers = sorted({l // 2 for (k, l) in stages if k == "A"})
    b_layers = sorted({l // 2 for (k, l) in stages if k == "B"})
    f_layers = sorted({l for (k, l) in stages if k == "F"})

    t_cst = dma("sp", cst[:], cstd, "c_cst")
    t_kc = dma("sp", tmpB[:, 0:NKC], kcd, "c_kc")
    t_kcb = cp("dve", kc_b[:], tmpB[:, 0:NKC], [t_kc])
    rows_free = [None]
    t_rowsc = dma("pool", rows_b16, rowsd, "c_rowsc")
    t_ones = P.op("dve", lambda e: e.memset(ones_bf[:], 1.0))
    t_eps = P.op("dve", lambda e: e.memset(eps_t[:, 0:1], RMS_EPS))
    t_eps = P.op("dve", lambda e: e.memset(eps_t[:, 1:2], LN_EPS))
    t_mh = P.op("pool", lambda e: e.memset(mhalf[:], -0.5))
    mcur = kc_b[:, KC_MCUR:KC_MCUR + 128]
    mprev = kc_b[:, KC_MPREV:KC_MPREV + 128]
    Rm = kc_b[:, KC_RM:KC_RM + 128]

    conv = {}

    conv_hist = []

    def cdma(dst, src, sem):
        thr = conv_hist[-2] if len(conv_hist) >= 2 else None
        return dma("pool", dst, src, sem, [thr])

    def convert(name, dst, src, rows_per, sem):
        n = src.shape[0]
        t = None
        for r0 in range(0, n, rows_per):
            r1 = min(n, r0 + rows_per)
            t = cdma(dst[r0:r1, :], src[r0:r1, :], sem)
        conv[name] = t
        conv_hist.append(t)

    def convert_stage(kind, l):
        j = l // 2
        if kind == "A" and ("in", j) not in conv:
            convert(("in", j), wb_in[j], w_in[j], 128, f"cv_in{j}")
            convert(("out", j), wb_out[j], w_out[j], 512, f"cv_out{j}")
        if kind == "B" and ("qkv", j) not in conv:
            t = None
            for r0 in range(0, D, 256):
                t = cdma(wb_qkv[j][r0:r0 + 256, 0:1024], w_qkv[j][r0:r0 + 256, 0:1024], f"cv_qkv{j}")
                for part, c0 in ((0, 1024), (1, 1280)):
                    src = w_qkv[j][r0:r0 + 256, c0:c0 + 256].rearrange("r (h d) -> r h d", d=64)
                    for dup in range(2):
                        dst = wb_qkv[j][r0:r0 + 256, 1024 + part * 512:1024 + part * 512 + 512].rearrange(
                            "r (h u d) -> r h u d", u=2, d=64)[:, :, dup, :]
                        t = cdma(dst, src, f"cv_qkv{j}")
            conv[("qkv", j)] = t
            conv_hist.append(t)
            convert(("o", j), wb_o[j], w_o[j], 512, f"cv_o{j}")
        if kind == "F" and ("gu", l) not in conv:
            convert(("gu", l), wb_gu[l], w_gu[l], 128, f"cv_gu{l}")
            convert(("dn", l), wb_dn[l], w_dn[l], 256, f"cv_dn{l}")

    convert_stage(*stages[0])

    a_done = {}

    def a_prologue(j, guard):
        if j in a_done:
            return
        wst_f = tmpB[:, 0:1024].rearrange("p (g t) -> p g t", g=8)
        bsb_f = tmpB[:, 1024:2048].rearrange("p (g t) -> p g t", g=8)
        t_w = dma("sp", tmpB[:, 0:1024], wsTd[j], "c_ws", [guard, t_kcb])
        t_b = dma("sp", tmpB[:, 1024:2048], bsd[j].partition_broadcast(128), "c_bs", [guard, t_kcb])
        t_m = tt("dve", wsTm[j][:], wst_f, mcur.unsqueeze(1).to_broadcast([128, 8, 128]), ALU.mult, [t_w, t_kcb, guard])
        t_c = None
        for half in range(2):
            bi, bfree = aux.get()
            b = auxbank(bi)
            for gg in range(4):
                g = half * 4 + gg
                t_r = mm(ps[:, b, gg * 128:(gg + 1) * 128], ones_bf[:], wsTm[j][:, g, :], True, True,
                         [t_m, t_ones, bfree], signal=True)
            for cc in range(8):
                cv = half * 8 + cc
                g = cv // 2
                gg = g - half * 4
                t_c = stt(Cm[j][:, cv, :], ps[:, b, gg * 128:(gg + 1) * 128],
                          cst[:, C_A + j * 48 + 32 + cv:C_A + j * 48 + 33 + cv], bsb_f[:, g, :],
                          ALU.mult, ALU.add, [t_r, t_b, t_cst])
            aux.rel(bi, t_c)
        a_done[j] = t_c

    if stages[0][0] == "A":
        a_prologue(stages[0][1] // 2, None)
    t_prolog_tmpB = list(a_done.values())

    for j in b_layers:
        t_e = act(esf[:], cst[:, C_SINK + j * 16:C_SINK + j * 16 + 16], AF.Exp, [t_cst, conv.get(("es", j - 1))])
        t_h = cp("dve", esh[:], esf[:], [t_e, conv.get(("es", j - 1))], force=True)
        t_hf = cp("dve", eshf[:], esh[:], [t_h], force=True)
        t_l = tt("dve", esl[:], esf[:], eshf[:], ALU.subtract, [t_hf], force=True)
        t_z = P.op("dve", lambda e, j=j: e.memset(es2[j][:], 0.0), [t_l], force=True)
        t_1 = cp("dve", es2[j][0:1, :], eshf[0:1, :], [t_z], force=True)
        t_2 = cp("dve", es2[j][32:33, :], esl[32:33, :], [t_1], force=True)
        conv[("es", j)] = t_2

    def rms_rinv(src, src_toks):
        bi, bfree = aux.get()
        b = auxbank(bi)
        t_m = None
        for k in range(8):
            si, sfree = sqr.get()
            t_s = act(sq_t[si][:], src[:, k, :], AF.Square, [src_toks[k], sfree])
            t_m = mm(ps[:, b, :], ones_bf[:], sq_t[si][:], k == 0, k == 7, [t_s, t_ones, bfree if k == 0 else None], signal=True)
            sqr.rel(si, t_m)
        ri, rfree = rinvr.get()
        t_l = act(lnv[:], ps[:, b, :], AF.Ln, [t_m, t_eps, lnv_free[0]], bias=eps_t[:, 0:1], scale=1.0 / D)
        t_r = act(rinv_t[ri][:], lnv[:], AF.Exp, [rfree, t_l], scale=-0.5)
        aux.rel(bi, t_l)
        lnv_free[0] = t_r
        return ri, t_r

    def rms_pre(gcol, h_toks):
        hT = hT_box[0]
        ri, t_r = rms_rinv(hT, h_toks)
        toks = []
        for k in range(8):
            toks.append(stt(hn[:, k, :], hT[:, k, :], cst[:, gcol + k:gcol + k + 1], rinv_t[ri][:],
                            ALU.mult, ALU.mult, [t_r, h_toks[k], t_cst]))
        rinvr.rel(ri, toks[-1])
        return toks

    def rms_post(gcol, y_toks, h_toks, out_buf=None):
        hT = hT_box[0]
        out_buf = hT if out_buf is None else out_buf
        ri, t_r = rms_rinv(y, y_toks)
        toks = []
        t_s = None
        for k in range(8):
            ti, tfree = tmpr.get()
            t_s = stt(tmp_t[ti][:], y[:, k, :], cst[:, gcol + k:gcol + k + 1], rinv_t[ri][:],
                      ALU.mult, ALU.mult, [t_r, tfree, t_cst])
            if k % 2:
                t_a = tt("pool", out_buf[:, k, :], hT[:, k, :], tmp_t[ti][:], ALU.add, [t_s, h_toks[k]])
            else:
                t_a = stt(out_buf[:, k, :], tmp_t[ti][:], 1.0, hT[:, k, :], ALU.mult, ALU.add, [t_s, h_toks[k]])
            tmpr.rel(ti, t_a)
            toks.append(t_a)
        rinvr.rel(ri, t_s)
        return toks

    def load_panel(view_fn, srcs, cdeps):
        si, sfree = wring.get()
        toks = []
        for (sel, src) in srcs:
            toks.append(dma("sp", sel(wring_t[si]), src, f"w{si}", [sfree, cdeps]))
        return si, toks

    def ffn(l, h_toks, last=False, after_pre=None, next_a=None):
        hid = big[:, 0:NFC * TT].rearrange("p (c t) -> p c t", c=NFC)
        hn_toks = rms_pre(c_norm(l, 2), h_toks)
        if after_pre is not None:
            after_pre(h_toks)
        elif next_a is not None:
            a_prologue(next_a, h_toks)
        hid_toks = []
        for cp2 in range(NFC // 2):
            c0 = cp2 * 256
            si, dt = load_panel(None, [
                (lambda s: s[:, 0:4096].rearrange("p (k g c) -> p k g c", k=8, g=2)[:, :, 0, :],
                 wb_gu[l][:, c0:c0 + 256].rearrange("(k p) c -> p k c", p=128)),
                (lambda s: s[:, 0:4096].rearrange("p (k g c) -> p k g c", k=8, g=2)[:, :, 1, :],
                 wb_gu[l][:, FH + c0:FH + c0 + 256].rearrange("(k p) c -> p k c", p=128)),
            ], conv[("gu", l)])
            sv = wring_t[si][:, 0:4096].rearrange("p (k g c) -> p k g c", k=8, g=2)
            t_u = None
            for cc in range(2):
                gb, gfree = main.get()
                ub, ufree = main.get()
                for k in range(8):
                    t_g = mm(ps[:, gb, :], sv[:, k, 0, cc * 128:(cc + 1) * 128], hn[:, k, :], k == 0, k == 7,
                             [dt, hn_toks[k], gfree if k == 0 else None], signal=(k == 7))
                for k in range(8):
                    t_u = mm(ps[:, ub, :], sv[:, k, 1, cc * 128:(cc + 1) * 128], hn[:, k, :], k == 0, k == 7,
                             [ufree if k == 0 else None], signal=(k == 7))
                gi, gf = sgr.get()
                t_s = act(sg_t[gi][:], ps[:, gb, :], AF.Silu, [t_g, gf])
                t_m = tt("dve", hid[:, cp2 * 2 + cc, :], sg_t[gi][:], ps[:, ub, :], ALU.mult, [t_s, t_u])
                main.rel(gb, t_s)
                main.rel(ub, t_m)
                sgr.rel(gi, t_m)
                hid_toks.append(t_m)
            wring.rel(si, t_u)
        y_toks = [None] * 8
        for dp in range(4):
            si, dt = load_panel(None, [
                (lambda s: s[:, 0:NFC * 256].rearrange("p (k c) -> p k c", k=NFC),
                 wb_dn[l][:, dp * 256:(dp + 1) * 256].rearrange("(k p) c -> p k c", p=128)),
            ], conv[("dn", l)])
            sv = wring_t[si][:, 0:NFC * 256].rearrange("p (k c) -> p k c", k=NFC)
            t_m = None
            for cc in range(2):
                d = dp * 2 + cc
                b, bfree = main.get()
                for k in range(NFC):
                    t_m = mm(ps[:, b, :], sv[:, k, cc * 128:(cc + 1) * 128], hid[:, k, :], k == 0, k == NFC - 1,
                             [dt, hid_toks[k], bfree if k == 0 else None], signal=(k == NFC - 1))
                t_y = act(y[:, d, :], ps[:, b, :], AF.Copy, [t_m, y_guard[0]])
                main.rel(b, t_y)
                y_toks[d] = t_y
            wring.rel(si, t_m)
        return rms_post(c_norm(l, 3), y_toks, h_toks, y if last else None)

    def mix_a(l, h_toks, last=False, after_pre=None):
        j = l // 2
        a_prologue(j, h_toks)
        uT = big[:, 0:8192].rearrange("p (c t) -> p c t", c=16)
        vt = big[:, 8192:16384].rearrange("p (b f) -> p b f", b=NBLK)
        svt = [tmpB[:, 0:512], tmpB[:, 512:1024]]
        svr = Ring(2)
        t_rows = dma("sp", rows_bf[0:1, 0:2048], rows_b16[0:1, R_BV + j * 2048:R_BV + (j + 1) * 2048], "c_rows", [rows_free[0], t_rowsc])
        hn_toks = rms_pre(c_norm(l, 0), h_toks)
        if after_pre is not None:
            after_pre(h_toks)
        cb = C_A + j * 48
        st_toks = [[None] * 4 for _ in range(NBLK)]
        for vp in range(4):
            si, dt = load_panel(None, [
                (lambda s: s[:, 0:4096].rearrange("p (k c) -> p k c", k=8),
                 wb_in[j][:, GW + vp * 512:GW + (vp + 1) * 512].rearrange("(k p) c -> p k c", p=128))], conv[("in", j)])
            sv = wring_t[si][:, 0:4096].rearrange("p (k c) -> p k c", k=8)
            t_m = None
            for blk in range(NBLK):
                b, bfree = main.get()
                for k in range(8):
                    mm(ps[:, b, :], hn[:, k, blk * 128:(blk + 1) * 128], sv[:, k, :], k == 0, False,
                       [dt, hn_toks[k], bfree if k == 0 else None])
                t_m = mm(ps[:, b, :], ones_bf[0:1, :], rows_bf[0:1, vp * 512:(vp + 1) * 512],
                         False, True, [t_rows, t_ones], signal=True)
                rows_free[0] = t_m
                t_v = act(vt[:, blk, vp * 512:(vp + 1) * 512], ps[:, b, :], AF.Gelu, [t_m])
                main.rel(b, t_v)
                st_toks[blk][vp] = P.op("dve", lambda e, blk=blk, vp=vp: e.bn_stats(out=lnst[:, blk, vp, :], in_=vt[:, blk, vp * 512:(vp + 1) * 512]), [t_v])
            wring.rel(si, t_m)
        u_toks = []
        for up in range(4):
            si, dt = load_panel(None, [
                (lambda s: s[:, 0:4096].rearrange("p (k c) -> p k c", k=8),
                 wb_in[j][:, up * 512:(up + 1) * 512].rearrange("(k p) c -> p k c", p=128))], conv[("in", j)])
            sv = wring_t[si][:, 0:4096].rearrange("p (k c) -> p k c", k=8)
            t_m = None
            for cc in range(4):
                cu = up * 4 + cc
                b, bfree = main.get()
                for k in range(8):
                    t_m = mm(ps[:, b, :], sv[:, k, cc * 128:(cc + 1) * 128], hn[:, k, :], k == 0, k == 7,
                             [dt, hn_toks[k], bfree if k == 0 else None], signal=(k == 7))
                t_u = act(uT[:, cu, :], ps[:, b, :], AF.Gelu, [t_m, t_cst], bias=cst[:, cb + cu:cb + cu + 1])
                main.rel(b, t_u)
                u_toks.append(t_u)
            wring.rel(si, t_m)
        vn_toks = []
        for blk in range(NBLK):
            t_a = P.op("dve", lambda e, blk=blk: e.bn_aggr(out=lnmv[:, blk, :], in_=lnst[:, blk, :, :].rearrange("p a s -> p (a s)")), [st_toks[blk]], True, True)
            t_e = ts("pool", lnr[:, blk:blk + 1], lnmv[:, blk, 1:2], LN_EPS, None, ALU.add, None, [t_a])
            t_p = tt("pool", lnr[:, blk:blk + 1], lnr[:, blk:blk + 1], mhalf[:, blk:blk + 1], ALU.pow, [t_e, t_mh], force=True)
            t_n = ts("dve", vt[:, blk, :], vt[:, blk, :], lnmv[:, blk, 0:1], lnr[:, blk:blk + 1], ALU.subtract, ALU.mult, [t_p, t_a], force=True)
            vn_toks.append(t_n)
        g_toks = []
        for cv in range(16):
            g = cv // 2
            b, bfree = main.get()
            t_m = None
            for blk in range(NBLK):
                t_m = mm(ps[:, b, blk * 128:(blk + 1) * 128], vt[:, blk, cv * 128:(cv + 1) * 128], wsTm[j][:, g, :], True, True,
                         [vn_toks[blk], bfree if blk == 0 else None, a_done[j]], signal=(blk == NBLK - 1))
            si2, sf2 = svr.get()
            t_s = stt(svt[si2].rearrange("p (b t) -> p b t", b=NBLK), ps[:, b, :].rearrange("p (b t) -> p b t", b=NBLK),
                      cst[:, cb + 16 + cv:cb + 17 + cv], Cm[j][:, cv, :].unsqueeze(1).to_broadcast([128, NBLK, 128]),
                      ALU.mult, ALU.add, [t_m, sf2])
            main.rel(b, t_s)
            t_g = tt("pool" if cv % 2 else "dve", uT[:, cv, :], uT[:, cv, :], svt[si2], ALU.mult, [t_s, u_toks[cv]])
            svr.rel(si2, t_g)
            g_toks.append(t_g)
        y_toks = [None] * 8
        for op_ in range(4):
            si, dt = load_panel(None, [
                (lambda s: s[:, 0:4096].rearrange("p (k c) -> p k c", k=16),
                 wb_out[j][:, op_ * 256:(op_ + 1) * 256].rearrange("(k p) c -> p k c", p=128))], conv[("out", j)])
            sv = wring_t[si][:, 0:4096].rearrange("p (k c) -> p k c", k=16)
            t_m = None
            for cc in range(2):
                d = op_ * 2 + cc
                b, bfree = main.get()
                for k in range(16):
                    t_m = mm(ps[:, b, :], sv[:, k, cc * 128:(cc + 1) * 128], uT[:, k, :], k == 0, k == 15,
                             [dt, g_toks[k], bfree if k == 0 else None], signal=(k == 15))
                t_y = act(y[:, d, :], ps[:, b, :], AF.Copy, [t_m, y_guard[0]])
                main.rel(b, t_y)
                y_toks[d] = t_y
            wring.rel(si, t_m)
        return rms_post(c_norm(l, 1), y_toks, h_toks, y if last else None)

    def mix_b(l, h_toks, ti, rope_tok, last=False):
        j = l // 2
        QT = big[:, 0:4096].rearrange("p (c t) -> p c t", c=8)
        aT = big[:, 4096:8192].rearrange("p (c t) -> p c t", c=8)
        KTe = big[:, 8192:8192 + 2560].rearrange("p (h s t) -> p h s t", h=4, s=5)
        KTo = big[:, 10752:10752 + 2560].rearrange("p (h s t) -> p h s t", h=4, s=5)
        Vv = big[:, 13312:13312 + 2560].rearrange("p (s f) -> p s f", s=5)
        pt = [pt_t[:, i * 512:(i + 1) * 512] for i in range(4)]
        ptr = Ring(4)
        qf = [tmpB[:, 0:512], tmpB[:, 512:1024]]
        t1 = tmpB[:, 1024:1536]
        lden = tmpB[:, 1536:2048]
        qb = [tmpA[:, 0:512], tmpA[:, 512:1024]]
        rden = [tmpA_f[:, 512:1024], tmpA_f[:, 1024:1536]]
        qr, rdr = Ring(2), Ring(2)
        first = (ti == 0)
        t_rows = dma("sp", rows_bf[0:1, 0:512], rows_b16[0:1, R_BVD + j * 512:R_BVD + (j + 1) * 512], "c_rows", [rows_free[0], t_rowsc])
        hn_toks = rms_pre(c_norm(l, 0), h_toks)
        cb = C_B + j * 12
        t_ck = t_cv = None
        t_esf = None
        for hq in range(16):
            t_esf = ts("dve", es2f[:, hq * 128:(hq + 1) * 128], ones_bf[0:64, :], es2[j][:, hq:hq + 1], None, ALU.mult, None,
                       [conv[("es", j)], h_toks, t_ones])
        P.op("pool", lambda e: e.memset(KTe[64:128, :, :, :], 0.0), [h_toks])
        t_kz = P.op("pool", lambda e: e.memset(KTo[0:64, :, :, :], 0.0))
        if not first:
            cp("pool", KTe[0:64, :, 0, :], KTc[j][0:64, :, :])
            t_ck = cp("pool", KTo[64:128, :, 0, :], KTc[j][64:128, :, :])
            t_cv = cp("pool", Vv[:, 0, :], Vc[j][:])

        def rope_p1(b, t_m, bias_ap):
            qi, qfree = qr.get()
            t_f = act(qf[qi], ps[:, b, :], AF.Identity, [t_m, qfree, t_cst], bias=bias_ap)
            t_b = act(qb[qi], ps[:, b, :], AF.Identity, [], bias=bias_ap)
            main.rel(b, t_b)
            return (qi, t_f, t_b)

        def rope_p2(state, outs):
            qi, t_f, t_b = state
            b2, b2free = main.get()
            t_r = mm(ps[:, b2, :], Rm, qb[qi], True, True, [t_b, t_kcb, b2free], signal=True)
            t_a = tt("dve", t1, qf[qi], cosF[:], ALU.mult, [t_f, rope_tok, t1_free[0]])
            t_s = tt("dve", qf[qi], ps[:, b2, :], sinF[:], ALU.mult, [t_r, t_a])
            main.rel(b2, t_s)
            t_o = None
            for (lo, hi, out_ap) in outs:
                t_o = tt("dve", out_ap, t1[lo:hi, :], qf[qi][lo:hi, :], ALU.add, [t_a, t_s])
            t1_free[0] = t_o
            qr.rel(qi, t_o)
            return t_o

        pending = []
        t1_free = [None]

        def flush_rope():
            while pending:
                kind_, idx_, state_, outs_ = pending.pop(0)
                t_o = rope_p2(state_, outs_)
                if kind_ == "q":
                    q_toks[idx_] = t_o
                else:
                    k_toks[idx_] = t_o

        q_toks = [None] * 8
        k_toks = [None] * 4
        v_toks = [None] * NBLK
        for pn in range(4):
            si, dt = load_panel(None, [
                (lambda s: s[:, 0:4096].rearrange("p (k c) -> p k c", k=8),
                 wb_qkv[j][:, pn * 512:(pn + 1) * 512].rearrange("(k p) c -> p k c", p=128))], conv[("qkv", j)])
            sv = wring_t[si][:, 0:4096].rearrange("p (k c) -> p k c", k=8)
            t_m = None
            if pn < 3:
                for cc in range(4):
                    b, bfree = main.get()
                    for k in range(8):
                        t_m = mm(ps[:, b, :], sv[:, k, cc * 128:(cc + 1) * 128], hn[:, k, :], k == 0, k == 7,
                                 [dt, hn_toks[k], bfree if k == 0 else None], signal=(k == 7))
                    if pn < 2:
                        cq = pn * 4 + cc
                        st_ = rope_p1(b, t_m, cst[:, cb + cq:cb + cq + 1])
                        flush_rope()
                        pending.append(("q", cq, st_, [(0, 128, QT[:, cq, :])]))
                    else:
                        st_ = rope_p1(b, t_m, cst[:, cb + 8 + cc:cb + 9 + cc])
                        flush_rope()
                        pending.append(("k", cc, st_, [(0, 64, KTe[0:64, cc, 1:5, :].rearrange("p s t -> p (s t)")),
                                                       (64, 128, KTo[64:128, cc, 1:5, :].rearrange("p s t -> p (s t)"))]))
            else:
                for blk in range(NBLK):
                    b, bfree = main.get()
                    for k in range(8):
                        mm(ps[:, b, :], hn[:, k, blk * 128:(blk + 1) * 128], sv[:, k, :], k == 0, False,
                           [dt, hn_toks[k], bfree if k == 0 else None])
                    t_m = mm(ps[:, b, :], ones_bf[0:1, :], rows_bf[0:1, 0:512],
                             False, True, [t_rows, t_ones], signal=True)
                    rows_free[0] = t_m
                    t_v = act(Vv[:, 1 + blk, :], ps[:, b, :], AF.Copy, [t_m, t_cv])
                    main.rel(b, t_v)
                    v_toks[blk] = t_v
                    flush_rope()
            wring.rel(si, t_m)
        a_toks = [None] * 8
        last_pe = [None]
        lden_free = [None]

        def s_phase(blk, kv):
            has_prev = not (first and blk == 0)
            pts = []
            for kb in (([0] if has_prev else []) + [1]):
                slot = blk + kb
                sb_, sfree = main.get()
                mm(ps[:, sb_, 0:256], KTe[:, kv, slot, :],
                   QT[:, 2 * kv:2 * kv + 2, blk * 128:(blk + 1) * 128], True, True,
                   [k_toks[kv], t_ck, t_kz, q_toks[2 * kv], q_toks[2 * kv + 1], sfree])
                t_m = mm(ps[:, sb_, 256:512], KTo[:, kv, slot, :],
                         QT[:, 2 * kv:2 * kv + 2, blk * 128:(blk + 1) * 128], True, True, [], signal=True)
                pi, pfree = ptr.get()
                t_e = act(pt[pi], ps[:, sb_, :], AF.Exp, [t_m, pfree], scale=0.125)
                main.rel(sb_, t_e)
                msk = mprev if kb == 0 else mcur
                t_k = tt("dve", pt[pi].rearrange("p (a q) -> p a q", a=4), pt[pi].rearrange("p (a q) -> p a q", a=4),
                         msk.unsqueeze(1).to_broadcast([128, 4, 128]), ALU.mult, [t_e, t_kcb])
                pts.append((pi, slot, t_k))
            return pts

        def pv_phase(blk, kv, pts):
            ob, ofree = main.get()
            db, dfree = main.get()
            for n_, (pi, slot, t_k) in enumerate(pts):
                mm(ps[:, ob, :], Vv[:, slot, kv * 128:(kv + 1) * 128], pt[pi], n_ == 0, n_ == len(pts) - 1,
                   [t_k, v_toks[blk], t_cv, ofree if n_ == 0 else None])
            for n_, (pi, slot, t_k) in enumerate(pts):
                t_pv = mm(ps[:, db, :], ones_bf[:], pt[pi], n_ == 0, False, [dfree if n_ == 0 else None, t_ones], signal=True)
                ptr.rel(pi, t_pv)
            t_d = mm(ps[:, db, :], ones_bf[0:64, :], es2f[:, kv * 512:(kv + 1) * 512], False, True, [t_esf], signal=True)
            last_pe[0] = t_d
            t_l = act(lden, ps[:, db, :], AF.Ln, [t_d, lden_free[0]])
            main.rel(db, t_l)
            ri, rfree = rdr.get()
            t_r = act(rden[ri], lden, AF.Exp, [rfree, t_l], scale=-1.0)
            lden_free[0] = t_r
            tt("dve", aT[0:64, 2 * kv:2 * kv + 2, blk * 128:(blk + 1) * 128],
               ps[0:64, ob, 0:256].rearrange("p (a q) -> p a q", a=2),
               rden[ri][0:64, 0:256].rearrange("p (a q) -> p a q", a=2), ALU.mult, [t_r, t_d])
            t_n1 = tt("dve", aT[64:128, 2 * kv:2 * kv + 2, blk * 128:(blk + 1) * 128],
                      ps[64:128, ob, 256:512].rearrange("p (a q) -> p a q", a=2),
                      rden[ri][64:128, 256:512].rearrange("p (a q) -> p a q", a=2), ALU.mult, [t_r, t_d])
            main.rel(ob, t_n1)
            rdr.rel(ri, t_n1)
            a_toks[2 * kv] = t_n1
            a_toks[2 * kv + 1] = t_n1

        items = [(blk, kv) for blk in range(NBLK) for kv in range(4)]
        nxt = s_phase(*items[0])
        for ii, (blk, kv) in enumerate(items):
            cur = nxt
            if ii + 1 < len(items):
                nxt = s_phase(*items[ii + 1])
            pv_phase(blk, kv, cur)
        last_pe = last_pe[0]
        cp("pool", KTc[j][0:64, :, :], KTe[0:64, :, 4, :], [k_toks, last_pe])
        t_ko = cp("pool", KTc[j][64:128, :, :], KTo[64:128, :, 4, :])
        t_vo = cp("pool", Vc[j][:], Vv[:, 4, :], [v_toks[NBLK - 1], last_pe])
        y_toks = [None] * 8
        for op_ in range(2):
            si, dt = load_panel(None, [
                (lambda s: s[:, 0:4096].rearrange("p (k c) -> p k c", k=8),
                 wb_o[j][:, op_ * 512:(op_ + 1) * 512].rearrange("(k p) c -> p k c", p=128))], conv[("o", j)])
            sv = wring_t[si][:, 0:4096].rearrange("p (k c) -> p k c", k=8)
            t_m = None
            for cc in range(4):
                d = op_ * 4 + cc
                b, bfree = main.get()
                for k in range(8):
                    t_m = mm(ps[:, b, :], sv[:, k, cc * 128:(cc + 1) * 128], aT[:, k, :], k == 0, k == 7,
                             [dt, a_toks[k], bfree if k == 0 else None], signal=(k == 7))
                t_y = act(y[:, d, :], ps[:, b, :], AF.Copy, [t_m, y_guard[0]])
                main.rel(b, t_y)
                y_toks[d] = t_y
            wring.rel(si, t_m)
        return rms_post(c_norm(l, 1), y_toks, h_toks, y if last else None), [t_ko, t_vo]

    def rope_tables(t0, guard):
        posi = tmpB_i[:, 0:512]
        r = tmpB[:, 512:1024]
        nf = tmpB[:, 1024:1536]
        m = tmpB[:, 1536:2048]
        ni = tmpA_i[:, 0:512]
        w = tmpA_f[:, 1024:1536]
        fl = tmpA_f[:, 1536:2048]
        t_p = dma("pool", posi, pos[:, t0:t0 + TT].partition_broadcast(128), "c_pos", [guard])
        c = cp("dve", r, posi, [t_p, guard])
        c = ts("dve", r, r, cst[:, C_INVF:C_INVF + 1], 1.0 / (2 * np.pi), ALU.mult, ALU.mult, [t_cst, c])
        last = None
        for (dst, shift) in ((sinF, 0.0), (cosF, 0.25)):
            if shift:
                c = ts("dve", nf, r, shift, None, ALU.add, None, [c])
                c = cp("dve", ni, nf, [c])
                c = cp("dve", fl, ni, [c])
                c = tt("dve", nf, nf, fl, ALU.subtract, [c])
                fr = nf
            else:
                c = cp("dve", ni, r, [c])
                c = cp("dve", m, ni, [c])
                c = tt("dve", m, r, m, ALU.subtract, [c])
                fr = m
            c = P.op("dve", lambda e, fr=fr: e.tensor_single_scalar(out=w, in_=fr, scalar=0.5, op=ALU.is_gt), [c])
            c = tt("dve", fr, fr, w, ALU.subtract, [c])
            c = P.op("dve", lambda e, fr=fr: e.tensor_single_scalar(out=w, in_=fr, scalar=-0.5, op=ALU.is_lt), [c])
            c = tt("dve", fr, fr, w, ALU.add, [c])
            last = act(dst[:], fr, AF.Sin, [c], scale=2 * np.pi * (1 - 2e-6))
        return last

    prev_store = None
    x_free = [None, None]
    stage_guard = [t_prolog_tmpB] + [conv.get(("es", j)) for j in b_layers]
    carry = []
    fb = next((i for i, (k_, _) in enumerate(stages) if k_ == "B"), None)
    rope_box = [None]

    def x_load(ti_):
        bi_ = ti_ % 2
        return dma("sp", hTs[bi_][:], xT[:, ti_ * TT:(ti_ + 1) * TT].rearrange("(k p) t -> p k t", p=128),
                   "c_x%d" % bi_, [x_free[bi_]])

    t_x_next = x_load(0)
    for ti in range(n_tiles):
        if ti > 0:
            P.new_epoch()
        t0 = ti * TT
        hT_box[0] = hTs[ti % 2]
        t_x = t_x_next
        h_toks = [t_x] * 8
        y_guard[0] = prev_store
        rope_box[0] = None

        def mk_rope(h_toks_, t0=t0):
            rope_box[0] = rope_tables(t0, [stage_guard, h_toks_, carry])

        for si_, (kind, l) in enumerate(stages):
            last = (si_ == len(stages) - 1)
            if ti == 0 and si_ + 1 < len(stages):
                convert_stage(*stages[si_ + 1])
            cb_ = mk_rope if (fb is not None and si_ == fb - 1) else None
            if kind == "A":
                h_toks = mix_a(l, h_toks, last, cb_)
            elif kind == "B":
                if rope_box[0] is None:
                    mk_rope(h_toks)
                h_toks, carry = mix_b(l, h_toks, ti, rope_box[0], last)
            else:
                nxa = None
                if ti == 0 and si_ + 1 < len(stages) and stages[si_ + 1][0] == "A":
                    nxa = stages[si_ + 1][1] // 2
                h_toks = ffn(l, h_toks, last, cb_, nxa)
            if rope_box[0] is not None:
                stage_guard = []
            if si_ == 0 and ti + 1 < n_tiles:
                t_x_next = x_load(ti + 1)
        x_free[ti % 2] = h_toks
        prev_store = dma("pool", oT[:, t0:t0 + TT].rearrange("(k p) t -> p k t", p=128), y[:], "c_o", [h_toks])
    P.op("sp", lambda e: e.nop(), [prev_store, dbg_toks], False)

    if SIM_CHECK:
        _simulate(P)
    sems = {}
    for n_, key in enumerate(P.semkeys):
        sems[key] = es.enter_context(nc.semaphore("s%d" % n_))
    block = es.enter_context(nc.Block())

    def make(name):
        def body(e):
            for (w, fn, key, inc) in P.st[name].ops:
                for (k, v) in w:
                    e.wait_ge(sems[k], v)
                ins = fn(e)
                if key is not None:
                    ins.then_inc(sems[key], inc)
        return body

    block.tensor(make("pe"))
    block.scalar(make("act"))
    block.vector(make("dve"))
    block.gpsimd(make("pool"))
    block.sync(make("sp"))
    es.close()
    return nc


def _col(v):
    return np.ascontiguousarray(v.reshape(-1, 128).T)


def host_consts(inp):
    f = np.float32
    cst = np.zeros((128, NCST), f)
    for l in range(4):
        for w, nm in enumerate(("pre_mix_g", "post_mix_g", "pre_ffn_g", "post_ffn_g")):
            cst[:, c_norm(l, w):c_norm(l, w) + 8] = _col(np.asarray(inp[nm][l], f))
    for j in range(2):
        cb = C_A + j * 48
        cst[:, cb:cb + 16] = _col(np.asarray(inp["a_b_in"][j][:GW], f))
        cst[:, cb + 16:cb + 32] = _col(np.asarray(inp["a_ln_g"][j], f))
        cst[:, cb + 32:cb + 48] = _col(np.asarray(inp["a_ln_b"][j], f))
        bq = np.asarray(inp["b_b_qkv"][j], f)
        cb = C_B + j * 12
        cst[:, cb:cb + 8] = _col(bq[:1024])
        bk = bq[1024:1280].reshape(4, 64)
        cst[:, cb + 8:cb + 12] = np.concatenate([bk, bk], axis=1).T
        sk = np.asarray(inp["b_sinks"][j], f).reshape(4, 4)[:, PERM].reshape(16)
        cst[:, C_SINK + j * 16:C_SINK + j * 16 + 16] = np.broadcast_to(sk, (128, 16))
    inv_freq = (np.float32(500000.0) ** (-np.arange(0, 16, 2, dtype=np.float32) / np.float32(16))).astype(f)
    p = np.arange(128) % 64
    cst[:, C_INVF] = np.where(p < 16, inv_freq[p % 8], 0.0)
    rows = np.zeros((1, NROWS), f)
    for j in range(2):
        rows[0, R_BV + j * 2048:R_BV + (j + 1) * 2048] = np.asarray(inp["a_b_in"][j][GW:], f)
        bv = np.asarray(inp["b_b_qkv"][j], f)[1280:1536].reshape(4, 1, 64)
        rows[0, R_BVD + j * 512:R_BVD + (j + 1) * 512] = np.broadcast_to(bv, (4, 2, 64)).reshape(512)
    kc = np.zeros((128, NKC), f)
    s = np.arange(128)[:, None]
    q = np.arange(128)[None, :]
    kc[:, KC_MCUR:KC_MCUR + 128] = (s <= q)
    kc[:, KC_MPREV:KC_MPREV + 128] = (s > q)
    rm = np.zeros((128, 128), f)
    for m in range(128):
        d = m % 64
        if d < 8:
            rm[m + 8, m] = -1.0
        elif d < 16:
            rm[m - 8, m] = 1.0
    kc[:, KC_RM:KC_RM + 128] = rm
    wsT = np.ascontiguousarray(np.asarray(inp["a_w_s"], f).transpose(0, 3, 1, 2)).reshape(2, 128, 1024)
    bs = np.ascontiguousarray(np.asarray(inp["a_b_s"], f)).reshape(2, 1, 1024)
    return cst, rows, kc, wsT, bs


_NC_CACHE = {}


def run(inp, stages=None, n_tiles=SEQ // TT, n_cores=8, trace=False):
    key = (tuple(stages) if stages else None, n_tiles)
    if key not in _NC_CACHE:
        _NC_CACHE[key] = build_nc(stages, n_tiles)
    nc = _NC_CACHE[key]
    S = n_tiles * TT
    cst, rows, kc, wsT, bs = host_consts(inp)
    x = np.asarray(inp["x"], np.float32)
    posn = np.asarray(inp["positions"], np.int32)
    shared = {
        "cst": cst, "rows": rows, "kc": kc, "wsT": wsT, "bs": bs,
        "a_w_in": np.asarray(inp["a_w_in"], np.float32), "a_w_out": np.asarray(inp["a_w_out"], np.float32),
        "b_w_qkv": np.asarray(inp["b_w_qkv"], np.float32), "b_w_o": np.asarray(inp["b_w_o"], np.float32),
        "ffn_w_gu": np.asarray(inp["ffn_w_gu"], np.float32), "ffn_w_down": np.asarray(inp["ffn_w_down"], np.float32),
    }
    in_maps = []
    for b in range(n_cores):
        m = dict(shared)
        m["xT"] = np.ascontiguousarray(x[b, :S, :].T)
        m["pos"] = np.ascontiguousarray(posn[b, :S].reshape(1, S))
        in_maps.append(m)
    res = run_bass_kernel_spmd(nc, in_maps, core_ids=list(range(n_cores)), trace=trace)
    out = np.stack([np.ascontiguousarray(r["oT"].T) for r in res.results], axis=0)
    return out, res


def kernel(**inputs):
    out, _ = run(inputs)
    return out.astype(np.float32)
```

```python
import numpy as np
from contextlib import ExitStack
import concourse.bass as bass
import concourse.mybir as mybir
from concourse.bass_utils import run_bass_kernel_spmd

F32, BF16, I32 = mybir.dt.float32, mybir.dt.bfloat16, mybir.dt.int32
AF = mybir.ActivationFunctionType
ALU = mybir.AluOpType

D = 1024
SEQ = 4096
TT = 512
NBLK = TT // 128
GW = 2048
FH = 2816
NFC = FH // 128
RMS_EPS = 1e-6
LN_EPS = 1e-5
PERM = [0, 2, 1, 3]
SLOT_ELEMS = 5632
NSLOT = 4
SAME_ENGINE_SYNC = True
DBG_B = 0
SIM_CHECK = False

def c_norm(l, w):
    return (l * 4 + w) * 8
C_A = 128
C_B = 224
C_INVF = 248
C_SINK = 249
NCST = 288
R_BV = 0
R_BVD = 4096
NROWS = 5120
KC_MCUR, KC_MPREV, KC_RM = 0, 128, 256
NKC = 384


class Tok:
    __slots__ = ("key", "val")

    def __init__(self, key, val):
        self.key, self.val = key, val


def _flat(deps, out):
    for d in deps:
        if d is None:
            continue
        if isinstance(d, (list, tuple)):
            _flat(d, out)
        else:
            out.append(d)
    return out


class Stream:
    def __init__(self, name):
        self.name, self.ops, self.epoch, self.count, self.waited = name, [], 0, 0, {}


class Prog:
    def __init__(self):
        self.st = {n: Stream(n) for n in ("pe", "act", "dve", "pool", "sp")}
        self.semkeys = {}
        self.dmacount = {}

    def new_epoch(self):
        for s in self.st.values():
            s.epoch += 1
            s.count = 0

    def _waits(self, st, deps, force=False):
        need = {}
        for d in _flat(deps, []):
            if d.key[0] == "E" and d.key[1] == st.name and not (SAME_ENGINE_SYNC or force):
                continue
            if need.get(d.key, 0) < d.val:
                need[d.key] = d.val
        w = []
        for key, val in need.items():
            if st.waited.get(key, 0) >= val:
                continue
            st.waited[key] = val
            w.append((key, val))
        return w

    def op(self, eng, fn, deps=(), signal=True, force=False):
        st = self.st[eng]
        w = self._waits(st, deps, force)
        key = tok = None
        if signal:
            st.count += 1
            key = ("E", eng, st.epoch)
            self.semkeys[key] = True
            tok = Tok(key, st.count)
        st.ops.append((w, fn, key, 1))
        return tok

    def dma(self, eng, fn, semname, deps=()):
        st = self.st[eng]
        w = self._waits(st, deps)
        key = ("D", semname)
        self.semkeys[key] = True
        self.dmacount[key] = self.dmacount.get(key, 0) + 16
        st.ops.append((w, fn, key, 16))
        return Tok(key, self.dmacount[key])


class Ring:
    def __init__(self, n):
        self.n, self.i, self.free = n, 0, [None] * n

    def get(self):
        idx = self.i
        self.i = (self.i + 1) % self.n
        return idx, self.free[idx]

    def rel(self, idx, tok):
        self.free[idx] = tok


def _simulate(P):
    sem = {}
    pc = {n: 0 for n in P.st}
    progress = True
    while progress:
        progress = False
        for n, st in P.st.items():
            while pc[n] < len(st.ops):
                w, fn, key, inc = st.ops[pc[n]]
                if any(sem.get(k, 0) < v for (k, v) in w):
                    break
                if key is not None:
                    sem[key] = sem.get(key, 0) + inc
                pc[n] += 1
                progress = True
    stuck = {n: pc[n] for n in P.st if pc[n] < len(P.st[n].ops)}
    for n, i in stuck.items():
        w = P.st[n].ops[i][0]
        print("SIM STUCK", n, "op", i, "of", len(P.st[n].ops), "waits", [(k, v, sem.get(k, 0)) for (k, v) in w if sem.get(k, 0) < v])
    if not stuck:
        print("SIM OK", {n: len(P.st[n].ops) for n in P.st})
    return not stuck


def default_stages():
    st = []
    for i in range(4):
        st.append(("A" if i % 2 == 0 else "B", i))
        st.append(("F", i))
    return st


def build_nc(stages=None, n_tiles=SEQ // TT):
    stages = default_stages() if stages is None else stages
    S = n_tiles * TT
    nc = bass.Bass("TRN2", target_bir_lowering=False)
    P = Prog()
    es = ExitStack()

    def dram(name, shape, dt, kind):
        return nc.dram_tensor(name, list(shape), dt, kind=kind).ap()

    xT = dram("xT", [D, S], F32, "ExternalInput")
    pos = dram("pos", [1, S], I32, "ExternalInput")
    cstd = dram("cst", [128, NCST], F32, "ExternalInput")
    rowsd = dram("rows", [1, NROWS], F32, "ExternalInput")
    kcd = dram("kc", [128, NKC], F32, "ExternalInput")
    wsTd = dram("wsT", [2, 128, 8 * 128], F32, "ExternalInput")
    bsd = dram("bs", [2, 1, 8 * 128], F32, "ExternalInput")
    w_in = dram("a_w_in", [2, D, 2 * GW], F32, "ExternalInput")
    w_out = dram("a_w_out", [2, GW, D], F32, "ExternalInput")
    w_qkv = dram("b_w_qkv", [2, D, 1536], F32, "ExternalInput")
    w_o = dram("b_w_o", [2, D, D], F32, "ExternalInput")
    w_gu = dram("ffn_w_gu", [4, D, 2 * FH], F32, "ExternalInput")
    w_dn = dram("ffn_w_down", [4, FH, D], F32, "ExternalInput")
    oT = dram("oT", [D, S], F32, "ExternalOutput")
    dbg = dram("dbg", [128, 4096], F32, "ExternalOutput") if DBG_B == 7 else None
    dbg_toks = []
    rows_b16 = dram("rows_b16", [1, NROWS], BF16, "Internal")
    wb_in = dram("wb_in", [2, D, 2 * GW], BF16, "Internal")
    wb_out = dram("wb_out", [2, GW, D], BF16, "Internal")
    wb_qkv = dram("wb_qkv", [2, D, 2048], BF16, "Internal")
    wb_o = dram("wb_o", [2, D, D], BF16, "Internal")
    wb_gu = dram("wb_gu", [4, D, 2 * FH], BF16, "Internal")
    wb_dn = dram("wb_dn", [4, FH, D], BF16, "Internal")

    def sb(name, shape, dt):
        return es.enter_context(nc.sbuf_tensor(name, list(shape), dt))

    hTs = [sb("hT0", [128, 8, TT], F32), sb("hT1", [128, 8, TT], F32)]
    hT_box = [hTs[0]]
    hn = sb("hn", [128, 8, TT], BF16)
    y = sb("y", [128, 8, TT], F32)
    big = sb("big", [128, 16384], BF16)
    wring_t = [sb(f"wslot{i}", [128, SLOT_ELEMS], BF16) for i in range(NSLOT)]
    cst = sb("cst_sb", [128, NCST], F32)
    rows_bf = sb("rows_bf", [1, 2048], BF16)
    kc_b = sb("kc_b", [128, NKC], BF16)
    ones_bf = sb("ones_bf", [128, 128], BF16)
    eps_t = sb("eps_t", [128, 2], F32)
    Cm = [sb(f"Cm{j}", [128, 16, 128], F32) for j in range(2)]
    wsTm = [sb(f"wsTm{j}", [128, 8, 128], BF16) for j in range(2)]
    cosF = sb("cosF", [128, TT], F32)
    sinF = sb("sinF", [128, TT], F32)
    sq_t = [sb(f"sq{i}", [128, TT], BF16) for i in range(3)]
    lnv = sb("lnv", [128, TT], F32)
    rinv_t = [sb(f"rinv{i}", [128, TT], F32) for i in range(2)]
    tmp_t = [sb(f"tmp{i}", [128, TT], F32) for i in range(2)]
    sg_t = [sb(f"sg{i}", [128, TT], BF16) for i in range(3)]
    tmpA = sb("tmpA", [128, 8 * TT], BF16)
    tmpB = sb("tmpB", [128, 4 * TT], F32)
    es2 = [sb(f"es2_{j}", [64, 16], F32) for j in range(2)]
    es2f = sb("es2f", [64, 16 * 128], BF16)
    esf = sb("esf", [128, 16], F32)
    esh = sb("esh", [128, 16], BF16)
    esl = sb("esl", [128, 16], F32)
    eshf = sb("eshf", [128, 16], F32)
    KTc = [sb(f"KTc{j}", [128, 4, 128], BF16) for j in range(2)]
    Vc = [sb(f"Vc{j}", [128, 512], BF16) for j in range(2)]
    lnst = sb("lnst", [128, NBLK, 4, 6], F32)
    lnmv = sb("lnmv", [128, NBLK, 2], F32)
    lnr = sb("lnr", [128, NBLK], F32)
    mhalf = sb("mhalf", [128, NBLK], F32)
    ps = es.enter_context(nc.psum_tensor("ps", [128, 8, 512], F32))
    pt_t = sb("pt_t", [128, 4 * 512], BF16)
    tmpA_f = tmpA[:].bitcast(F32)
    tmpA_i = tmpA[:].bitcast(I32)
    tmpB_i = tmpB[:].bitcast(I32)

    y_guard = [None]
    lnv_free = [None]
    main = Ring(6)
    aux = Ring(2)
    sqr, rinvr, tmpr, sgr, wring = Ring(3), Ring(2), Ring(2), Ring(3), Ring(NSLOT)

    def auxbank(i):
        return 6 + i

    def mm(out, lhsT, rhs, start, stop, deps=(), signal=False):
        return P.op("pe", lambda e: e.matmul(out, lhsT=lhsT, rhs=rhs, start=start, stop=stop), deps, signal)

    def act(out, in_, func, deps=(), bias=None, scale=None, signal=True):
        kw = {}
        if bias is not None:
            kw["bias"] = bias
        if scale is not None:
            kw["scale"] = scale
        return P.op("act", lambda e: e.activation(out=out, in_=in_, func=func, **kw), deps, signal)

    def tt(eng, out, in0, in1, op, deps=(), signal=True, force=False):
        return P.op(eng, lambda e: e.tensor_tensor(out=out, in0=in0, in1=in1, op=op), deps, signal, force)

    def ts(eng, out, in0, s1, s2, op0, op1=None, deps=(), signal=True, force=False):
        if op1 is None:
            return P.op(eng, lambda e: e.tensor_scalar(out=out, in0=in0, scalar1=s1, scalar2=None, op0=op0), deps, signal, force)
        return P.op(eng, lambda e: e.tensor_scalar(out=out, in0=in0, scalar1=s1, scalar2=s2, op0=op0, op1=op1), deps, signal, force)

    def stt(out, in0, scalar, in1, op0, op1, deps=(), signal=True):
        return P.op("dve", lambda e: e.scalar_tensor_tensor(out=out, in0=in0, scalar=scalar, in1=in1, op0=op0, op1=op1), deps, signal)

    def cp(eng, out, in_, deps=(), signal=True, force=False):
        return P.op(eng, lambda e: e.tensor_copy(out=out, in_=in_), deps, signal, force)

    def dma(eng, out, in_, sem, deps=()):
        return P.dma(eng, lambda e: e.dma_start(out=out, in_=in_), sem, deps)

    a_layers = sorted({l // 2 for (k, l) in stages if k == "A"})
    b_layers = sorted({l // 2 for (k, l) in stages if k == "B"})
    f_layers = sorted({l for (k, l) in stages if k == "F"})

    t_cst = dma("sp", cst[:], cstd, "c_cst")
    t_kc = dma("sp", tmpB[:, 0:NKC], kcd, "c_kc")
    t_kcb = cp("dve", kc_b[:], tmpB[:, 0:NKC], [t_kc])
    rows_free = [None]
    t_rowsc = dma("pool", rows_b16, rowsd, "c_rowsc")
    t_ones = P.op("dve", lambda e: e.memset(ones_bf[:], 1.0))
    t_eps = P.op("dve", lambda e: e.memset(eps_t[:, 0:1], RMS_EPS))
    t_eps = P.op("dve", lambda e: e.memset(eps_t[:, 1:2], LN_EPS))
    t_mh = P.op("pool", lambda e: e.memset(mhalf[:], -0.5))
    mcur = kc_b[:, KC_MCUR:KC_MCUR + 128]
    mprev = kc_b[:, KC_MPREV:KC_MPREV + 128]
    Rm = kc_b[:, KC_RM:KC_RM + 128]

    conv = {}

    conv_hist = []

    def cdma(dst, src, sem):
        thr = conv_hist[-2] if len(conv_hist) >= 2 else None
        return dma("pool", dst, src, sem, [thr])

    def convert(name, dst, src, rows_per, sem):
        n = src.shape[0]
        t = None
        for r0 in range(0, n, rows_per):
            r1 = min(n, r0 + rows_per)
            t = cdma(dst[r0:r1, :], src[r0:r1, :], sem)
        conv[name] = t
        conv_hist.append(t)

    def convert_stage(kind, l):
        j = l // 2
        if kind == "A" and ("in", j) not in conv:
            convert(("in", j), wb_in[j], w_in[j], 128, f"cv_in{j}")
            convert(("out", j), wb_out[j], w_out[j], 512, f"cv_out{j}")
        if kind == "B" and ("qkv", j) not in conv:
            t = None
            for r0 in range(0, D, 256):
                t = cdma(wb_qkv[j][r0:r0 + 256, 0:1024], w_qkv[j][r0:r0 + 256, 0:1024], f"cv_qkv{j}")
                for part, c0 in ((0, 1024), (1, 1280)):
                    src = w_qkv[j][r0:r0 + 256, c0:c0 + 256].rearrange("r (h d) -> r h d", d=64)
                    for dup in range(2):
                        dst = wb_qkv[j][r0:r0 + 256, 1024 + part * 512:1024 + part * 512 + 512].rearrange(
                            "r (h u d) -> r h u d", u=2, d=64)[:, :, dup, :]
                        t = cdma(dst, src, f"cv_qkv{j}")
            conv[("qkv", j)] = t
            conv_hist.append(t)
            convert(("o", j), wb_o[j], w_o[j], 512, f"cv_o{j}")
        if kind == "F" and ("gu", l) not in conv:
            convert(("gu", l), wb_gu[l], w_gu[l], 128, f"cv_gu{l}")
            convert(("dn", l), wb_dn[l], w_dn[l], 256, f"cv_dn{l}")

    convert_stage(*stages[0])

    a_done = {}

    def a_prologue(j, guard):
        if j in a_done:
            return
        wst_f = tmpB[:, 0:1024].rearrange("p (g t) -> p g t", g=8)
        bsb_f = tmpB[:, 1024:2048].rearrange("p (g t) -> p g t", g=8)
        t_w = dma("act", tmpB[:, 0:1024], wsTd[j], "c_ws", [guard, t_kcb])
        t_b = dma("act", tmpB[:, 1024:2048], bsd[j].partition_broadcast(128), "c_bs", [guard, t_kcb])
        t_m = tt("dve", wsTm[j][:], wst_f, mcur.unsqueeze(1).to_broadcast([128, 8, 128]), ALU.mult, [t_w, t_kcb, guard])
        t_c = None
        for half in range(2):
            bi, bfree = aux.get()
            b = auxbank(bi)
            for gg in range(4):
                g = half * 4 + gg
                t_r = mm(ps[:, b, gg * 128:(gg + 1) * 128], ones_bf[:], wsTm[j][:, g, :], True, True,
                         [t_m, t_ones, bfree], signal=True)
            for cc in range(8):
                cv = half * 8 + cc
                g = cv // 2
                gg = g - half * 4
                t_c = stt(Cm[j][:, cv, :], ps[:, b, gg * 128:(gg + 1) * 128],
                          cst[:, C_A + j * 48 + 32 + cv:C_A + j * 48 + 33 + cv], bsb_f[:, g, :],
                          ALU.mult, ALU.add, [t_r, t_b, t_cst])
            aux.rel(bi, t_c)
        a_done[j] = t_c

    if stages[0][0] == "A":
        a_prologue(stages[0][1] // 2, None)
    t_prolog_tmpB = list(a_done.values())

    for j in b_layers:
        t_e = act(esf[:], cst[:, C_SINK + j * 16:C_SINK + j * 16 + 16], AF.Exp, [t_cst, conv.get(("es", j - 1))])
        t_h = cp("dve", esh[:], esf[:], [t_e, conv.get(("es", j - 1))], force=True)
        t_hf = cp("dve", eshf[:], esh[:], [t_h], force=True)
        t_l = tt("dve", esl[:], esf[:], eshf[:], ALU.subtract, [t_hf], force=True)
        t_z = P.op("dve", lambda e, j=j: e.memset(es2[j][:], 0.0), [t_l], force=True)
        t_1 = cp("dve", es2[j][0:1, :], eshf[0:1, :], [t_z], force=True)
        t_2 = cp("dve", es2[j][32:33, :], esl[32:33, :], [t_1], force=True)
        conv[("es", j)] = t_2

    def rms_rinv(src, src_toks):
        bi, bfree = aux.get()
        b = auxbank(bi)
        t_m = None
        for k in range(8):
            si, sfree = sqr.get()
            t_s = act(sq_t[si][:], src[:, k, :], AF.Square, [src_toks[k], sfree])
            t_m = mm(ps[:, b, :], ones_bf[:], sq_t[si][:], k == 0, k == 7, [t_s, t_ones, bfree if k == 0 else None], signal=True)
            sqr.rel(si, t_m)
        ri, rfree = rinvr.get()
        t_l = act(lnv[:], ps[:, b, :], AF.Ln, [t_m, t_eps, lnv_free[0]], bias=eps_t[:, 0:1], scale=1.0 / D)
        t_r = act(rinv_t[ri][:], lnv[:], AF.Exp, [rfree, t_l], scale=-0.5)
        aux.rel(bi, t_l)
        lnv_free[0] = t_r
        return ri, t_r

    def rms_pre(gcol, h_toks):
        hT = hT_box[0]
        ri, t_r = rms_rinv(hT, h_toks)
        toks = []
        for k in range(8):
            toks.append(stt(hn[:, k, :], hT[:, k, :], cst[:, gcol + k:gcol + k + 1], rinv_t[ri][:],
                            ALU.mult, ALU.mult, [t_r, h_toks[k], t_cst]))
        rinvr.rel(ri, toks[-1])
        return toks

    def rms_post(gcol, y_toks, h_toks, out_buf=None):
        hT = hT_box[0]
        out_buf = hT if out_buf is None else out_buf
        ri, t_r = rms_rinv(y, y_toks)
        toks = []
        t_s = None
        for k in range(8):
            ti, tfree = tmpr.get()
            t_s = stt(tmp_t[ti][:], y[:, k, :], cst[:, gcol + k:gcol + k + 1], rinv_t[ri][:],
                      ALU.mult, ALU.mult, [t_r, tfree, t_cst])
            if k % 2:
                t_a = tt("pool", out_buf[:, k, :], hT[:, k, :], tmp_t[ti][:], ALU.add, [t_s, h_toks[k]])
            else:
                t_a = stt(out_buf[:, k, :], tmp_t[ti][:], 1.0, hT[:, k, :], ALU.mult, ALU.add, [t_s, h_toks[k]])
            tmpr.rel(ti, t_a)
            toks.append(t_a)
        rinvr.rel(ri, t_s)
        return toks

    def load_panel(view_fn, srcs, cdeps):
        si, sfree = wring.get()
        toks = []
        for (sel, src) in srcs:
            toks.append(dma("sp", sel(wring_t[si]), src, f"w{si}", [sfree, cdeps]))
        return si, toks

    def ffn(l, h_toks, last=False, after_pre=None, next_a=None):
        hid = big[:, 0:NFC * TT].rearrange("p (c t) -> p c t", c=NFC)
        hn_toks = rms_pre(c_norm(l, 2), h_toks)
        if after_pre is not None:
            after_pre(h_toks)
        elif next_a is not None:
            a_prologue(next_a, h_toks)
        hid_toks = []
        for cp2 in range(NFC // 2):
            c0 = cp2 * 256
            si, dt = load_panel(None, [
                (lambda s: s[:, 0:4096].rearrange("p (k g c) -> p k g c", k=8, g=2)[:, :, 0, :],
                 wb_gu[l][:, c0:c0 + 256].rearrange("(k p) c -> p k c", p=128)),
                (lambda s: s[:, 0:4096].rearrange("p (k g c) -> p k g c", k=8, g=2)[:, :, 1, :],
                 wb_gu[l][:, FH + c0:FH + c0 + 256].rearrange("(k p) c -> p k c", p=128)),
            ], conv[("gu", l)])
            sv = wring_t[si][:, 0:4096].rearrange("p (k g c) -> p k g c", k=8, g=2)
            t_u = None
            for cc in range(2):
                gb, gfree = main.get()
                ub, ufree = main.get()
                for k in range(8):
                    t_g = mm(ps[:, gb, :], sv[:, k, 0, cc * 128:(cc + 1) * 128], hn[:, k, :], k == 0, k == 7,
                             [dt, hn_toks[k], gfree if k == 0 else None], signal=(k == 7))
                for k in range(8):
                    t_u = mm(ps[:, ub, :], sv[:, k, 1, cc * 128:(cc + 1) * 128], hn[:, k, :], k == 0, k == 7,
                             [ufree if k == 0 else None], signal=(k == 7))
                gi, gf = sgr.get()
                t_s = act(sg_t[gi][:], ps[:, gb, :], AF.Silu, [t_g, gf])
                t_m = tt("dve", hid[:, cp2 * 2 + cc, :], sg_t[gi][:], ps[:, ub, :], ALU.mult, [t_s, t_u])
                main.rel(gb, t_s)
                main.rel(ub, t_m)
                sgr.rel(gi, t_m)
                hid_toks.append(t_m)
            wring.rel(si, t_u)
        y_toks = [None] * 8
        for dp in range(4):
            si, dt = load_panel(None, [
                (lambda s: s[:, 0:NFC * 256].rearrange("p (k c) -> p k c", k=NFC),
                 wb_dn[l][:, dp * 256:(dp + 1) * 256].rearrange("(k p) c -> p k c", p=128)),
            ], conv[("dn", l)])
            sv = wring_t[si][:, 0:NFC * 256].rearrange("p (k c) -> p k c", k=NFC)
            t_m = None
            for cc in range(2):
                d = dp * 2 + cc
                b, bfree = main.get()
                for k in range(NFC):
                    t_m = mm(ps[:, b, :], sv[:, k, cc * 128:(cc + 1) * 128], hid[:, k, :], k == 0, k == NFC - 1,
                             [dt, hid_toks[k], bfree if k == 0 else None], signal=(k == NFC - 1))
                t_y = act(y[:, d, :], ps[:, b, :], AF.Copy, [t_m, y_guard[0]])
                main.rel(b, t_y)
                y_toks[d] = t_y
            wring.rel(si, t_m)
        return rms_post(c_norm(l, 3), y_toks, h_toks, y if last else None)

    def mix_a(l, h_toks, last=False, after_pre=None):
        j = l // 2
        a_prologue(j, h_toks)
        uT = big[:, 0:8192].rearrange("p (c t) -> p c t", c=16)
        vt = big[:, 8192:16384].rearrange("p (b f) -> p b f", b=NBLK)
        svt = [tmpB[:, 0:512], tmpB[:, 512:1024]]
        svr = Ring(2)
        t_rows = dma("act", rows_bf[0:1, 0:2048], rows_b16[0:1, R_BV + j * 2048:R_BV + (j + 1) * 2048], "c_rows", [rows_free[0], t_rowsc])
        hn_toks = rms_pre(c_norm(l, 0), h_toks)
        if after_pre is not None:
            after_pre(h_toks)
        cb = C_A + j * 48
        st_toks = [[None] * 4 for _ in range(NBLK)]
        for vp in range(4):
            si, dt = load_panel(None, [
                (lambda s: s[:, 0:4096].rearrange("p (k c) -> p k c", k=8),
                 wb_in[j][:, GW + vp * 512:GW + (vp + 1) * 512].rearrange("(k p) c -> p k c", p=128))], conv[("in", j)])
            sv = wring_t[si][:, 0:4096].rearrange("p (k c) -> p k c", k=8)
            t_m = None
            for blk in range(NBLK):
                b, bfree = main.get()
                for k in range(8):
                    mm(ps[:, b, :], hn[:, k, blk * 128:(blk + 1) * 128], sv[:, k, :], k == 0, False,
                       [dt, hn_toks[k], bfree if k == 0 else None])
                t_m = mm(ps[:, b, :], ones_bf[0:1, :], rows_bf[0:1, vp * 512:(vp + 1) * 512],
                         False, True, [t_rows, t_ones], signal=True)
                rows_free[0] = t_m
                t_v = act(vt[:, blk, vp * 512:(vp + 1) * 512], ps[:, b, :], AF.Gelu, [t_m])
                main.rel(b, t_v)
                st_toks[blk][vp] = P.op("dve", lambda e, blk=blk, vp=vp: e.bn_stats(out=lnst[:, blk, vp, :], in_=vt[:, blk, vp * 512:(vp + 1) * 512]), [t_v])
            wring.rel(si, t_m)
        u_toks = []
        for up in range(4):
            si, dt = load_panel(None, [
                (lambda s: s[:, 0:4096].rearrange("p (k c) -> p k c", k=8),
                 wb_in[j][:, up * 512:(up + 1) * 512].rearrange("(k p) c -> p k c", p=128))], conv[("in", j)])
            sv = wring_t[si][:, 0:4096].rearrange("p (k c) -> p k c", k=8)
            t_m = None
            for cc in range(4):
                cu = up * 4 + cc
                b, bfree = main.get()
                for k in range(8):
                    t_m = mm(ps[:, b, :], sv[:, k, cc * 128:(cc + 1) * 128], hn[:, k, :], k == 0, k == 7,
                             [dt, hn_toks[k], bfree if k == 0 else None], signal=(k == 7))
                t_u = act(uT[:, cu, :], ps[:, b, :], AF.Gelu, [t_m, t_cst], bias=cst[:, cb + cu:cb + cu + 1])
                main.rel(b, t_u)
                u_toks.append(t_u)
            wring.rel(si, t_m)
        vn_toks = []
        for blk in range(NBLK):
            t_a = P.op("dve", lambda e, blk=blk: e.bn_aggr(out=lnmv[:, blk, :], in_=lnst[:, blk, :, :].rearrange("p a s -> p (a s)")), [st_toks[blk]], True, True)
            t_e = ts("pool", lnr[:, blk:blk + 1], lnmv[:, blk, 1:2], LN_EPS, None, ALU.add, None, [t_a])
            t_p = tt("pool", lnr[:, blk:blk + 1], lnr[:, blk:blk + 1], mhalf[:, blk:blk + 1], ALU.pow, [t_e, t_mh], force=True)
            t_n = ts("dve", vt[:, blk, :], vt[:, blk, :], lnmv[:, blk, 0:1], lnr[:, blk:blk + 1], ALU.subtract, ALU.mult, [t_p, t_a], force=True)
            vn_toks.append(t_n)
        g_toks = []
        for cv in range(16):
            g = cv // 2
            b, bfree = main.get()
            t_m = None
            for blk in range(NBLK):
                t_m = mm(ps[:, b, blk * 128:(blk + 1) * 128], vt[:, blk, cv * 128:(cv + 1) * 128], wsTm[j][:, g, :], True, True,
                         [vn_toks[blk], bfree if blk == 0 else None, a_done[j]], signal=(blk == NBLK - 1))
            si2, sf2 = svr.get()
            t_s = stt(svt[si2].rearrange("p (b t) -> p b t", b=NBLK), ps[:, b, :].rearrange("p (b t) -> p b t", b=NBLK),
                      cst[:, cb + 16 + cv:cb + 17 + cv], Cm[j][:, cv, :].unsqueeze(1).to_broadcast([128, NBLK, 128]),
                      ALU.mult, ALU.add, [t_m, sf2])
            main.rel(b, t_s)
            t_g = tt("pool" if cv % 2 else "dve", uT[:, cv, :], uT[:, cv, :], svt[si2], ALU.mult, [t_s, u_toks[cv]])
            svr.rel(si2, t_g)
            g_toks.append(t_g)
        y_toks = [None] * 8
        for op_ in range(4):
            si, dt = load_panel(None, [
                (lambda s: s[:, 0:4096].rearrange("p (k c) -> p k c", k=16),
                 wb_out[j][:, op_ * 256:(op_ + 1) * 256].rearrange("(k p) c -> p k c", p=128))], conv[("out", j)])
            sv = wring_t[si][:, 0:4096].rearrange("p (k c) -> p k c", k=16)
            t_m = None
            for cc in range(2):
                d = op_ * 2 + cc
                b, bfree = main.get()
                for k in range(16):
                    t_m = mm(ps[:, b, :], sv[:, k, cc * 128:(cc + 1) * 128], uT[:, k, :], k == 0, k == 15,
                             [dt, g_toks[k], bfree if k == 0 else None], signal=(k == 15))
                t_y = act(y[:, d, :], ps[:, b, :], AF.Copy, [t_m, y_guard[0]])
                main.rel(b, t_y)
                y_toks[d] = t_y
            wring.rel(si, t_m)
        return rms_post(c_norm(l, 1), y_toks, h_toks, y if last else None)

    def mix_b(l, h_toks, ti, rope_tok, last=False):
        j = l // 2
        QT = big[:, 0:4096].rearrange("p (c t) -> p c t", c=8)
        aT = big[:, 4096:8192].rearrange("p (c t) -> p c t", c=8)
        KTe = big[:, 8192:8192 + 2560].rearrange("p (h s t) -> p h s t", h=4, s=5)
        KTo = big[:, 10752:10752 + 2560].rearrange("p (h s t) -> p h s t", h=4, s=5)
        Vv = big[:, 13312:13312 + 2560].rearrange("p (s f) -> p s f", s=5)
        pt = [pt_t[:, i * 512:(i + 1) * 512] for i in range(4)]
        ptr = Ring(4)
        qf = [tmpB[:, 0:512], tmpB[:, 512:1024]]
        t1 = tmpB[:, 1024:1536]
        lden = tmpB[:, 1536:2048]
        qb = [tmpA[:, 0:512], tmpA[:, 512:1024]]
        rden = [tmpA_f[:, 512:1024], tmpA_f[:, 1024:1536]]
        qr, rdr = Ring(2), Ring(2)
        first = (ti == 0)
        t_rows = dma("act", rows_bf[0:1, 0:512], rows_b16[0:1, R_BVD + j * 512:R_BVD + (j + 1) * 512], "c_rows", [rows_free[0], t_rowsc])
        hn_toks = rms_pre(c_norm(l, 0), h_toks)
        cb = C_B + j * 12
        t_ck = t_cv = None
        t_esf = None
        for hq in range(16):
            t_esf = ts("dve", es2f[:, hq * 128:(hq + 1) * 128], ones_bf[0:64, :], es2[j][:, hq:hq + 1], None, ALU.mult, None,
                       [conv[("es", j)], h_toks, t_ones])
        P.op("pool", lambda e: e.memset(KTe[64:128, :, :, :], 0.0), [h_toks])
        t_kz = P.op("pool", lambda e: e.memset(KTo[0:64, :, :, :], 0.0))
        if not first:
            cp("pool", KTe[0:64, :, 0, :], KTc[j][0:64, :, :])
            t_ck = cp("pool", KTo[64:128, :, 0, :], KTc[j][64:128, :, :])
            t_cv = cp("pool", Vv[:, 0, :], Vc[j][:])

        def rope_p1(b, t_m, bias_ap):
            qi, qfree = qr.get()
            t_f = act(qf[qi], ps[:, b, :], AF.Identity, [t_m, qfree, t_cst], bias=bias_ap)
            t_b = act(qb[qi], ps[:, b, :], AF.Identity, [], bias=bias_ap)
            main.rel(b, t_b)
            return (qi, t_f, t_b)

        def rope_p2(state, outs):
            qi, t_f, t_b = state
            b2, b2free = main.get()
            t_r = mm(ps[:, b2, :], Rm, qb[qi], True, True, [t_b, t_kcb, b2free], signal=True)
            t_a = tt("dve", t1, qf[qi], cosF[:], ALU.mult, [t_f, rope_tok, t1_free[0]])
            t_s = tt("dve", qf[qi], ps[:, b2, :], sinF[:], ALU.mult, [t_r, t_a])
            main.rel(b2, t_s)
            t_o = None
            for (lo, hi, out_ap) in outs:
                t_o = tt("dve", out_ap, t1[lo:hi, :], qf[qi][lo:hi, :], ALU.add, [t_a, t_s])
            t1_free[0] = t_o
            qr.rel(qi, t_o)
            return t_o

        pending = []
        t1_free = [None]

        def flush_rope():
            while pending:
                kind_, idx_, state_, outs_ = pending.pop(0)
                t_o = rope_p2(state_, outs_)
                if kind_ == "q":
                    q_toks[idx_] = t_o
                else:
                    k_toks[idx_] = t_o

        q_toks = [None] * 8
        k_toks = [None] * 4
        v_toks = [None] * NBLK
        for pn in range(4):
            si, dt = load_panel(None, [
                (lambda s: s[:, 0:4096].rearrange("p (k c) -> p k c", k=8),
                 wb_qkv[j][:, pn * 512:(pn + 1) * 512].rearrange("(k p) c -> p k c", p=128))], conv[("qkv", j)])
            sv = wring_t[si][:, 0:4096].rearrange("p (k c) -> p k c", k=8)
            t_m = None
            if pn < 3:
                for cc in range(4):
                    b, bfree = main.get()
                    for k in range(8):
                        t_m = mm(ps[:, b, :], sv[:, k, cc * 128:(cc + 1) * 128], hn[:, k, :], k == 0, k == 7,
                                 [dt, hn_toks[k], bfree if k == 0 else None], signal=(k == 7))
                    if pn < 2:
                        cq = pn * 4 + cc
                        st_ = rope_p1(b, t_m, cst[:, cb + cq:cb + cq + 1])
                        flush_rope()
                        pending.append(("q", cq, st_, [(0, 128, QT[:, cq, :])]))
                    else:
                        st_ = rope_p1(b, t_m, cst[:, cb + 8 + cc:cb + 9 + cc])
                        flush_rope()
                        pending.append(("k", cc, st_, [(0, 64, KTe[0:64, cc, 1:5, :].rearrange("p s t -> p (s t)")),
                                                       (64, 128, KTo[64:128, cc, 1:5, :].rearrange("p s t -> p (s t)"))]))
            else:
                for blk in range(NBLK):
                    b, bfree = main.get()
                    for k in range(8):
                        mm(ps[:, b, :], hn[:, k, blk * 128:(blk + 1) * 128], sv[:, k, :], k == 0, False,
                           [dt, hn_toks[k], bfree if k == 0 else None])
                    t_m = mm(ps[:, b, :], ones_bf[0:1, :], rows_bf[0:1, 0:512],
                             False, True, [t_rows, t_ones], signal=True)
                    rows_free[0] = t_m
                    t_v = act(Vv[:, 1 + blk, :], ps[:, b, :], AF.Copy, [t_m, t_cv])
                    main.rel(b, t_v)
                    v_toks[blk] = t_v
                    flush_rope()
            wring.rel(si, t_m)
        a_toks = [None] * 8
        last_pe = [None]
        lden_free = [None]

        def s_phase(blk, kv):
            has_prev = not (first and blk == 0)
            pts = []
            for kb in (([0] if has_prev else []) + [1]):
                slot = blk + kb
                sb_, sfree = main.get()
                mm(ps[:, sb_, 0:256], KTe[:, kv, slot, :],
                   QT[:, 2 * kv:2 * kv + 2, blk * 128:(blk + 1) * 128], True, True,
                   [k_toks[kv], t_ck, t_kz, q_toks[2 * kv], q_toks[2 * kv + 1], sfree])
                t_m = mm(ps[:, sb_, 256:512], KTo[:, kv, slot, :],
                         QT[:, 2 * kv:2 * kv + 2, blk * 128:(blk + 1) * 128], True, True, [], signal=True)
                pi, pfree = ptr.get()
                t_e = act(pt[pi], ps[:, sb_, :], AF.Exp, [t_m, pfree], scale=0.125)
                main.rel(sb_, t_e)
                msk = mprev if kb == 0 else mcur
                t_k = tt("dve", pt[pi].rearrange("p (a q) -> p a q", a=4), pt[pi].rearrange("p (a q) -> p a q", a=4),
                         msk.unsqueeze(1).to_broadcast([128, 4, 128]), ALU.mult, [t_e, t_kcb])
                pts.append((pi, slot, t_k))
            return pts

        def pv_phase(blk, kv, pts):
            ob, ofree = main.get()
            db, dfree = main.get()
            for n_, (pi, slot, t_k) in enumerate(pts):
                mm(ps[:, ob, :], Vv[:, slot, kv * 128:(kv + 1) * 128], pt[pi], n_ == 0, n_ == len(pts) - 1,
                   [t_k, v_toks[blk], t_cv, ofree if n_ == 0 else None])
            for n_, (pi, slot, t_k) in enumerate(pts):
                t_pv = mm(ps[:, db, :], ones_bf[:], pt[pi], n_ == 0, False, [dfree if n_ == 0 else None, t_ones], signal=True)
                ptr.rel(pi, t_pv)
            t_d = mm(ps[:, db, :], ones_bf[0:64, :], es2f[:, kv * 512:(kv + 1) * 512], False, True, [t_esf], signal=True)
            last_pe[0] = t_d
            t_l = act(lden, ps[:, db, :], AF.Ln, [t_d, lden_free[0]])
            main.rel(db, t_l)
            ri, rfree = rdr.get()
            t_r = act(rden[ri], lden, AF.Exp, [rfree, t_l], scale=-1.0)
            lden_free[0] = t_r
            tt("dve", aT[0:64, 2 * kv:2 * kv + 2, blk * 128:(blk + 1) * 128],
               ps[0:64, ob, 0:256].rearrange("p (a q) -> p a q", a=2),
               rden[ri][0:64, 0:256].rearrange("p (a q) -> p a q", a=2), ALU.mult, [t_r, t_d])
            t_n1 = tt("dve", aT[64:128, 2 * kv:2 * kv + 2, blk * 128:(blk + 1) * 128],
                      ps[64:128, ob, 256:512].rearrange("p (a q) -> p a q", a=2),
                      rden[ri][64:128, 256:512].rearrange("p (a q) -> p a q", a=2), ALU.mult, [t_r, t_d])
            main.rel(ob, t_n1)
            rdr.rel(ri, t_n1)
            a_toks[2 * kv] = t_n1
            a_toks[2 * kv + 1] = t_n1

        items = [(blk, kv) for blk in range(NBLK) for kv in range(4)]
        nxt = s_phase(*items[0])
        for ii, (blk, kv) in enumerate(items):
            cur = nxt
            if ii + 1 < len(items):
                nxt = s_phase(*items[ii + 1])
            pv_phase(blk, kv, cur)
        last_pe = last_pe[0]
        cp("pool", KTc[j][0:64, :, :], KTe[0:64, :, 4, :], [k_toks, last_pe])
        t_ko = cp("pool", KTc[j][64:128, :, :], KTo[64:128, :, 4, :])
        t_vo = cp("pool", Vc[j][:], Vv[:, 4, :], [v_toks[NBLK - 1], last_pe])
        y_toks = [None] * 8
        for op_ in range(2):
            si, dt = load_panel(None, [
                (lambda s: s[:, 0:4096].rearrange("p (k c) -> p k c", k=8),
                 wb_o[j][:, op_ * 512:(op_ + 1) * 512].rearrange("(k p) c -> p k c", p=128))], conv[("o", j)])
            sv = wring_t[si][:, 0:4096].rearrange("p (k c) -> p k c", k=8)
            t_m = None
            for cc in range(4):
                d = op_ * 4 + cc
                b, bfree = main.get()
                for k in range(8):
                    t_m = mm(ps[:, b, :], sv[:, k, cc * 128:(cc + 1) * 128], aT[:, k, :], k == 0, k == 7,
                             [dt, a_toks[k], bfree if k == 0 else None], signal=(k == 7))
                t_y = act(y[:, d, :], ps[:, b, :], AF.Copy, [t_m, y_guard[0]])
                main.rel(b, t_y)
                y_toks[d] = t_y
            wring.rel(si, t_m)
        return rms_post(c_norm(l, 1), y_toks, h_toks, y if last else None), [t_ko, t_vo]

    def rope_tables(t0, guard):
        posi = tmpB_i[:, 0:512]
        r = tmpB[:, 512:1024]
        nf = tmpB[:, 1024:1536]
        m = tmpB[:, 1536:2048]
        ni = tmpA_i[:, 0:512]
        w = tmpA_f[:, 1024:1536]
        fl = tmpA_f[:, 1536:2048]
        t_p = dma("pool", posi, pos[:, t0:t0 + TT].partition_broadcast(128), "c_pos", [guard])
        c = cp("dve", r, posi, [t_p, guard])
        c = ts("dve", r, r, cst[:, C_INVF:C_INVF + 1], 1.0 / (2 * np.pi), ALU.mult, ALU.mult, [t_cst, c])
        last = None
        for (dst, shift) in ((sinF, 0.0), (cosF, 0.25)):
            if shift:
                c = ts("dve", nf, r, shift, None, ALU.add, None, [c])
                c = cp("dve", ni, nf, [c])
                c = cp("dve", fl, ni, [c])
                c = tt("dve", nf, nf, fl, ALU.subtract, [c])
                fr = nf
            else:
                c = cp("dve", ni, r, [c])
                c = cp("dve", m, ni, [c])
                c = tt("dve", m, r, m, ALU.subtract, [c])
                fr = m
            c = P.op("dve", lambda e, fr=fr: e.tensor_single_scalar(out=w, in_=fr, scalar=0.5, op=ALU.is_gt), [c])
            c = tt("dve", fr, fr, w, ALU.subtract, [c])
            c = P.op("dve", lambda e, fr=fr: e.tensor_single_scalar(out=w, in_=fr, scalar=-0.5, op=ALU.is_lt), [c])
            c = tt("dve", fr, fr, w, ALU.add, [c])
            last = act(dst[:], fr, AF.Sin, [c], scale=2 * np.pi * (1 - 2e-6))
        return last

    prev_store = None
    x_free = [None, None]
    stage_guard = [t_prolog_tmpB] + [conv.get(("es", j)) for j in b_layers]
    carry = []
    fb = next((i for i, (k_, _) in enumerate(stages) if k_ == "B"), None)
    rope_box = [None]

    def x_load(ti_):
        bi_ = ti_ % 2
        return dma("sp", hTs[bi_][:], xT[:, ti_ * TT:(ti_ + 1) * TT].rearrange("(k p) t -> p k t", p=128),
                   "c_x%d" % bi_, [x_free[bi_]])

    t_x_next = x_load(0)
    for ti in range(n_tiles):
        if ti > 0:
            P.new_epoch()
        t0 = ti * TT
        hT_box[0] = hTs[ti % 2]
        t_x = t_x_next
        h_toks = [t_x] * 8
        y_guard[0] = prev_store
        rope_box[0] = None

        def mk_rope(h_toks_, t0=t0):
            rope_box[0] = rope_tables(t0, [stage_guard, h_toks_, carry])

        for si_, (kind, l) in enumerate(stages):
            last = (si_ == len(stages) - 1)
            if ti == 0 and si_ + 1 < len(stages):
                convert_stage(*stages[si_ + 1])
            cb_ = mk_rope if (fb is not None and si_ == fb - 1) else None
            if kind == "A":
                h_toks = mix_a(l, h_toks, last, cb_)
            elif kind == "B":
                if rope_box[0] is None:
                    mk_rope(h_toks)
                h_toks, carry = mix_b(l, h_toks, ti, rope_box[0], last)
            else:
                nxa = None
                if ti == 0 and si_ + 1 < len(stages) and stages[si_ + 1][0] == "A":
                    nxa = stages[si_ + 1][1] // 2
                h_toks = ffn(l, h_toks, last, cb_, nxa)
            if rope_box[0] is not None:
                stage_guard = []
            if si_ == 0 and ti + 1 < n_tiles:
                t_x_next = x_load(ti + 1)
        x_free[ti % 2] = h_toks
        prev_store = dma("pool", oT[:, t0:t0 + TT].rearrange("(k p) t -> p k t", p=128), y[:], "c_o", [h_toks])
    P.op("sp", lambda e: e.nop(), [prev_store, dbg_toks], False)

    if SIM_CHECK:
        _simulate(P)
    sems = {}
    for n_, key in enumerate(P.semkeys):
        sems[key] = es.enter_context(nc.semaphore("s%d" % n_))
    block = es.enter_context(nc.Block())

    def make(name):
        def body(e):
            for (w, fn, key, inc) in P.st[name].ops:
                for (k, v) in w:
                    e.wait_ge(sems[k], v)
                ins = fn(e)
                if key is not None:
                    ins.then_inc(sems[key], inc)
        return body

    block.tensor(make("pe"))
    block.scalar(make("act"))
    block.vector(make("dve"))
    block.gpsimd(make("pool"))
    block.sync(make("sp"))
    es.close()
    return nc


def _col(v):
    return np.ascontiguousarray(v.reshape(-1, 128).T)


def host_consts(inp):
    f = np.float32
    cst = np.zeros((128, NCST), f)
    for l in range(4):
        for w, nm in enumerate(("pre_mix_g", "post_mix_g", "pre_ffn_g", "post_ffn_g")):
            cst[:, c_norm(l, w):c_norm(l, w) + 8] = _col(np.asarray(inp[nm][l], f))
    for j in range(2):
        cb = C_A + j * 48
        cst[:, cb:cb + 16] = _col(np.asarray(inp["a_b_in"][j][:GW], f))
        cst[:, cb + 16:cb + 32] = _col(np.asarray(inp["a_ln_g"][j], f))
        cst[:, cb + 32:cb + 48] = _col(np.asarray(inp["a_ln_b"][j], f))
        bq = np.asarray(inp["b_b_qkv"][j], f)
        cb = C_B + j * 12
        cst[:, cb:cb + 8] = _col(bq[:1024])
        bk = bq[1024:1280].reshape(4, 64)
        cst[:, cb + 8:cb + 12] = np.concatenate([bk, bk], axis=1).T
        sk = np.asarray(inp["b_sinks"][j], f).reshape(4, 4)[:, PERM].reshape(16)
        cst[:, C_SINK + j * 16:C_SINK + j * 16 + 16] = np.broadcast_to(sk, (128, 16))
    inv_freq = (np.float32(500000.0) ** (-np.arange(0, 16, 2, dtype=np.float32) / np.float32(16))).astype(f)
    p = np.arange(128) % 64
    cst[:, C_INVF] = np.where(p < 16, inv_freq[p % 8], 0.0)
    rows = np.zeros((1, NROWS), f)
    for j in range(2):
        rows[0, R_BV + j * 2048:R_BV + (j + 1) * 2048] = np.asarray(inp["a_b_in"][j][GW:], f)
        bv = np.asarray(inp["b_b_qkv"][j], f)[1280:1536].reshape(4, 1, 64)
        rows[0, R_BVD + j * 512:R_BVD + (j + 1) * 512] = np.broadcast_to(bv, (4, 2, 64)).reshape(512)
    kc = np.zeros((128, NKC), f)
    s = np.arange(128)[:, None]
    q = np.arange(128)[None, :]
    kc[:, KC_MCUR:KC_MCUR + 128] = (s <= q)
    kc[:, KC_MPREV:KC_MPREV + 128] = (s > q)
    rm = np.zeros((128, 128), f)
    for m in range(128):
        d = m % 64
        if d < 8:
            rm[m + 8, m] = -1.0
        elif d < 16:
            rm[m - 8, m] = 1.0
    kc[:, KC_RM:KC_RM + 128] = rm
    wsT = np.ascontiguousarray(np.asarray(inp["a_w_s"], f).transpose(0, 3, 1, 2)).reshape(2, 128, 1024)
    bs = np.ascontiguousarray(np.asarray(inp["a_b_s"], f)).reshape(2, 1, 1024)
    return cst, rows, kc, wsT, bs


_NC_CACHE = {}


def run(inp, stages=None, n_tiles=SEQ // TT, n_cores=8, trace=False):
    key = (tuple(stages) if stages else None, n_tiles)
    if key not in _NC_CACHE:
        _NC_CACHE[key] = build_nc(stages, n_tiles)
    nc = _NC_CACHE[key]
    S = n_tiles * TT
    cst, rows, kc, wsT, bs = host_consts(inp)
    x = np.asarray(inp["x"], np.float32)
    posn = np.asarray(inp["positions"], np.int32)
    shared = {
        "cst": cst, "rows": rows, "kc": kc, "wsT": wsT, "bs": bs,
        "a_w_in": np.asarray(inp["a_w_in"], np.float32), "a_w_out": np.asarray(inp["a_w_out"], np.float32),
        "b_w_qkv": np.asarray(inp["b_w_qkv"], np.float32), "b_w_o": np.asarray(inp["b_w_o"], np.float32),
        "ffn_w_gu": np.asarray(inp["ffn_w_gu"], np.float32), "ffn_w_down": np.asarray(inp["ffn_w_down"], np.float32),
    }
    in_maps = []
    for b in range(n_cores):
        m = dict(shared)
        m["xT"] = np.ascontiguousarray(x[b, :S, :].T)
        m["pos"] = np.ascontiguousarray(posn[b, :S].reshape(1, S))
        in_maps.append(m)
    res = run_bass_kernel_spmd(nc, in_maps, core_ids=list(range(n_cores)), trace=trace)
    out = np.stack([np.ascontiguousarray(r["oT"].T) for r in res.results], axis=0)
    return out, res


def kernel(**inputs):
    out, _ = run(inputs)
    return out.astype(np.float32)
```
